# Optimizing a Trainium2 kernel written in Bass

```python
import math
import jax, jax.numpy as jnp
from jax import lax
import numpy as np


D_MODEL = 2048
BATCH = 1
SEQ = 8192
DEPTH = 4

MEM_LEN = 256
GLA_HEADS = 4
GLA_DK = 128
GLA_DV = 256
GLA_GATE_RANK = 16
GLA_GATE_NORMALIZER = 16.0
GLA_CHUNK = 64
DIL_HEADS = 4
DIL_HEAD_DIM = 128
DIL_CONFIGS = ((128, 1), (512, 4), (2048, 16))
MEM_HEADS = 4
MEM_HEAD_DIM = 128
REL_BUCKETS = 32
REL_MAX_DISTANCE = 1024
D_FF = 4 * D_MODEL
EPS = 1e-6
NEG_INF = -1e30

GLA_QK_WIDTH = GLA_HEADS * GLA_DK
GLA_V_WIDTH = GLA_HEADS * GLA_DV
DIL_WIDTH = DIL_HEADS * DIL_HEAD_DIM
MEM_WIDTH = MEM_HEADS * MEM_HEAD_DIM
MIX_WIDTH = GLA_V_WIDTH + DIL_WIDTH + MEM_WIDTH
IN_SPLITS = (GLA_QK_WIDTH, GLA_QK_WIDTH, GLA_V_WIDTH, GLA_V_WIDTH, GLA_GATE_RANK, GLA_GATE_RANK,
             DIL_WIDTH, DIL_WIDTH, DIL_WIDTH, MEM_WIDTH)
IN_WIDTH = 2 * GLA_QK_WIDTH + 2 * GLA_V_WIDTH + 2 * GLA_GATE_RANK + 3 * DIL_WIDTH + MEM_WIDTH

kernel_name = 'hymba_gla_dilated_memxattn_encoder'


def rms_norm(x, gain):
    xf = x.astype(jnp.float32)
    y = xf * lax.rsqrt(jnp.mean(xf * xf, axis=-1, keepdims=True) + EPS)
    return (y * gain.astype(jnp.float32)).astype(x.dtype)


def head_rms_norm(x, gain):
    h, e = x.shape[-2:]
    xf = x.astype(jnp.float32)
    y = xf * lax.rsqrt(jnp.mean(xf * xf, axis=-1, keepdims=True) + EPS)
    return y * gain.reshape(h, e).astype(jnp.float32)


def split_columns(t, sizes):
    outs, start = [], 0
    for s in sizes:
        outs.append(t[..., start:start + s])
        start += s
    return outs


def t5_bucket(rel):
    half = REL_BUCKETS // 2
    max_exact = half // 2
    ret = jnp.where(rel > 0, half, 0)
    n = jnp.abs(rel)
    nf = jnp.maximum(n, 1).astype(jnp.float32)
    large = max_exact + (jnp.log(nf / max_exact) / math.log(REL_MAX_DISTANCE / max_exact)
                         * (half - max_exact)).astype(jnp.int32)
    large = jnp.minimum(large, half - 1)
    return ret + jnp.where(n < max_exact, n, large)


def gla_direction(q, k, v, g):
    b_, h_, s_, dk = q.shape
    dv = v.shape[-1]
    c = GLA_CHUNK
    n = s_ // c
    q, k, g = [t.reshape(b_, h_, n, c, dk) for t in (q, k, g)]
    v = v.reshape(b_, h_, n, c, dv)
    b = jnp.cumsum(g, axis=3)
    b_last = b[:, :, :, -1:, :]
    q_dec = q * jnp.exp(b)
    k_inv = k * jnp.exp(-b)
    k_end = k * jnp.exp(b_last - b)
    causal = jnp.tril(jnp.ones((c, c), dtype=bool))
    a = jnp.where(causal, jnp.einsum('bhncd,bhnsd->bhncs', q_dec, k_inv), 0.0)
    o_intra = jnp.einsum('bhncs,bhnsv->bhncv', a, v)
    d_state = jnp.einsum('bhncd,bhncv->nbhdv', k_end, v)
    decay = jnp.moveaxis(jnp.exp(b_last[:, :, :, 0, :]), 2, 0)
    q_chunks = jnp.moveaxis(q_dec, 2, 0)

    def step(state, inp):
        ds, dec, qc = inp
        out = jnp.einsum('bhcd,bhdv->bhcv', qc, state)
        return dec[..., None] * state + ds, out

    state0 = jnp.zeros((b_, h_, dk, dv), dtype=q.dtype)
    _, o_inter = lax.scan(step, state0, (d_state, decay, q_chunks))
    o = o_intra + jnp.moveaxis(o_inter, 0, 2)
    return o.reshape(b_, h_, s_, dv)


def gla_mixer(q, k, v, r, lr_f, lr_b, up_f, bias_f, up_b, bias_b, norm_gain):
    b_, s_ = q.shape[:2]

    def heads(t, e):
        return t.reshape(b_, s_, GLA_HEADS, e).transpose(0, 2, 1, 3).astype(jnp.float32)

    def log_gate(lr, up, bias):
        logits = (jnp.einsum('bsr,rk->bsk', lr, up) + bias).astype(jnp.float32)
        return heads(jax.nn.log_sigmoid(logits) / GLA_GATE_NORMALIZER, GLA_DK)

    qh = heads(q, GLA_DK) * (GLA_DK ** -0.5)
    kh = heads(k, GLA_DK)
    vh = heads(v, GLA_DV)
    g_f = log_gate(lr_f, up_f, bias_f)
    g_b = log_gate(lr_b, up_b, bias_b)
    o_f = gla_direction(qh, kh, vh, g_f)
    flip = lambda t: jnp.flip(t, axis=2)
    o_b = flip(gla_direction(flip(qh), flip(kh), flip(vh), flip(g_b)))
    o = (o_f + o_b).transpose(0, 2, 1, 3)
    gate = jax.nn.silu(r.reshape(b_, s_, GLA_HEADS, GLA_DV).astype(jnp.float32))
    o = head_rms_norm(o, norm_gain) * gate
    return o.reshape(b_, s_, GLA_V_WIDTH)


def dilated_branch(q, k, v, rel_table, window, dilation):
    b_, s_, h_, e = q.shape
    w = window // (2 * dilation)
    l = s_ // dilation
    nb = -(-l // w)
    lp = nb * w

    def stride(t):
        return t.reshape(b_, l, dilation, h_, e).transpose(0, 2, 3, 1, 4)

    qs = jnp.pad(stride(q), ((0, 0), (0, 0), (0, 0), (0, lp - l), (0, 0))).reshape(b_, dilation, h_, nb, w, e)

    def windows(t):
        tp = jnp.pad(stride(t), ((0, 0), (0, 0), (0, 0), (w, lp - l + w), (0, 0)))
        tp = tp.reshape(b_, dilation, h_, nb + 2, w, e)
        return jnp.concatenate([tp[:, :, :, :-2], tp[:, :, :, 1:-1], tp[:, :, :, 2:]], axis=4)

    kw = windows(k)
    vw = windows(v)
    rel_sub = jnp.arange(3 * w)[None, :] - w - jnp.arange(w)[:, None]
    bias = jnp.transpose(rel_table[t5_bucket(rel_sub * dilation)], (2, 0, 1)).astype(jnp.float32)
    key_pos = jnp.arange(nb)[:, None] * w - w + jnp.arange(3 * w)[None, :]
    mask = (jnp.abs(rel_sub) <= w)[None] & ((key_pos >= 0) & (key_pos < l))[:, None, :]
    s = jnp.einsum('bdhnqe,bdhnke->bdhnqk', qs, kw, preferred_element_type=jnp.float32) * (e ** -0.5)
    s = jnp.where(mask, s + bias[None, None, :, None], NEG_INF)
    m = jnp.max(s, axis=-1, keepdims=True)
    p = jnp.exp(s - m)
    den = jnp.sum(p, axis=-1, keepdims=True)
    o = jnp.einsum('bdhnqk,bdhnke->bdhnqe', p, vw.astype(jnp.float32)) / den
    lse = m + jnp.log(den)

    def unstride(t):
        t = t.reshape(b_, dilation, h_, lp, t.shape[-1])[:, :, :, :l]
        return t.transpose(0, 3, 1, 2, 4).reshape(b_, s_, h_, t.shape[-1])

    return unstride(o), unstride(lse)[..., 0]


def dilated_mixer(q, k, v, rel_table, norm_gain):
    b_, s_ = q.shape[:2]
    qh, kh, vh = [t.reshape(b_, s_, DIL_HEADS, DIL_HEAD_DIM) for t in (q, k, v)]
    outs, lses = [], []
    for window, dilation in DIL_CONFIGS:
        o, lse = dilated_branch(qh, kh, vh, rel_table, window, dilation)
        outs.append(o)
        lses.append(lse)
    weights = jax.nn.softmax(jnp.stack(lses, axis=0), axis=0)
    o = jnp.einsum('rbsh,rbshe->bshe', weights, jnp.stack(outs, axis=0))
    return head_rms_norm(o, norm_gain).reshape(b_, s_, DIL_WIDTH)


def memory_mixer(q, mem, mem_gain, w_mem_kv, norm_gain):
    b_, s_ = q.shape[:2]
    qh = q.reshape(b_, s_, MEM_HEADS, MEM_HEAD_DIM)
    kv = jnp.einsum('bmd,dk->bmk', rms_norm(mem, mem_gain), w_mem_kv)
    km, vm = [t.reshape(b_, mem.shape[1], MEM_HEADS, MEM_HEAD_DIM) for t in split_columns(kv, (MEM_WIDTH, MEM_WIDTH))]
    s = jnp.einsum('bshe,bmhe->bhsm', qh, km, preferred_element_type=jnp.float32) * (MEM_HEAD_DIM ** -0.5)
    p = jax.nn.softmax(s, axis=-1)
    o = jnp.einsum('bhsm,bmhe->bshe', p, vm.astype(jnp.float32))
    return head_rms_norm(o, norm_gain).reshape(b_, s_, MEM_WIDTH)


def setup_inputs(seed: int = 0) -> dict:
    key = jax.random.key(seed)
    ks = jax.random.split(key, 24)
    f32 = jnp.float32
    nrm = lambda k, shape, scale: jax.random.normal(k, shape, f32) * scale
    gain = lambda k, shape: 1.0 + 0.02 * jax.random.normal(k, shape, f32)
    return {
        'x': nrm(ks[0], (BATCH, SEQ, D_MODEL), 1.0),
        'mem': nrm(ks[1], (BATCH, MEM_LEN, D_MODEL), 1.0),
        'norm_mix': gain(ks[2], (DEPTH, D_MODEL)),
        'w_in': nrm(ks[3], (DEPTH, D_MODEL, IN_WIDTH), D_MODEL ** -0.5),
        'gla_gate_up_fwd': nrm(ks[4], (DEPTH, GLA_GATE_RANK, GLA_QK_WIDTH), GLA_GATE_RANK ** -0.5),
        'gla_gate_bias_fwd': nrm(ks[5], (DEPTH, GLA_QK_WIDTH), 0.1),
        'gla_gate_up_bwd': nrm(ks[6], (DEPTH, GLA_GATE_RANK, GLA_QK_WIDTH), GLA_GATE_RANK ** -0.5),
        'gla_gate_bias_bwd': nrm(ks[7], (DEPTH, GLA_QK_WIDTH), 0.1),
        'gla_norm': gain(ks[8], (DEPTH, GLA_V_WIDTH)),
        'rel_bias': nrm(ks[9], (REL_BUCKETS, DIL_HEADS), 0.5),
        'dil_norm': gain(ks[10], (DEPTH, DIL_WIDTH)),
        'mem_norm': gain(ks[11], (DEPTH, D_MODEL)),
        'w_mem_kv': nrm(ks[12], (DEPTH, D_MODEL, 2 * MEM_WIDTH), D_MODEL ** -0.5),
        'mem_out_norm': gain(ks[13], (DEPTH, MEM_WIDTH)),
        'w_out': nrm(ks[14], (DEPTH, MIX_WIDTH, D_MODEL), MIX_WIDTH ** -0.5),
        'norm_mlp': gain(ks[15], (DEPTH, D_MODEL)),
        'w_up': nrm(ks[16], (DEPTH, D_MODEL, D_FF), D_MODEL ** -0.5),
        'w_down': nrm(ks[17], (DEPTH, D_FF, D_MODEL), D_FF ** -0.5),
        'norm_final': gain(ks[18], (D_MODEL,)),
    }


def reference(x, mem, norm_mix, w_in, gla_gate_up_fwd, gla_gate_bias_fwd, gla_gate_up_bwd, gla_gate_bias_bwd,
              gla_norm, rel_bias, dil_norm, mem_norm, w_mem_kv, mem_out_norm, w_out, norm_mlp, w_up, w_down,
              norm_final):
    for l in range(DEPTH):
        h = rms_norm(x, norm_mix[l])
        proj = jnp.einsum('bsd,dk->bsk', h, w_in[l])
        (g_q, g_k, g_v, g_r, lr_f, lr_b, d_q, d_k, d_v, m_q) = split_columns(proj, IN_SPLITS)
        gla_out = gla_mixer(g_q, g_k, g_v, g_r, lr_f, lr_b, gla_gate_up_fwd[l], gla_gate_bias_fwd[l],
                            gla_gate_up_bwd[l], gla_gate_bias_bwd[l], gla_norm[l])
        dil_out = dilated_mixer(d_q, d_k, d_v, rel_bias, dil_norm[l])
        mem_out = memory_mixer(m_q, mem, mem_norm[l], w_mem_kv[l], mem_out_norm[l])
        mixed = jnp.concatenate([gla_out, dil_out, mem_out], axis=-1).astype(x.dtype)
        x = x + jnp.einsum('bsk,kd->bsd', mixed, w_out[l])
        h = rms_norm(x, norm_mlp[l])
        u = jnp.square(jax.nn.relu(jnp.einsum('bsd,df->bsf', h, w_up[l])))
        x = x + jnp.einsum('bsf,fd->bsd', u, w_down[l])
    return rms_norm(x, norm_final)
```

```python
import contextlib
import math
import numpy as np
import ml_dtypes
import concourse.bass as bass
import concourse.mybir as mybir
from concourse.bass_utils import run_bass_kernel_spmd

F32 = mybir.dt.float32
BF16 = mybir.dt.bfloat16
AF = mybir.ActivationFunctionType
ALU = mybir.AluOpType

NCORES = 8
D = 2048
SEQ = 8192
T = SEQ // NCORES
NT = T // 128
C = D // 128
DEPTH = 4
DFF = 4 * D
EPS = 1e-6
MEM_LEN = 256
IN_W = 5152
O_GQ, O_GK, O_GV, O_GR, O_LRF, O_LRB, O_DQ, O_DK, O_DV, O_MQ = 0, 512, 1024, 2048, 3072, 3088, 3104, 3616, 4128, 4640
NKT = 24
VA = 132
NSLOT = 4
BLK = 4096
ENGS = ("pe", "act", "dve", "pool", "sp")


class Tile:
    __slots__ = ("name", "w", "r")

    def __init__(self, name):
        self.name = name
        self.w = None
        self.r = []


class Rec:
    def __init__(self):
        self.calls = []

    def __getattr__(self, name):
        def f(*a, **k):
            self.calls.append((name, a, k))
            return None
        return f


def record(fn):
    r = Rec()
    fn(r)
    assert r.calls
    return r.calls


class Sched:
    def __init__(self, nc, stack):
        self.nc = nc
        self.stack = stack
        self.streams = {e: [] for e in ENGS}
        self.cnt = {e: 0 for e in ENGS}
        self.waited = {e: {} for e in ENGS}
        self.sems = {}
        self.dma_cnt = {}
        for e in ("pe", "act", "dve", "pool"):
            self.sems[e] = stack.enter_context(nc.semaphore("s_" + e))

    def dma_sem(self, name):
        key = "dma:" + name
        if key not in self.sems:
            self.sems[key] = self.stack.enter_context(self.nc.semaphore("d_" + name))
            self.dma_cnt[key] = 0
        return key

    def _deps(self, eng, reads, writes):
        deps = {}

        def add(tok):
            if tok is None:
                return
            k, c = tok
            if deps.get(k, 0) < c:
                deps[k] = c
        for t in reads:
            add(t.w)
        for t in writes:
            add(t.w)
            for r in t.r:
                add(r)
        waits = []
        for k, c in deps.items():
            if k == eng and eng not in ("act", "dve"):
                continue
            if self.waited[eng].get(k, 0) >= c:
                continue
            self.waited[eng][k] = c
            waits.append((k, c))
        return waits

    def _commit(self, tok, reads, writes):
        for t in reads:
            t.r.append(tok)
        for t in writes:
            t.w = tok
            t.r = []

    def op(self, eng, fn, reads=(), writes=()):
        waits = self._deps(eng, reads, writes)
        self.cnt[eng] += 1
        tok = (eng, self.cnt[eng])
        self.streams[eng].append((waits, record(fn), eng, 1))
        self._commit(tok, reads, writes)
        return tok

    def dma(self, eng, semname, fn, reads=(), writes=()):
        key = self.dma_sem(semname)
        waits = self._deps(eng, reads, writes)
        prev = self.dma_cnt[key]
        if prev > 0 and self.waited[eng].get(key, 0) < prev:
            self.waited[eng][key] = prev
            waits.append((key, prev))
        self.dma_cnt[key] += 16
        tok = (key, self.dma_cnt[key])
        self.streams[eng].append((waits, record(fn), key, 16))
        self._commit(tok, reads, writes)
        return tok

    def wait_all_dma(self, eng):
        waits = [(k, c) for k, c in self.dma_cnt.items() if c > 0]
        self.streams[eng].append((waits, None, None, 0))

    def emit(self):
        nc = self.nc
        with nc.Block() as block:
            def run(e, name):
                for waits, fn, key, inc in self.streams[name]:
                    for k, c in waits:
                        e.wait_ge(self.sems[k], c)
                    if fn is not None:
                        ins = None
                        for (nm, a, k) in fn:
                            ins = getattr(e, nm)(*a, **k)
                        ins.then_inc(self.sems[key], inc)

            @block.tensor
            def _(e):
                run(e, "pe")

            @block.scalar
            def _(e):
                run(e, "act")

            @block.vector
            def _(e):
                run(e, "dve")

            @block.gpsimd
            def _(e):
                run(e, "pool")

            @block.sync
            def _(e):
                run(e, "sp")


PW = 256


class Buf:
    def __init__(self, pool, off, n, nbytes):
        self.pool, self.off, self.n, self.nb = pool, off, n, nbytes
        self.tiles = pool.tiles[off // PW:(off + n + PW - 1) // PW]

    def f32(self):
        return self.pool.t[:, self.off:self.off + self.nb // 4]

    def bf(self):
        return self.pool.t[:, self.off:self.off + self.n].bitcast(BF16)[:, 0:self.nb // 2]


class PagePool:
    def __init__(self, t, nwords):
        self.t = t
        self.np = nwords // PW
        self.tiles = [Tile(f"pg{i}") for i in range(self.np)]
        self.used = [False] * self.np
        self.peak = 0
        self.cur = 0

    def alloc(self, nbytes):
        k = (nbytes + 4 * PW - 1) // (4 * PW)
        for start in (self.cur, 0):
            run = 0
            for i in range(start, self.np):
                run = run + 1 if not self.used[i] else 0
                if run == k:
                    s = i - k + 1
                    for j in range(s, i + 1):
                        self.used[j] = True
                    self.peak = max(self.peak, sum(self.used))
                    self.cur = (i + 1) % self.np
                    return Buf(self, s * PW, k * PW, nbytes)
        raise RuntimeError(f"pool OOM: need {k} pages, used {sum(self.used)}/{self.np}")

    def free(self, *bufs):
        for b in bufs:
            for j in range(b.off // PW, (b.off + b.n) // PW):
                self.used[j] = False


def mmgroup(items):
    def fn(e):
        ins = None
        for (o, l, r, st, sp) in items:
            ins = e.matmul(o, lhsT=l, rhs=r, start=st, stop=sp)
        return ins
    return fn


def blocks_A():
    b = []
    b += [("w_in", 0, 16, O_DK + 256 * j, 256) for j in range(2)]
    b += [("w_in", 0, 16, O_DV + 256 * j, 256) for j in range(2)]
    b += [("w_in", 0, 16, O_LRF, 32)]
    for h in range(4):
        b += [("gqk", h)]
        b += [("w_in", 0, 16, O_GV + 256 * h, 256)]
    return b


def blocks_B():
    b = []
    b += [("w_mem_kv", 0, 16, 256 * j, 256) for j in range(4)]
    b += [("w_in", 0, 16, O_MQ + 256 * j, 256) for j in range(2)]
    b += [("w_in", 0, 16, O_DQ + 256 * j, 256) for j in range(2)]
    b += [("w_out", 8, 8, 512 * j, 512) for j in range(4)]
    b += [("w_in", 0, 16, O_LRF, 32)]
    for h in range(4):
        b += [("gqk", h)]
        b += [("w_in", 0, 16, O_GV + 256 * h, 256)]
        b += [("w_in", 0, 16, O_GR + 256 * h, 256)]
    b += [("w_out", 0, 8, 512 * j, 512) for j in range(4)]
    for fq in range(4):
        b += [("w_up", 0, 16, fq * 2048 + 256 * j, 256) for j in range(8)]
        b += [("w_down", fq * 16, 16, 256 * j, 256) for j in range(8)]
    return b


def host_blocks(blist, w):
    out = np.zeros((len(blist), 128, BLK), np.float32)
    for i, bd in enumerate(blist):
        if bd[0] == "gqk":
            h = bd[1]
            m = w["w_in"]
            a = np.concatenate([m[:, O_GQ + 128 * h:O_GQ + 128 * h + 128],
                                m[:, O_GK + 128 * h:O_GK + 128 * h + 128]], axis=1)
            blk = a.reshape(16, 128, 256).transpose(1, 0, 2).reshape(128, 16 * 256)
        else:
            name, kc0, nkc, c0, ncol = bd
            m = w[name][kc0 * 128:(kc0 + nkc) * 128, c0:c0 + ncol]
            blk = m.reshape(nkc, 128, ncol).transpose(1, 0, 2).reshape(128, nkc * ncol)
        out[i, :, :blk.shape[1]] = blk
    return out


def t5_bucket_np(rel):
    half, max_exact = 16, 8
    ret = np.where(rel > 0, half, 0)
    n = np.abs(rel)
    nf = np.maximum(n, 1).astype(np.float32)
    large = max_exact + (np.log(nf / np.float32(max_exact)) / np.float32(math.log(1024 / max_exact))
                         * np.float32(half - max_exact)).astype(np.int32)
    large = np.minimum(large, half - 1)
    return ret + np.where(n < max_exact, n, large)


def key_tile_offsets():
    offs = [(1, -64), (1, 64)]
    offs += [(4, -256 + 128 * j) for j in range(5)]
    offs += [(16, -1024 + 128 * j) for j in range(17)]
    return offs


def host_bias_tiles(rel_bias):
    out = np.full((4, 128, NKT, 128), -30000.0, np.float32)
    k = np.arange(128)[:, None]
    q = np.arange(128)[None, :]
    for j, (d, off) in enumerate(key_tile_offsets()):
        rel = off + k - q
        valid = (rel % d == 0) & (np.abs(rel) <= 64 * d)
        bk = t5_bucket_np(rel)
        for h in range(4):
            vals = rel_bias[bk, h]
            out[h, :, j, :] = np.where(valid, vals, np.float32(-30000.0))
    return out


def host_consts():
    s = np.arange(128)[:, None]
    t = np.arange(128)[None, :]
    v = np.float32(-1.0 / 16.0)
    cm = np.zeros((4, 128, 128), np.float32)
    cm[0] = np.where(s <= t, v, 0)
    cm[1] = np.where(s >= t, v, 0)
    cm[2] = np.where(s > t, v, 0)
    cm[3] = np.where(s < t, v, 0)
    mask2 = np.concatenate([(s <= t), (s >= t)], axis=1).astype(np.float32)
    ident = np.eye(128, dtype=np.float32)
    ones = np.ones((128, 128), np.float32)
    return cm, mask2, ident, ones


def feat_cols(vec):
    return np.ascontiguousarray(vec.reshape(-1, 128).T)


class Prog:
    def __init__(self, mode, last=False):
        self.mode = mode
        self.last = last
        self.nc = bass.Bass("TRN2", target_bir_lowering=False)

    def build(self):
        nc = self.nc
        mode = self.mode
        with contextlib.ExitStack() as st:
            st.enter_context(nc.allow_low_precision("bf16 matmul operands, fp32 accumulate"))
            self.st = st
            self.S = S = Sched(nc, st)
            din = lambda name, shape, dt=F32: nc.dram_tensor(name, shape, dt, kind="ExternalInput").ap()
            dout = lambda name, shape, dt=F32: nc.dram_tensor(name, shape, dt, kind="ExternalOutput").ap()
            sb = lambda name, shape, dt: st.enter_context(nc.sbuf_tensor(name, shape, dt))

            self.blist = blocks_A() if mode == "A" else blocks_B()
            self.d_x = din("xT", [128, C, T])
            self.d_w = din("wst", [len(self.blist), 128, BLK])
            self.d_gains = din("gains", [128, 80])
            self.d_cm = din("cm", [4, 128, 128])
            self.d_mask2 = din("mask2", [128, 256])
            self.d_ident = din("ident", [128, 128])
            self.d_ones = din("ones", [128, 128])
            self.d_upaug = din("upaug", [2, 17, 512])
            if mode == "A":
                self.o_kT = dout("o_kT", [4, 128, T], BF16)
                self.o_va = dout("o_va", [T, 4 * VA], BF16)
                self.o_S = dout("o_S", [2, 4, 128, 256])
                self.o_D = dout("o_D", [2, 4, 128, 1])
            else:
                self.d_memT = din("memT", [128, C, MEM_LEN])
                self.d_kT = din("kT_all", [4, 128, 3 * T], BF16)
                self.d_va = din("va_al", [3 * T, 4 * VA], BF16)
                self.d_vs = din("va_sh", [9 * 128, 4 * VA], BF16)
                self.d_Sp = din("Sprev", [2, 7, 4, 128, 256])
                self.d_Dp = din("Dprev", [2, 4, 128, 7])
                self.d_bias = din("biasT", [4, 128, NKT * 128])
                self.o_y = dout("yT", [128, C, T])

            self.xT = sb("xT_sb", [128, C, T], F32)
            self.t_x = [Tile(f"x{c}") for c in range(C)]
            self.hT = sb("hT_sb", [128, C, T], BF16)
            self.t_h = [Tile(f"h{c}") for c in range(C)]
            self.wsl = sb("wslots", [128, NSLOT, BLK], BF16)
            self.t_ws = [Tile(f"ws{i}") for i in range(NSLOT)]
            self.gains = sb("gains_sb", [128, 80], F32)
            self.t_gains = Tile("gains")
            self.cm = sb("cm_sb", [128, 4, 128], F32)
            self.mask2 = sb("mask2_sb", [128, 256], F32)
            self.ident = sb("ident_sb", [128, 128], BF16)
            self.ones = sb("ones_sb", [128, 128], BF16)
            self.t_const = Tile("const")
            self.t_constb = Tile("constb")
            npw = (nc.sbuf_bytes_remaining - 2048) // 4
            npw = (npw // PW) * PW
            self.pool = PagePool(sb("pool_sb", [128, npw], F32), npw)
            self.ps = [st.enter_context(nc.psum_tensor(f"ps{i}", [128, 512], F32)) for i in range(8)]
            self.t_ps = [Tile(f"ps{i}") for i in range(8)]
            self.ps_i = 0
            self.w_load = 0
            self.w_use = 0

            S.dma("sp", "c0", lambda e: e.dma_start(out=self.gains[:], in_=self.d_gains[:, :]), writes=[self.t_gains])
            S.dma("sp", "c1", lambda e: e.dma_start(out=self.cm[:], in_=self.d_cm.rearrange("a p n -> p a n")), writes=[self.t_const])
            S.dma("sp", "c1", lambda e: e.dma_start(out=self.mask2[:], in_=self.d_mask2[:, :]), writes=[self.t_const])
            S.dma("pool", "c2", lambda e: e.dma_start(out=self.ident[:], in_=self.d_ident[:, :]), writes=[self.t_constb])
            S.dma("pool", "c2", lambda e: e.dma_start(out=self.ones[:], in_=self.d_ones[:, :]), writes=[self.t_constb])
            for c in range(C):
                S.dma("sp", f"x{c % 4}", (lambda c: lambda e: e.dma_start(out=self.xT[:, c, :], in_=self.d_x[:, c, :]))(c),
                      writes=[self.t_x[c]])

            if mode == "A":
                self.program_A()
            else:
                self.program_B()
            S.emit()
        return nc

    def bank(self):
        i = self.ps_i
        self.ps_i = (i + 1) % 8
        return self.ps[i], self.t_ps[i]

    def wnext(self):
        i = self.w_use
        self.w_use += 1
        nblk = len(self.blist)
        while self.w_load < min(nblk, i + NSLOT - 1):
            j = self.w_load
            s = j % NSLOT
            self.S.dma("pool", f"w{s}", (lambda j, s: lambda e: e.dma_start(out=self.wsl[:, s, :], in_=self.d_w[j]))(j, s),
                       writes=[self.t_ws[s]])
            self.w_load += 1
        s = i % NSLOT
        return self.wsl[:, s, :], self.t_ws[s]

    def oname(self):
        self._on = getattr(self, "_on", 0) + 1
        return f"oA{self._on % 4}"

    def gcol(self, idx):
        return self.gains[:, idx:idx + 1]

    def rstd_from(self, out_ap, in_ap, scale, reads, writes):
        S = self.S
        S.op("act", lambda e: e.activation(out=out_ap, in_=in_ap, func=AF.Ln, scale=scale, bias=EPS_AP(self)), reads=reads + [self.t_eps], writes=writes)
        S.op("act", lambda e: e.activation(out=out_ap, in_=out_ap, func=AF.Exp, scale=-0.5), reads=writes, writes=writes)

    def norm_T(self, src, t_src, ntok, gbase, dst, t_dst, out_f32_dram=None):
        S, pool = self.S, self.pool
        nh = (ntok + 511) // 512
        for th in range(nh):
            n = min(512, ntok - th * 512)
            sl = slice(th * 512, th * 512 + n)
            acc, t_acc = self.bank()
            sq = [pool.alloc(n * 2) for _ in range(2)]
            for c in range(C):
                b = sq[c % 2]
                S.op("act", (lambda c, b: lambda e: e.activation(out=b.bf()[:, 0:n], in_=src[:, c, sl], func=AF.Square))(c, b),
                     reads=t_src[c], writes=b.tiles)
                S.op("pe", (lambda c, b: lambda e: e.matmul(acc[:, 0:n], lhsT=self.ones[:], rhs=b.bf()[:, 0:n], start=(c == 0), stop=(c == C - 1)))(c, b),
                     reads=b.tiles + [self.t_constb], writes=[t_acc])
            rs = pool.alloc(n * 4)
            self.rstd_from(rs.f32()[:, 0:n], acc[:, 0:n], 1.0 / D, [t_acc], rs.tiles)
            for c in range(C):
                if out_f32_dram is None:
                    S.op("dve", (lambda c: lambda e: e.scalar_tensor_tensor(out=dst[:, c, sl], in0=src[:, c, sl], scalar=self.gcol(gbase + c),
                                                                           in1=rs.f32()[:, 0:n], op0=ALU.mult, op1=ALU.mult))(c),
                         reads=t_src[c] + [self.t_gains] + rs.tiles, writes=t_dst[c])
                else:
                    ob = pool.alloc(n * 4)
                    S.op("dve", (lambda c, ob: lambda e: e.scalar_tensor_tensor(out=ob.f32()[:, 0:n], in0=src[:, c, sl], scalar=self.gcol(gbase + c),
                                                                               in1=rs.f32()[:, 0:n], op0=ALU.mult, op1=ALU.mult))(c, ob),
                         reads=t_src[c] + [self.t_gains] + rs.tiles, writes=ob.tiles)
                    S.dma("sp", f"yout{c % 4}", (lambda c, ob: lambda e: e.dma_start(out=out_f32_dram[:, c, sl], in_=ob.f32()[:, 0:n]))(c, ob),
                          reads=ob.tiles)
                    pool.free(ob)
            pool.free(rs, *sq)

    def proj_feat(self, wap, wt, kcols, col0, M, rhsT, t_rhs, ntok, evac):
        S = self.S
        w3 = wap[:, 0:16 * kcols].rearrange("p (k n) -> p k n", n=kcols)
        nh = (ntok + 511) // 512
        for th in range(nh):
            n = min(512, ntok - th * 512)
            bk, tb = self.bank()
            items = [(bk[0:M, 0:n], w3[:, kc, col0:col0 + M], rhsT[:, kc, th * 512:th * 512 + n], kc == 0, kc == C - 1) for kc in range(C)]
            S.op("pe", mmgroup(items), reads=[wt] + t_rhs, writes=[tb])
            evac(th, n, bk, tb)

    def proj_tok(self, wap, wt, kcols, col0, N, lhsT, t_lhs, tt, evac):
        S = self.S
        w3 = wap[:, 0:16 * kcols].rearrange("p (k n) -> p k n", n=kcols)
        bk, tb = self.bank()
        items = [(bk[:, 0:N], lhsT[:, kc, tt * 128:(tt + 1) * 128], w3[:, kc, col0:col0 + N], kc == 0, kc == C - 1) for kc in range(C)]
        S.op("pe", mmgroup(items), reads=[wt] + t_lhs, writes=[tb])
        evac(bk, tb)

    def setup_eps(self):
        S = self.S
        self.epsb = self.pool.alloc(1024)
        self.t_eps = Tile("eps")
        S.op("dve", lambda e: e.memset(self.epsb.f32()[:, 0:1], EPS), writes=[self.t_eps])

    def load_upaug(self):
        S, pool = self.S, self.pool
        self.upaug = pool.alloc(2 * 512 * 2)
        ap = self.upaug.bf().rearrange("p (a n) -> p a n", n=512)
        S.dma("pool", "upaug", lambda e: e.dma_start(out=ap[0:17, :, :], in_=self.d_upaug.rearrange("a r n -> r a n")), writes=self.upaug.tiles)
        self.upaug_ap = ap

    def compute_lr(self):
        S, pool = self.S, self.pool
        wap, wt = self.wnext()
        self.lrT = pool.alloc(2 * T * 2)
        lr = self.lrT.bf().rearrange("p (a n) -> p a n", n=T)
        S.op("dve", lambda e: e.memset(lr[0:17, :, :], 1.0), writes=self.lrT.tiles)
        for d in range(2):
            def evac(th, n, bk, tb, d=d):
                S.op("act", lambda e: e.activation(out=lr[0:16, d, th * 512:th * 512 + n], in_=bk[0:16, 0:n], func=AF.Copy),
                     reads=[tb], writes=self.lrT.tiles)
            self.proj_feat(wap, wt, 32, 16 * d, 16, self.hT, self.t_h, T, evac)
        self.lr_ap = lr

    def gla_gates(self, h, d):
        S, pool = self.S, self.pool
        g = pool.alloc(NT * 128 * 4)
        g3 = g.f32().rearrange("p (a n) -> p a n", n=128)
        for half in range(2):
            bk, tb = self.bank()
            items = []
            for i in range(4):
                tt = half * 4 + i
                items.append((bk[:, i * 128:(i + 1) * 128], self.lr_ap[0:17, d, tt * 128:(tt + 1) * 128],
                              self.upaug_ap[0:17, d, h * 128:(h + 1) * 128], True, True))
            S.op("pe", mmgroup(items), reads=self.lrT.tiles + self.upaug.tiles, writes=[tb])
            dst = g.f32()[:, half * 512:(half + 1) * 512]
            S.op("act", lambda e, bk=bk, dst=dst: e.activation(out=dst, in_=bk[:, :], func=AF.Exp, scale=-1.0), reads=[tb], writes=g.tiles)
            S.op("act", lambda e, dst=dst: e.activation(out=dst, in_=dst, func=AF.Ln, bias=self.one_ap()), reads=g.tiles + [self.t_eps], writes=g.tiles)
        return g, g3

    def one_ap(self):
        return self.epsb.f32()[:, 1:2]

    def gla_decays(self, h, d, g, g3, qT, kT, need_q):
        S, pool = self.S, self.pool
        res = {}
        bb = [self.bank(), self.bank()]
        for half in range(2):
            bk, tb = bb[half]
            items = []
            for i in range(4):
                tt = half * 4 + i
                items.append((bk[:, i * 128:(i + 1) * 128], g3[:, tt, :], self.cm[:, d, :], True, True))
            S.op("pe", mmgroup(items), reads=g.tiles + [self.t_const], writes=[tb])
        tmp = pool.alloc(T * 4)
        for half in range(2):
            bk, tb = bb[half]
            S.op("act", lambda e, bk=bk, half=half: e.activation(out=tmp.f32()[:, half * 512:(half + 1) * 512], in_=bk[:, :], func=AF.Exp),
                 reads=[tb], writes=tmp.tiles)
        dec = pool.alloc(1024)
        col = 127 if d == 0 else 0
        S.op("dve", lambda e: e.tensor_copy(out=dec.f32()[:, 0:NT], in_=tmp.f32()[:, col:T:128]), reads=tmp.tiles, writes=dec.tiles)
        res["dec"] = dec
        if need_q:
            qd = pool.alloc(T * 2)
            S.op("dve", lambda e: e.tensor_tensor(out=qd.bf()[:, 0:T], in0=qT.f32()[:, 0:T], in1=tmp.f32()[:, 0:T], op=ALU.mult),
                 reads=qT.tiles + tmp.tiles, writes=qd.tiles)
            res["q_dec"] = qd
            tmp2 = pool.alloc(T * 4)
            for half in range(2):
                bk, tb = bb[half]
                S.op("act", lambda e, bk=bk, half=half: e.activation(out=tmp2.f32()[:, half * 512:(half + 1) * 512], in_=bk[:, :], func=AF.Exp, scale=-1.0),
                     reads=[tb], writes=tmp2.tiles)
            ki = pool.alloc(T * 2)
            S.op("dve", lambda e: e.tensor_tensor(out=ki.bf()[:, 0:T], in0=kT.f32()[:, 0:T], in1=tmp2.f32()[:, 0:T], op=ALU.mult),
                 reads=kT.tiles + tmp2.tiles, writes=ki.tiles)
            res["k_inv"] = ki
            pool.free(tmp2)
        eb = [self.bank(), self.bank()]
        for half in range(2):
            bk, tb = eb[half]
            items = []
            for i in range(4):
                tt = half * 4 + i
                items.append((bk[:, i * 128:(i + 1) * 128], g3[:, tt, :], self.cm[:, 2 + d, :], True, True))
            S.op("pe", mmgroup(items), reads=g.tiles + [self.t_const], writes=[tb])
        for half in range(2):
            bk, tb = eb[half]
            S.op("act", lambda e, bk=bk, half=half: e.activation(out=tmp.f32()[:, half * 512:(half + 1) * 512], in_=bk[:, :], func=AF.Exp),
                 reads=[tb], writes=tmp.tiles)
        keT = pool.alloc(T * 2)
        S.op("dve", lambda e: e.tensor_tensor(out=keT.bf()[:, 0:T], in0=kT.f32()[:, 0:T], in1=tmp.f32()[:, 0:T], op=ALU.mult),
             reads=kT.tiles + tmp.tiles, writes=keT.tiles)
        bk, tb = self.bank()
        bkb = bk[:].bitcast(BF16)
        def trfn(e):
            ins = None
            for tt in range(NT):
                ins = e.transpose(bkb[:, tt * 128:(tt + 1) * 128], keT.bf()[:, tt * 128:(tt + 1) * 128], self.ident[:])
            return ins
        S.op("pe", trfn, reads=keT.tiles + [self.t_constb], writes=[tb])
        ke = pool.alloc(T * 2)
        S.op("act", lambda e: e.activation(out=ke.bf()[:, 0:T], in_=bkb[:, :], func=AF.Copy), reads=[tb], writes=ke.tiles)
        res["k_end"] = ke
        pool.free(tmp, keT)
        return res

    def gla_scan(self, d, ke, vh, dec, st, snaps):
        S, pool = self.S, self.pool
        ke3 = ke.bf().rearrange("p (a n) -> p a n", n=128)
        v3 = vh.bf().rearrange("p (a n) -> p a n", n=256)
        order = list(range(NT)) if d == 0 else list(range(NT - 1, -1, -1))
        for tt in order:
            if snaps is not None:
                sn3 = snaps.bf().rearrange("p (a n) -> p a n", n=256)
                S.op("act", lambda e, tt=tt, sn3=sn3: e.activation(out=sn3[:, tt, :], in_=st.f32()[:, 0:256], func=AF.Copy),
                     reads=st.tiles, writes=snaps.tiles)
            if snaps is not None and tt == order[-1]:
                break
            bk, tb = self.bank()
            S.op("pe", lambda e, tt=tt, bk=bk: e.matmul(bk[:, 0:256], lhsT=ke3[:, tt, :], rhs=v3[:, tt, :], start=True, stop=True),
                 reads=ke.tiles + vh.tiles, writes=[tb])
            S.op("dve", lambda e, tt=tt, bk=bk: e.scalar_tensor_tensor(out=st.f32()[:, 0:256], in0=st.f32()[:, 0:256], scalar=dec.f32()[:, tt:tt + 1],
                                                                      in1=bk[:, 0:256], op0=ALU.mult, op1=ALU.add),
                 reads=st.tiles + dec.tiles + [tb], writes=st.tiles)

    def program_A(self):
        S, pool = self.S, self.pool
        self.setup_eps()
        S.op("dve", lambda e: e.memset(self.epsb.f32()[:, 1:2], 1.0), writes=[self.t_eps])
        self.norm_T(self.xT, [[t] for t in self.t_x], T, 0, self.hT, [[t] for t in self.t_h])
        for j in range(2):
            wap, wt = self.wnext()
            for hh in range(2):
                h = 2 * j + hh
                kb = pool.alloc(T * 2)
                def evac(th, n, bk, tb, kb=kb):
                    S.op("act", lambda e: e.activation(out=kb.bf()[:, th * 512:th * 512 + n], in_=bk[:, 0:n], func=AF.Copy), reads=[tb], writes=kb.tiles)
                self.proj_feat(wap, wt, 256, 128 * hh, 128, self.hT, self.t_h, T, evac)
                S.dma("sp", self.oname(), lambda e, h=h, kb=kb: e.dma_start(out=self.o_kT[h], in_=kb.bf()[:, 0:T]), reads=kb.tiles)
                pool.free(kb)
        wv = [self.wnext(), self.wnext()]
        for tt in range(NT):
            vb = pool.alloc(4 * VA * 2)
            v3 = vb.bf().rearrange("p (a n) -> p a n", n=VA)
            S.op("dve", lambda e, v3=v3: e.memset(v3[:, :, 128:VA], 1.0), writes=vb.tiles)
            for j in range(2):
                def evac(bk, tb, j=j, v3=v3, vb=vb):
                    S.op("act", lambda e: e.activation(out=v3[:, 2 * j:2 * j + 2, 0:128], in_=bk[:, 0:256].rearrange("p (a n) -> p a n", n=128), func=AF.Copy),
                         reads=[tb], writes=vb.tiles)
                self.proj_tok(wv[j][0], wv[j][1], 256, 0, 256, self.hT, self.t_h, tt, evac)
            S.dma("sp", self.oname(), lambda e, tt=tt, vb=vb: e.dma_start(out=self.o_va[tt * 128:(tt + 1) * 128, :], in_=vb.bf()[:, 0:4 * VA]), reads=vb.tiles)
            pool.free(vb)
        self.load_upaug()
        self.compute_lr()
        for h in range(4):
            wqk, tqk = self.wnext()
            kT = pool.alloc(T * 4)
            def evk(th, n, bk, tb, kT=kT):
                S.op("act", lambda e: e.activation(out=kT.f32()[:, th * 512:th * 512 + n], in_=bk[:, 0:n], func=AF.Copy), reads=[tb], writes=kT.tiles)
            self.proj_feat(wqk, tqk, 256, 128, 128, self.hT, self.t_h, T, evk)
            wv_, tv_ = self.wnext()
            vh = pool.alloc(NT * 256 * 2)
            v3 = vh.bf().rearrange("p (a n) -> p a n", n=256)
            for tt in range(NT):
                def evv(bk, tb, tt=tt):
                    S.op("act", lambda e: e.activation(out=v3[:, tt, :], in_=bk[:, 0:256], func=AF.Copy), reads=[tb], writes=vh.tiles)
                self.proj_tok(wv_, tv_, 256, 0, 256, self.hT, self.t_h, tt, evv)
            for d in range(2):
                g, g3 = self.gla_gates(h, d)
                r = self.gla_decays(h, d, g, g3, None, kT, False)
                pool.free(g)
                st_ = pool.alloc(1024)
                S.op("dve", lambda e, st_=st_: e.memset(st_.f32()[:, 0:256], 0.0), writes=st_.tiles)
                self.gla_scan(d, r["k_end"], vh, r["dec"], st_, None)
                S.dma("sp", self.oname(), lambda e, d=d, h=h, st_=st_: e.dma_start(out=self.o_S[d, h], in_=st_.f32()[:, 0:256]), reads=st_.tiles)
                dec = r["dec"]
                dt_ = pool.alloc(1024)
                S.op("dve", lambda e, dec=dec, dt_=dt_: e.tensor_tensor(out=dt_.f32()[:, 0:1], in0=dec.f32()[:, 0:1], in1=dec.f32()[:, 1:2], op=ALU.mult),
                     reads=dec.tiles, writes=dt_.tiles)
                for tt in range(2, NT):
                    S.op("dve", lambda e, dec=dec, dt_=dt_, tt=tt: e.tensor_tensor(out=dt_.f32()[:, 0:1], in0=dt_.f32()[:, 0:1], in1=dec.f32()[:, tt:tt + 1], op=ALU.mult),
                         reads=dec.tiles + dt_.tiles, writes=dt_.tiles)
                S.dma("sp", self.oname(), lambda e, d=d, h=h, dt_=dt_: e.dma_start(out=self.o_D[d, h], in_=dt_.f32()[:, 0:1]), reads=dt_.tiles)
                pool.free(st_, dt_, r["k_end"], r["dec"])
            pool.free(kT, vh)
        S.wait_all_dma("sp")

    def attn_epilogue(self, obk, tob, mixed_ap, t_mixed):
        S, pool = self.S, self.pool
        sm = pool.alloc(1024)
        on = pool.alloc(128 * 4)
        junk = pool.alloc(128 * 4)
        S.op("dve", lambda e: e.reciprocal(out=sm.f32()[:, 0:1], in_=obk[:, 128:129]), reads=[tob], writes=sm.tiles)
        S.op("dve", lambda e: e.tensor_scalar(out=on.f32()[:, 0:128], in0=obk[:, 0:128], scalar1=sm.f32()[:, 0:1], scalar2=0.0, op0=ALU.mult, op1=ALU.add),
             reads=[tob] + sm.tiles, writes=on.tiles)
        S.op("act", lambda e: e.activation(out=junk.f32()[:, 0:128], in_=on.f32()[:, 0:128], func=AF.Square, accum_out=sm.f32()[:, 1:2]),
             reads=on.tiles, writes=junk.tiles + sm.tiles)
        self.rstd_from(sm.f32()[:, 2:3], sm.f32()[:, 1:2], 1.0 / 128, sm.tiles, sm.tiles)
        S.op("act", lambda e: e.activation(out=mixed_ap, in_=on.f32()[:, 0:128], func=AF.Copy, scale=sm.f32()[:, 2:3]),
             reads=on.tiles + sm.tiles, writes=t_mixed)
        pool.free(sm, on, junk)

    def transpose_to_mixT(self, mixed, nchunk, chunk0, gbase):
        S = self.S
        w = nchunk * 128
        m3 = mixed.bf().rearrange("p (a n) -> p a n", n=w)
        for ci in range(nchunk):
            bk, tb = self.bank()
            bkb = bk[:].bitcast(BF16)
            def trfn(e, ci=ci, bkb=bkb):
                ins = None
                for tt in range(NT):
                    ins = e.transpose(bkb[:, tt * 128:(tt + 1) * 128], m3[:, tt, ci * 128:(ci + 1) * 128], self.ident[:])
                return ins
            S.op("pe", trfn, reads=mixed.tiles + [self.t_constb], writes=[tb])
            dst = self.mixT_ap[:, (chunk0 + ci) % 8, :]
            S.op("act", lambda e, dst=dst, bkb=bkb, ci=ci: e.activation(out=dst, in_=bkb[:, :], func=AF.Copy, scale=self.gcol(gbase + chunk0 + ci)),
                 reads=[tb, self.t_gains], writes=self.t_mixT[(chunk0 + ci) % 8])

    def alloc_mixT(self):
        self.mixT = self.pool.alloc(8 * T * 2)
        self.mixT_ap = self.mixT.bf().rearrange("p (c n) -> p c n", n=T)
        self.t_mixT = [self.mixT.tiles[2 * c:2 * c + 2] for c in range(8)]

    def out_proj(self, chunk0):
        S = self.S
        for j in range(4):
            wap, wt = self.wnext()
            w3 = wap[:, 0:8 * 512].rearrange("p (k n) -> p k n", n=512)
            for dd in range(4):
                dc = 4 * j + dd
                for th in range(2):
                    bk, tb = self.bank()
                    items = [(bk[:, :], w3[:, kc, dd * 128:(dd + 1) * 128], self.mixT_ap[:, kc, th * 512:(th + 1) * 512], kc == 0, kc == 7) for kc in range(8)]
                    S.op("pe", mmgroup(items), reads=[wt] + self.mixT.tiles, writes=[tb])
                    S.op("dve", lambda e, dc=dc, th=th, bk=bk: e.tensor_tensor(out=self.xT[:, dc, th * 512:(th + 1) * 512], in0=self.xT[:, dc, th * 512:(th + 1) * 512],
                                                                            in1=bk[:, :], op=ALU.add),
                         reads=[tb, self.t_x[dc]], writes=[self.t_x[dc]])

    def program_B(self):
        S, pool = self.S, self.pool
        self.setup_eps()
        S.op("dve", lambda e: e.memset(self.epsb.f32()[:, 1:2], 1.0), writes=[self.t_eps])
        self.norm_T(self.xT, [[t] for t in self.t_x], T, 0, self.hT, [[t] for t in self.t_h])
        self.alloc_mixT()
        SC = 128 ** -0.5

        memT = pool.alloc(C * MEM_LEN * 4)
        m3 = memT.f32().rearrange("p (c n) -> p c n", n=MEM_LEN)
        t_mem = [[memT.tiles[c]] for c in range(C)]
        for c in range(C):
            S.dma("sp", f"mem{c % 2}", lambda e, c=c: e.dma_start(out=m3[:, c, :], in_=self.d_memT[:, c, :]), writes=t_mem[c])
        hm = pool.alloc(C * MEM_LEN * 2)
        hm3 = hm.bf().rearrange("p (c n) -> p c n", n=MEM_LEN)
        t_hm = [[hm.tiles[c // 2]] for c in range(C)]
        self.norm_T(m3, t_mem, MEM_LEN, 32, hm3, t_hm)
        kmT = pool.alloc(4 * MEM_LEN * 2)
        km3 = kmT.bf().rearrange("p (h n) -> p h n", n=MEM_LEN)
        for j in range(2):
            wap, wt = self.wnext()
            for hh in range(2):
                h = 2 * j + hh
                def evac(th, n, bk, tb, h=h):
                    S.op("act", lambda e: e.activation(out=km3[:, h, 0:n], in_=bk[:, 0:n], func=AF.Copy), reads=[tb], writes=kmT.tiles)
                self.proj_feat(wap, wt, 256, 128 * hh, 128, hm3, hm.tiles, MEM_LEN, evac)
        vma = pool.alloc(2 * 4 * VA * 2)
        vm4 = vma.bf().rearrange("p (m h n) -> p m h n", m=2, h=4)
        S.op("dve", lambda e: e.memset(vm4[:, :, :, 128:VA], 1.0), writes=vma.tiles)
        for j in range(2):
            wap, wt = self.wnext()
            for mt in range(2):
                def evac(bk, tb, j=j, mt=mt):
                    S.op("act", lambda e: e.activation(out=vm4[:, mt, 2 * j:2 * j + 2, 0:128], in_=bk[:, 0:256].rearrange("p (a n) -> p a n", n=128), func=AF.Copy),
                         reads=[tb], writes=vma.tiles)
                self.proj_tok(wap, wt, 256, 0, 256, hm3, hm.tiles, mt, evac)
        pool.free(memT, hm)
        pbufs = [pool.alloc(256 * 2) for _ in range(3)]
        for h in range(4):
            if h % 2 == 0:
                wap, wt = self.wnext()
            qT = pool.alloc(T * 2)
            def evq(th, n, bk, tb, qT=qT):
                S.op("act", lambda e: e.activation(out=qT.bf()[:, th * 512:th * 512 + n], in_=bk[:, 0:n], func=AF.Copy, scale=SC), reads=[tb], writes=qT.tiles)
            self.proj_feat(wap, wt, 256, 128 * (h % 2), 128, self.hT, self.t_h, T, evq)
            mixed = pool.alloc(NT * 128 * 2)
            mx3 = mixed.bf().rearrange("p (a n) -> p a n", n=128)
            for qt in range(NT):
                bk, tb = self.bank()
                items = [(bk[:, mt * 128:(mt + 1) * 128], km3[:, h, mt * 128:(mt + 1) * 128], qT.bf()[:, qt * 128:(qt + 1) * 128], True, True) for mt in range(2)]
                S.op("pe", mmgroup(items), reads=kmT.tiles + qT.tiles, writes=[tb])
                p = pbufs[qt % 3]
                S.op("act", lambda e, bk=bk, p=p: e.activation(out=p.bf()[:, 0:256], in_=bk[:, 0:256], func=AF.Exp), reads=[tb], writes=p.tiles)
                ob, tob = self.bank()
                items = [(ob[:, 0:129], p.bf()[:, mt * 128:(mt + 1) * 128], vm4[:, mt, h, 0:129], mt == 0, mt == 1) for mt in range(2)]
                S.op("pe", mmgroup(items), reads=p.tiles + vma.tiles, writes=[tob])
                self.attn_epilogue(ob, tob, mx3[:, qt, :], mixed.tiles)
            self.transpose_to_mixT(mixed, 1, 12 + h, 48)
            pool.free(qT, mixed)
        pool.free(kmT, vma, *pbufs)

        offs = key_tile_offsets()
        for h in range(4):
            if h % 2 == 0:
                wap, wt = self.wnext()
            qT = pool.alloc(T * 2)
            def evq(th, n, bk, tb, qT=qT):
                S.op("act", lambda e: e.activation(out=qT.bf()[:, th * 512:th * 512 + n], in_=bk[:, 0:n], func=AF.Copy, scale=SC), reads=[tb], writes=qT.tiles)
            self.proj_feat(wap, wt, 256, 128 * (h % 2), 128, self.hT, self.t_h, T, evq)
            kT = pool.alloc(3 * T * 2)
            S.dma("sp", "dk", lambda e, h=h, kT=kT: e.dma_start(out=kT.bf()[:, 0:3 * T], in_=self.d_kT[h]), writes=kT.tiles)
            va = pool.alloc(24 * VA * 2)
            va3 = va.bf()[:, 0:24 * VA].rearrange("p (a n) -> p a n", n=VA)
            S.dma("sp", "dv", lambda e, h=h, va3=va3: e.dma_start(out=va3, in_=self.d_va[:, h * VA:(h + 1) * VA].rearrange("(a p) n -> p a n", p=128)), writes=va.tiles)
            vs = pool.alloc(9 * VA * 2)
            vs3 = vs.bf()[:, 0:9 * VA].rearrange("p (a n) -> p a n", n=VA)
            S.dma("sp", "dvs", lambda e, h=h, vs3=vs3: e.dma_start(out=vs3, in_=self.d_vs[:, h * VA:(h + 1) * VA].rearrange("(a p) n -> p a n", p=128)), writes=vs.tiles)
            em = pool.alloc(NKT * 128 * 2)
            for piece in range(3):
                bt_ = pool.alloc(1024 * 4)
                S.dma("sp", "bias", lambda e, h=h, piece=piece, bt_=bt_: e.dma_start(out=bt_.f32()[:, 0:1024], in_=self.d_bias[h, :, piece * 1024:(piece + 1) * 1024]), writes=bt_.tiles)
                S.op("act", lambda e, piece=piece, bt_=bt_, em=em: e.activation(out=em.bf()[:, piece * 1024:(piece + 1) * 1024], in_=bt_.f32()[:, 0:1024], func=AF.Exp),
                     reads=bt_.tiles, writes=em.tiles)
                pool.free(bt_)
            mixed = pool.alloc(NT * 128 * 2)
            mx3 = mixed.bf().rearrange("p (a n) -> p a n", n=128)
            pes = [pool.alloc(512 * 2) for _ in range(3)]
            pms = [pool.alloc(512 * 2) for _ in range(3)]
            for qt in range(NT):
                q0 = T + qt * 128
                ob, tob = self.bank()
                for grp in range(6):
                    bk, tb = self.bank()
                    items = []
                    for i in range(4):
                        j = grp * 4 + i
                        ks = q0 + offs[j][1]
                        items.append((bk[:, i * 128:(i + 1) * 128], kT.bf()[:, ks:ks + 128], qT.bf()[:, qt * 128:(qt + 1) * 128], True, True))
                    S.op("pe", mmgroup(items), reads=kT.tiles + qT.tiles, writes=[tb])
                    pe_ = pes[(qt * 6 + grp) % 3]
                    S.op("act", lambda e, bk=bk, pe_=pe_: e.activation(out=pe_.bf()[:, 0:512], in_=bk[:, :], func=AF.Exp), reads=[tb], writes=pe_.tiles)
                    pm = pms[(qt * 6 + grp) % 3]
                    S.op("dve", lambda e, pe_=pe_, pm=pm, grp=grp, em=em: e.tensor_tensor(out=pm.bf()[:, 0:512], in0=pe_.bf()[:, 0:512], in1=em.bf()[:, grp * 512:(grp + 1) * 512], op=ALU.mult),
                         reads=pe_.tiles + em.tiles, writes=pm.tiles)
                    items = []
                    for i in range(4):
                        j = grp * 4 + i
                        ks = q0 + offs[j][1]
                        if offs[j][0] == 1:
                            vt = vs3[:, (ks - (T - 64)) // 128, 0:129]
                        else:
                            vt = va3[:, ks // 128, 0:129]
                        items.append((ob[:, 0:129], pm.bf()[:, i * 128:(i + 1) * 128], vt, j == 0, j == NKT - 1))
                    S.op("pe", mmgroup(items), reads=pm.tiles + va.tiles + vs.tiles, writes=[tob])
                self.attn_epilogue(ob, tob, mx3[:, qt, :], mixed.tiles)
            self.transpose_to_mixT(mixed, 1, 8 + h, 48)
            pool.free(qT, kT, va, vs, em, mixed, *pes, *pms)

        self.out_proj(8)
        pool.free(self.mixT)

        self.alloc_mixT()
        self.load_upaug()
        self.compute_lr()
        QS = 128 ** -0.5
        for h in range(4):
            wqk, tqk = self.wnext()
            qT = pool.alloc(T * 4)
            kT = pool.alloc(T * 4)
            def evq(th, n, bk, tb, qT=qT):
                S.op("act", lambda e: e.activation(out=qT.f32()[:, th * 512:th * 512 + n], in_=bk[:, 0:n], func=AF.Copy, scale=QS), reads=[tb], writes=qT.tiles)
            def evk(th, n, bk, tb, kT=kT):
                S.op("act", lambda e: e.activation(out=kT.f32()[:, th * 512:th * 512 + n], in_=bk[:, 0:n], func=AF.Copy), reads=[tb], writes=kT.tiles)
            self.proj_feat(wqk, tqk, 256, 0, 128, self.hT, self.t_h, T, evq)
            self.proj_feat(wqk, tqk, 256, 128, 128, self.hT, self.t_h, T, evk)
            R = []
            for d in range(2):
                g, g3 = self.gla_gates(h, d)
                R.append(self.gla_decays(h, d, g, g3, qT, kT, True))
                pool.free(g)
            pool.free(qT, kT)
            wv_, tv_ = self.wnext()
            vh = pool.alloc(NT * 256 * 2)
            v3 = vh.bf().rearrange("p (a n) -> p a n", n=256)
            for tt in range(NT):
                def evv(bk, tb, tt=tt):
                    S.op("act", lambda e: e.activation(out=v3[:, tt, :], in_=bk[:, 0:256], func=AF.Copy), reads=[tb], writes=vh.tiles)
                self.proj_tok(wv_, tv_, 256, 0, 256, self.hT, self.t_h, tt, evv)
            wr_, tr_ = self.wnext()
            sr = pool.alloc(NT * 256 * 2)
            sr3 = sr.bf().rearrange("p (a n) -> p a n", n=256)
            for tt in range(NT):
                def evr(bk, tb, tt=tt):
                    S.op("act", lambda e: e.activation(out=sr3[:, tt, :], in_=bk[:, 0:256], func=AF.Silu), reads=[tb], writes=sr.tiles)
                self.proj_tok(wr_, tr_, 256, 0, 256, self.hT, self.t_h, tt, evr)
            snaps = []
            for d in range(2):
                st_ = pool.alloc(1024)
                S.op("dve", lambda e, st_=st_: e.memset(st_.f32()[:, 0:256], 0.0), writes=st_.tiles)
                dp = pool.alloc(1024)
                S.dma("sp", "dp", lambda e, d=d, h=h, dp=dp: e.dma_start(out=dp.f32()[:, 0:7], in_=self.d_Dp[d, h]), writes=dp.tiles)
                for j in range(7):
                    sp_ = pool.alloc(1024)
                    S.dma("sp", "sp", lambda e, d=d, h=h, j=j, sp_=sp_: e.dma_start(out=sp_.f32()[:, 0:256], in_=self.d_Sp[d, j, h]), writes=sp_.tiles)
                    S.op("dve", lambda e, st_=st_, dp=dp, sp_=sp_, j=j: e.scalar_tensor_tensor(out=st_.f32()[:, 0:256], in0=st_.f32()[:, 0:256], scalar=dp.f32()[:, j:j + 1],
                                                                                             in1=sp_.f32()[:, 0:256], op0=ALU.mult, op1=ALU.add),
                         reads=st_.tiles + dp.tiles + sp_.tiles, writes=st_.tiles)
                    pool.free(sp_)
                sn = pool.alloc(NT * 256 * 2)
                self.gla_scan(d, R[d]["k_end"], vh, R[d]["dec"], st_, sn)
                snaps.append(sn)
                pool.free(st_, dp)
            mixed = pool.alloc(NT * 256 * 2)
            mx3 = mixed.bf().rearrange("p (a n) -> p a n", n=256)
            sn3 = [s_.bf().rearrange("p (a n) -> p a n", n=256) for s_ in snaps]
            for tt in range(NT):
                tsl = slice(tt * 128, (tt + 1) * 128)
                bk, tb = self.bank()
                items = [(bk[:, d * 128:(d + 1) * 128], R[d]["k_inv"].bf()[:, tsl], R[d]["q_dec"].bf()[:, tsl], True, True) for d in range(2)]
                S.op("pe", mmgroup(items), reads=R[0]["k_inv"].tiles + R[0]["q_dec"].tiles + R[1]["k_inv"].tiles + R[1]["q_dec"].tiles, writes=[tb])
                at = pool.alloc(256 * 2)
                S.op("dve", lambda e, bk=bk, at=at: e.tensor_tensor(out=at.bf()[:, 0:256], in0=bk[:, 0:256], in1=self.mask2[:, :], op=ALU.mult),
                     reads=[tb, self.t_const], writes=at.tiles)
                ob, tob = self.bank()
                items = [(ob[:, 0:256], at.bf()[:, 0:128], v3[:, tt, :], True, False),
                         (ob[:, 0:256], at.bf()[:, 128:256], v3[:, tt, :], False, False),
                         (ob[:, 0:256], R[0]["q_dec"].bf()[:, tsl], sn3[0][:, tt, :], False, False),
                         (ob[:, 0:256], R[1]["q_dec"].bf()[:, tsl], sn3[1][:, tt, :], False, True)]
                S.op("pe", mmgroup(items), reads=at.tiles + vh.tiles + R[0]["q_dec"].tiles + R[1]["q_dec"].tiles + snaps[0].tiles + snaps[1].tiles, writes=[tob])
                pool.free(at)
                sm = pool.alloc(1024)
                junk = pool.alloc(256 * 4)
                S.op("act", lambda e, ob=ob, junk=junk, sm=sm: e.activation(out=junk.f32()[:, 0:256], in_=ob[:, 0:256], func=AF.Square, accum_out=sm.f32()[:, 1:2]),
                     reads=[tob], writes=junk.tiles + sm.tiles)
                self.rstd_from(sm.f32()[:, 2:3], sm.f32()[:, 1:2], 1.0 / 256, sm.tiles, sm.tiles)
                S.op("dve", lambda e, ob=ob, sm=sm, tt=tt: e.scalar_tensor_tensor(out=mx3[:, tt, :], in0=ob[:, 0:256], scalar=sm.f32()[:, 2:3], in1=sr3[:, tt, :],
                                                                               op0=ALU.mult, op1=ALU.mult),
                     reads=[tob] + sm.tiles + sr.tiles, writes=mixed.tiles)
                pool.free(sm, junk)
            self.transpose_to_mixT(mixed, 2, 2 * h, 48)
            pool.free(mixed, vh, sr, snaps[0], snaps[1])
            for d in range(2):
                pool.free(R[d]["q_dec"], R[d]["k_inv"], R[d]["k_end"], R[d]["dec"])
        pool.free(self.lrT, self.upaug)
        self.out_proj(0)
        pool.free(self.mixT)

        self.norm_T(self.xT, [[t] for t in self.t_x], T, 16, self.hT, [[t] for t in self.t_h])
        for fq in range(4):
            uT = pool.alloc(16 * T * 2)
            u3 = uT.bf().rearrange("p (c n) -> p c n", n=T)
            for j in range(8):
                wap, wt = self.wnext()
                for cc in range(2):
                    fc = 2 * j + cc
                    def evac(th, n, bk, tb, fc=fc):
                        r_ = pool.alloc(512 * 4)
                        S.op("act", lambda e: e.activation(out=r_.f32()[:, 0:512], in_=bk[:, :], func=AF.Relu), reads=[tb], writes=r_.tiles)
                        S.op("dve", lambda e: e.tensor_tensor(out=u3[:, fc, th * 512:(th + 1) * 512], in0=r_.f32()[:, 0:512], in1=r_.f32()[:, 0:512], op=ALU.mult),
                             reads=r_.tiles, writes=uT.tiles[2 * fc:2 * fc + 2])
                        pool.free(r_)
                    self.proj_feat(wap, wt, 256, 128 * cc, 128, self.hT, self.t_h, T, evac)
            for j in range(8):
                wap, wt = self.wnext()
                w3 = wap[:, 0:16 * 256].rearrange("p (k n) -> p k n", n=256)
                for cc in range(2):
                    dc = 2 * j + cc
                    for th in range(2):
                        bk, tb = self.bank()
                        items = [(bk[:, :], w3[:, kc, cc * 128:(cc + 1) * 128], u3[:, kc, th * 512:(th + 1) * 512], kc == 0, kc == 15) for kc in range(16)]
                        S.op("pe", mmgroup(items), reads=[wt] + uT.tiles, writes=[tb])
                        S.op("dve", lambda e, dc=dc, th=th, bk=bk: e.tensor_tensor(out=self.xT[:, dc, th * 512:(th + 1) * 512], in0=self.xT[:, dc, th * 512:(th + 1) * 512],
                                                                                in1=bk[:, :], op=ALU.add),
                             reads=[tb, self.t_x[dc]], writes=[self.t_x[dc]])
            pool.free(uT)

        if self.last:
            self.norm_T(self.xT, [[t] for t in self.t_x], T, 64, None, None, out_f32_dram=self.o_y)
        else:
            for c in range(C):
                S.dma("sp", f"yout{c % 4}", lambda e, c=c: e.dma_start(out=self.o_y[:, c, :], in_=self.xT[:, c, :]), reads=[self.t_x[c]])
        S.wait_all_dma("sp")


def EPS_AP(prog):
    return prog.epsb.f32()[:, 0:1]


_PROGS = {}


def get_prog(mode, last=False):
    key = (mode, last)
    if key not in _PROGS:
        p = Prog(mode, last)
        p.build()
        _PROGS[key] = p
    return _PROGS[key]


def kernel(x, mem, norm_mix, w_in, gla_gate_up_fwd, gla_gate_bias_fwd, gla_gate_up_bwd, gla_gate_bias_bwd,
           gla_norm, rel_bias, dil_norm, mem_norm, w_mem_kv, mem_out_norm, w_out, norm_mlp, w_up, w_down,
           norm_final):
    f = lambda a: np.asarray(a, dtype=np.float32)
    x, mem, w_in, w_mem_kv, w_out, w_up, w_down = map(f, (x, mem, w_in, w_mem_kv, w_out, w_up, w_down))
    cm, mask2, ident, ones = host_consts()
    biasT = host_bias_tiles(f(rel_bias)).reshape(4, 128, NKT * 128)
    memT = np.ascontiguousarray(mem[0].T.reshape(C, 128, MEM_LEN).transpose(1, 0, 2))
    xs = [np.ascontiguousarray(x[0, c * T:(c + 1) * T, :].T.reshape(C, 128, T).transpose(1, 0, 2)) for c in range(NCORES)]
    bA, bB = blocks_A(), blocks_B()
    cores = list(range(NCORES))
    for l in range(DEPTH):
        wl = {"w_in": w_in[l], "w_mem_kv": w_mem_kv[l], "w_out": w_out[l], "w_up": w_up[l], "w_down": w_down[l]}
        gains = np.concatenate([feat_cols(f(norm_mix)[l]), feat_cols(f(norm_mlp)[l]), feat_cols(f(mem_norm)[l]),
                                feat_cols(np.concatenate([f(gla_norm)[l], f(dil_norm)[l], f(mem_out_norm)[l]])),
                                feat_cols(f(norm_final))], axis=1)
        upaug = np.stack([np.concatenate([f(gla_gate_up_fwd)[l], f(gla_gate_bias_fwd)[l][None]], axis=0),
                          np.concatenate([f(gla_gate_up_bwd)[l], f(gla_gate_bias_bwd)[l][None]], axis=0)])
        common = dict(gains=np.ascontiguousarray(gains), cm=cm, mask2=mask2, ident=ident, ones=ones, upaug=np.ascontiguousarray(upaug))
        wA = host_blocks(bA, wl)
        pA = get_prog("A")
        resA = run_bass_kernel_spmd(pA.nc, [dict(xT=xs[c], wst=wA, **common) for c in cores], core_ids=cores).results
        kT = np.concatenate([np.asarray(r["o_kT"]) for r in resA], axis=2)
        va = np.concatenate([np.asarray(r["o_va"]) for r in resA], axis=0)
        Sall = np.stack([np.asarray(r["o_S"]) for r in resA])
        Dall = np.stack([np.asarray(r["o_D"]) for r in resA])
        kTp = np.concatenate([np.zeros_like(kT[:, :, :T]), kT, np.zeros_like(kT[:, :, :T])], axis=2)
        vap = np.concatenate([np.zeros_like(va[:T]), va, np.zeros_like(va[:T])], axis=0)
        wB = host_blocks(bB, wl)
        pB = get_prog("B", last=(l == DEPTH - 1))
        maps = []
        for c in cores:
            Sp = np.zeros((2, 7, 4, 128, 256), np.float32)
            Dp = np.zeros((2, 4, 128, 7), np.float32)
            for j in range(7):
                cf = c - 7 + j
                if cf >= 0:
                    Sp[0, j], Dp[0, :, :, j] = Sall[cf, 0], Dall[cf, 0, :, :, 0]
                cb = c + 7 - j
                if cb < NCORES:
                    Sp[1, j], Dp[1, :, :, j] = Sall[cb, 1], Dall[cb, 1, :, :, 0]
            maps.append(dict(xT=xs[c], wst=wB, memT=memT,
                             kT_all=np.ascontiguousarray(kTp[:, :, c * T:c * T + 3 * T]),
                             va_al=np.ascontiguousarray(vap[c * T:c * T + 3 * T]),
                             va_sh=np.ascontiguousarray(vap[c * T + T - 64:c * T + T - 64 + 9 * 128]),
                             Sprev=Sp, Dprev=Dp, biasT=biasT, **common))
        resB = run_bass_kernel_spmd(pB.nc, maps, core_ids=cores).results
        xs = [np.asarray(r["yT"]) for r in resB]
    out = np.empty((1, SEQ, D), np.float32)
    for c in cores:
        out[0, c * T:(c + 1) * T, :] = xs[c].transpose(1, 0, 2).reshape(D, T).T
    return out
```

```python
import contextlib
import math
import numpy as np
import ml_dtypes
import concourse.bass as bass
import concourse.mybir as mybir
from concourse.bass_utils import run_bass_kernel_spmd

F32 = mybir.dt.float32
BF16 = mybir.dt.bfloat16
AF = mybir.ActivationFunctionType
ALU = mybir.AluOpType

NCORES = 8
D = 2048
SEQ = 8192
T = SEQ // NCORES
NT = T // 128
C = D // 128
DEPTH = 4
DFF = 4 * D
EPS = 1e-6
MEM_LEN = 256
IN_W = 5152
O_GQ, O_GK, O_GV, O_GR, O_LRF, O_LRB, O_DQ, O_DK, O_DV, O_MQ = 0, 512, 1024, 2048, 3072, 3088, 3104, 3616, 4128, 4640
NKT = 25
VA = 132
NSLOT = 4
WCH = 53
SW = 264
NG = 64 * DEPTH + 16
BLK = 4096
ENGS = ("pe", "act", "dve", "pool", "sp")


class Tile:
    __slots__ = ("name", "w", "r")

    def __init__(self, name):
        self.name = name
        self.w = None
        self.r = []


class Rec:
    def __init__(self):
        self.calls = []

    def __getattr__(self, name):
        def f(*a, **k):
            self.calls.append((name, a, k))
            return None
        return f


def record(fn):
    r = Rec()
    fn(r)
    assert r.calls
    return r.calls


class Sched:
    def __init__(self, nc, stack):
        self.nc = nc
        self.stack = stack
        self.streams = {e: [] for e in ENGS}
        self.cnt = {e: 0 for e in ENGS}
        self.waited = {e: {} for e in ENGS}
        self.sems = {}
        self.dma_cnt = {}
        for e in ("pe", "act", "dve", "pool"):
            self.sems[e] = stack.enter_context(nc.semaphore("s_" + e))

    def dma_sem(self, name):
        key = "dma:" + name
        if key not in self.sems:
            self.sems[key] = self.stack.enter_context(self.nc.semaphore("d_" + name))
            self.dma_cnt[key] = 0
        return key

    def _deps(self, eng, reads, writes):
        deps = {}

        def add(tok):
            if tok is None:
                return
            k, c = tok
            if deps.get(k, 0) < c:
                deps[k] = c
        for t in reads:
            add(t.w)
        for t in writes:
            add(t.w)
            for r in t.r:
                add(r)
        waits = []
        for k, c in deps.items():
            if k == eng and eng not in ("act", "dve"):
                continue
            if self.waited[eng].get(k, 0) >= c:
                continue
            self.waited[eng][k] = c
            waits.append((k, c))
        return waits

    def _commit(self, tok, reads, writes):
        for t in reads:
            t.r.append(tok)
        for t in writes:
            t.w = tok
            t.r = []

    def op(self, eng, fn, reads=(), writes=()):
        waits = self._deps(eng, reads, writes)
        self.cnt[eng] += 1
        tok = (eng, self.cnt[eng])
        self.streams[eng].append((waits, record(fn), eng, 1))
        self._commit(tok, reads, writes)
        return tok

    def dma(self, eng, semname, fn, reads=(), writes=(), inc=16):
        key = self.dma_sem(semname)
        waits = self._deps(eng, reads, writes)
        prev = self.dma_cnt[key]
        if prev > 0 and self.waited[eng].get(key, 0) < prev:
            self.waited[eng][key] = prev
            waits.append((key, prev))
        self.dma_cnt[key] += inc
        tok = (key, self.dma_cnt[key])
        self.streams[eng].append((waits, record(fn), key, inc))
        self._commit(tok, reads, writes)
        return tok

    def wait_all_dma(self, eng):
        waits = [(k, c) for k, c in self.dma_cnt.items() if c > 0]
        self.streams[eng].append((waits, None, None, 0))

    def emit(self):
        nc = self.nc
        with nc.Block() as block:
            def run(e, name):
                for waits, fn, key, inc in self.streams[name]:
                    for k, c in waits:
                        e.wait_ge(self.sems[k], c)
                    if fn is not None:
                        ins = None
                        for (nm, a, k) in fn:
                            ins = getattr(e, nm)(*a, **k)
                        ins.then_inc(self.sems[key], inc)

            @block.tensor
            def _(e):
                run(e, "pe")

            @block.scalar
            def _(e):
                run(e, "act")

            @block.vector
            def _(e):
                run(e, "dve")

            @block.gpsimd
            def _(e):
                run(e, "pool")

            @block.sync
            def _(e):
                run(e, "sp")


PW = 256


class Buf:
    def __init__(self, pool, off, n, nbytes):
        self.pool, self.off, self.n, self.nb = pool, off, n, nbytes
        self.tiles = pool.tiles[off // PW:(off + n + PW - 1) // PW]

    def f32(self):
        return self.pool.t[:, self.off:self.off + self.nb // 4]

    def bf(self):
        return self.pool.t[:, self.off:self.off + self.n].bitcast(BF16)[:, 0:self.nb // 2]


class PagePool:
    def __init__(self, t, nwords):
        self.t = t
        self.np = nwords // PW
        self.tiles = [Tile(f"pg{i}") for i in range(self.np)]
        self.used = [False] * self.np
        self.peak = 0
        self.cur = 0

    def alloc(self, nbytes, top=False):
        k = (nbytes + 4 * PW - 1) // (4 * PW)
        SM = 20
        n = self.np

        def scan(idx_iter):
            run = 0
            prev = None
            for i in idx_iter:
                if prev is not None and abs(i - prev) != 1:
                    run = 0
                run = run + 1 if not self.used[i] else 0
                prev = i
                if run == k:
                    return min(i, i - (k - 1) * (1 if prev is None else 1)) if False else i
            return None
        cands = []
        if top:
            cands = [list(range(n - 1, SM - 1, -1)), list(range(n - 1, -1, -1))]
        elif k <= 2:
            cands = [list(range(self.cur, SM)), list(range(0, SM)), list(range(SM, n))]
        else:
            cands = [list(range(SM, n)), list(range(0, n))]
        for idxs in cands:
            run = 0
            for pos, i in enumerate(idxs):
                run = run + 1 if not self.used[i] else 0
                if run == k:
                    pages = idxs[pos - k + 1:pos + 1]
                    s0 = min(pages)
                    for j in pages:
                        self.used[j] = True
                    self.peak = max(self.peak, sum(self.used))
                    if k <= 2 and not top:
                        self.cur = (s0 + k) % SM
                    return Buf(self, s0 * PW, k * PW, nbytes)
        raise RuntimeError(f"pool OOM: need {k} pages, used {sum(self.used)}/{self.np}: " + "".join("X" if u else "." for u in self.used))

    def free(self, *bufs):
        for b in bufs:
            for j in range(b.off // PW, (b.off + b.n) // PW):
                self.used[j] = False


def mmgroup(items):
    def fn(e):
        ins = None
        for (o, l, r, st, sp) in items:
            ins = e.matmul(o, lhsT=l, rhs=r, start=st, stop=sp)
        return ins
    return fn


def blocks_A():
    b = []
    b += [("w_in", 0, 16, O_DK + 256 * j, 256) for j in range(2)]
    b += [("w_in", 0, 16, O_DV + 256 * j, 256) for j in range(2)]
    b += [("w_in", 0, 16, O_LRF, 32)]
    for h in range(4):
        b += [("gqk", h)]
        b += [("w_in", 0, 16, O_GV + 256 * h, 256)]
    return b


def blocks_B():
    b = []
    b += [("w_mem_kv", 0, 16, 256 * j, 256) for j in range(4)]
    b += [("w_in", 0, 16, O_MQ + 256 * j, 256) for j in range(2)]
    b += [("w_in", 0, 16, O_DQ + 256 * j, 256) for j in range(2)]
    b += [("w_out", 8, 8, 512 * j, 512) for j in range(4)]
    b += [("w_in", 0, 16, O_LRF, 32)]
    for h in range(4):
        b += [("gqk", h)]
        b += [("w_in", 0, 16, O_GV + 256 * h, 256)]
        b += [("w_in", 0, 16, O_GR + 256 * h, 256)]
    b += [("w_out", 0, 8, 512 * j, 512) for j in range(4)]
    for fq in range(4):
        b += [("w_up", 0, 16, fq * 2048 + 256 * j, 256) for j in range(8)]
        b += [("w_down", fq * 16, 16, 256 * j, 256) for j in range(8)]
    return b


def blocks_layer():
    return blocks_A() + blocks_B()


def host_blocks(blist, w):
    out = np.zeros((len(blist), 128, BLK), np.float32)
    for i, bd in enumerate(blist):
        if bd[0] == "gqk":
            h = bd[1]
            m = w["w_in"]
            a = np.concatenate([m[:, O_GQ + 128 * h:O_GQ + 128 * h + 128],
                                m[:, O_GK + 128 * h:O_GK + 128 * h + 128]], axis=1)
            blk = a.reshape(16, 128, 256).transpose(1, 0, 2).reshape(128, 16 * 256)
        else:
            name, kc0, nkc, c0, ncol = bd
            m = w[name][kc0 * 128:(kc0 + nkc) * 128, c0:c0 + ncol]
            blk = m.reshape(nkc, 128, ncol).transpose(1, 0, 2).reshape(128, nkc * ncol)
        out[i, :, :blk.shape[1]] = blk
    return out


def t5_bucket_np(rel):
    half, max_exact = 16, 8
    ret = np.where(rel > 0, half, 0)
    n = np.abs(rel)
    nf = np.maximum(n, 1).astype(np.float32)
    large = max_exact + (np.log(nf / np.float32(max_exact)) / np.float32(math.log(1024 / max_exact))
                         * np.float32(half - max_exact)).astype(np.int32)
    large = np.minimum(large, half - 1)
    return ret + np.where(n < max_exact, n, large)


def key_tile_offsets():
    offs = [(1, -128), (1, 0), (1, 128)]
    offs += [(4, -256 + 128 * j) for j in range(5)]
    offs += [(16, -1024 + 128 * j) for j in range(17)]
    return offs


def host_bias_tiles(rel_bias):
    out = np.full((4, 128, NKT, 128), -30000.0, np.float32)
    k = np.arange(128)[:, None]
    q = np.arange(128)[None, :]
    for j, (d, off) in enumerate(key_tile_offsets()):
        rel = off + k - q
        valid = (rel % d == 0) & (np.abs(rel) <= 64 * d)
        bk = t5_bucket_np(rel)
        for h in range(4):
            vals = rel_bias[bk, h]
            out[h, :, j, :] = np.where(valid, vals, np.float32(-30000.0))
    return out


def host_consts():
    s = np.arange(128)[:, None]
    t = np.arange(128)[None, :]
    v = np.float32(-1.0 / 16.0)
    cm = np.zeros((4, 128, 128), np.float32)
    cm[0] = np.where(s <= t, v, 0)
    cm[1] = np.where(s >= t, v, 0)
    cm[2] = np.where(s > t, v, 0)
    cm[3] = np.where(s < t, v, 0)
    mask2 = np.concatenate([(s <= t), (s >= t)], axis=1).astype(np.float32)
    ident = np.eye(128, dtype=np.float32)
    ones = np.ones((128, 128), np.float32)
    return cm, mask2, ident, ones


def feat_cols(vec):
    return np.ascontiguousarray(vec.reshape(-1, 128).T)


class Prog:
    def __init__(self, depth=DEPTH):
        self.depth = depth
        self.nc = bass.Bass("TRN2", target_bir_lowering=False)

    def build(self):
        nc = self.nc
        depth = self.depth
        with contextlib.ExitStack() as st:
            st.enter_context(nc.allow_low_precision("bf16 matmul operands, fp32 accumulate"))
            self.st = st
            self.S = S = Sched(nc, st)
            din = lambda name, shape, dt=F32: nc.dram_tensor(name, shape, dt, kind="ExternalInput").ap()
            dout = lambda name, shape, dt=F32: nc.dram_tensor(name, shape, dt, kind="ExternalOutput").ap()
            sb = lambda name, shape, dt: st.enter_context(nc.sbuf_tensor(name, shape, dt))

            import os
            self.dbg = os.environ.get("KDBG", "")
            self.blist = (blocks_layer() * depth)[:int(os.environ.get("KNBLK", "100000"))]
            self.d_x = din("xT", [128, C, T])
            nb = len(self.blist)
            self.d_w = [din(f"wst{i}", [min(WCH, nb - i * WCH), 128, BLK]) for i in range((nb + WCH - 1) // WCH)]
            self.d_gains = din("gains", [128, NG])
            self.d_cm = din("cm", [4, 128, 128])
            self.d_mask2 = din("mask2", [128, 256])
            self.d_ident = din("ident", [128, 128])
            self.d_ones = din("ones", [128, 128])
            self.d_upaug = din("upaug", [DEPTH, 2, 17, 512])
            self.d_memT = din("memT", [128, C, MEM_LEN])
            self.d_bias = din("biasT", [4, 128, NKT * 128])
            self.d_selm = din("selm", [128, 48])
            self.o_y = dout("yT", [128, C, T])
            idram = lambda name, shape, dt: nc.dram_tensor(name, shape, dt)
            self.payK = [idram(f"payK{l}", [512, T], BF16) for l in range(depth)]
            self.gatK = [idram(f"gatK{l}", [NCORES * 512, T], BF16) for l in range(depth)]
            self.payV = [idram(f"payV{l}", [T, 4 * VA], BF16) for l in range(depth)]
            self.gatV = [idram(f"gatV{l}", [NCORES * T, 4 * VA], BF16) for l in range(depth)]
            self.payS = [idram(f"payS{l}", [1024, SW], F32) for l in range(depth)]
            self.gatS = [idram(f"gatS{l}", [NCORES * 1024, SW], F32) for l in range(depth)]
            self.t_pay = {(k, l): Tile(f"pay{k}{l}") for k in "KVS" for l in range(depth)}
            self.t_gat = {(k, l): Tile(f"gat{k}{l}") for k in "KVS" for l in range(depth)}

            self.xT = sb("xT_sb", [128, C, T], F32)
            self.t_x = [Tile(f"x{c}") for c in range(C)]
            self.hT = sb("hT_sb", [128, C, T], BF16)
            self.t_h = [Tile(f"h{c}") for c in range(C)]
            self.wsl = sb("wslots", [128, NSLOT, BLK], BF16)
            self.t_ws = [Tile(f"ws{i}") for i in range(NSLOT)]
            self.gains = [sb(f"gains_sb{i}", [128, 64], F32) for i in range(DEPTH + 1)]
            self.selm = sb("selm_sb", [128, 48], F32)
            self.IL = sb("IL_sb", [128, NCORES, 128], BF16)
            self.IR = sb("IR_sb", [128, NCORES, 128], BF16)
            self.t_sel = Tile("sel")
            self.t_gains = Tile("gains")
            self.cm = sb("cm_sb", [128, 4, 128], F32)
            self.mask2 = sb("mask2_sb", [128, 256], F32)
            self.ident = sb("ident_sb", [128, 128], BF16)
            self.ones = sb("ones_sb", [128, 128], BF16)
            self.t_const = Tile("const")
            self.t_constb = Tile("constb")
            npw = (nc.sbuf_bytes_remaining - 2048) // 4
            npw = (npw // PW) * PW
            self.pool = PagePool(sb("pool_sb", [128, npw], F32), npw)
            self.ps = [st.enter_context(nc.psum_tensor(f"ps{i}", [128, 512], F32)) for i in range(8)]
            self.t_ps = [Tile(f"ps{i}") for i in range(8)]
            self.ps_i = 0
            self.w_load = 0
            self.w_use = 0

            for i in range(DEPTH + 1):
                w_ = 64 if i < DEPTH else 16
                S.dma("sp", "c0", lambda e, i=i, w_=w_: e.dma_start(out=self.gains[i][:, 0:w_], in_=self.d_gains[:, 64 * i:64 * i + w_]), writes=[self.t_gains])
            S.dma("sp", "c0", lambda e: e.dma_start(out=self.selm[:], in_=self.d_selm[:, :]), writes=[self.t_sel])
            S.dma("sp", "c1", lambda e: e.dma_start(out=self.cm[:], in_=self.d_cm.rearrange("a p n -> p a n")), writes=[self.t_const])
            S.dma("sp", "c1", lambda e: e.dma_start(out=self.mask2[:], in_=self.d_mask2[:, :]), writes=[self.t_const])
            S.dma("pool", "c2", lambda e: e.dma_start(out=self.ident[:], in_=self.d_ident[:, :]), writes=[self.t_constb])
            S.dma("pool", "c2", lambda e: e.dma_start(out=self.ones[:], in_=self.d_ones[:, :]), writes=[self.t_constb])
            for c in range(C):
                S.dma("sp", f"x{c % 4}", (lambda c: lambda e: e.dma_start(out=self.xT[:, c, :], in_=self.d_x[:, c, :]))(c),
                      writes=[self.t_x[c]])

            self.program_F()
            S.emit()
        return nc

    def bank(self):
        i = self.ps_i
        self.ps_i = (i + 1) % 8
        return self.ps[i], self.t_ps[i]

    def wnext(self):
        i = self.w_use
        self.w_use += 1
        nblk = len(self.blist)
        while self.w_load < min(nblk, i + NSLOT - 1):
            j = self.w_load
            s = j % NSLOT
            self.S.dma("pool", f"w{s}", (lambda j, s: lambda e: e.dma_start(out=self.wsl[:, s, :], in_=self.d_w[j // WCH][j % WCH]))(j, s),
                       writes=[self.t_ws[s]])
            self.w_load += 1
        s = i % NSLOT
        return self.wsl[:, s, :], self.t_ws[s]

    def oname(self):
        self._on = getattr(self, "_on", 0) + 1
        return f"oA{self._on % 4}"

    def gcol(self, idx):
        return self.gains[idx // 64][:, idx % 64:idx % 64 + 1]

    def rstd_from(self, out_ap, in_ap, scale, reads, writes):
        S = self.S
        S.op("act", lambda e: e.activation(out=out_ap, in_=in_ap, func=AF.Ln, scale=scale, bias=EPS_AP(self)), reads=reads + [self.t_eps], writes=writes)
        S.op("act", lambda e: e.activation(out=out_ap, in_=out_ap, func=AF.Exp, scale=-0.5), reads=writes, writes=writes)

    def norm_T(self, src, t_src, ntok, gbase, dst, t_dst, out_f32_dram=None):
        S, pool = self.S, self.pool
        nh = (ntok + 511) // 512
        for th in range(nh):
            n = min(512, ntok - th * 512)
            sl = slice(th * 512, th * 512 + n)
            acc, t_acc = self.bank()
            sq = [pool.alloc(n * 2) for _ in range(2)]
            for c in range(C):
                b = sq[c % 2]
                S.op("act", (lambda c, b: lambda e: e.activation(out=b.bf()[:, 0:n], in_=src[:, c, sl], func=AF.Square))(c, b),
                     reads=t_src[c], writes=b.tiles)
                S.op("pe", (lambda c, b: lambda e: e.matmul(acc[:, 0:n], lhsT=self.ones[:], rhs=b.bf()[:, 0:n], start=(c == 0), stop=(c == C - 1)))(c, b),
                     reads=b.tiles + [self.t_constb], writes=[t_acc])
            rs = pool.alloc(n * 4)
            self.rstd_from(rs.f32()[:, 0:n], acc[:, 0:n], 1.0 / D, [t_acc], rs.tiles)
            for c in range(C):
                if out_f32_dram is None:
                    S.op("dve", (lambda c: lambda e: e.scalar_tensor_tensor(out=dst[:, c, sl], in0=src[:, c, sl], scalar=self.gcol(gbase + c),
                                                                           in1=rs.f32()[:, 0:n], op0=ALU.mult, op1=ALU.mult))(c),
                         reads=t_src[c] + [self.t_gains] + rs.tiles, writes=t_dst[c])
                else:
                    ob = pool.alloc(n * 4)
                    S.op("dve", (lambda c, ob: lambda e: e.scalar_tensor_tensor(out=ob.f32()[:, 0:n], in0=src[:, c, sl], scalar=self.gcol(gbase + c),
                                                                               in1=rs.f32()[:, 0:n], op0=ALU.mult, op1=ALU.mult))(c, ob),
                         reads=t_src[c] + [self.t_gains] + rs.tiles, writes=ob.tiles)
                    S.dma("sp", f"yout{c % 4}", (lambda c, ob: lambda e: e.dma_start(out=out_f32_dram[:, c, sl], in_=ob.f32()[:, 0:n]))(c, ob),
                          reads=ob.tiles)
                    pool.free(ob)
            pool.free(rs, *sq)

    def proj_feat(self, wap, wt, kcols, col0, M, rhsT, t_rhs, ntok, evac):
        S = self.S
        w3 = wap[:, 0:16 * kcols].rearrange("p (k n) -> p k n", n=kcols)
        nh = (ntok + 511) // 512
        for th in range(nh):
            n = min(512, ntok - th * 512)
            bk, tb = self.bank()
            items = [(bk[0:M, 0:n], w3[:, kc, col0:col0 + M], rhsT[:, kc, th * 512:th * 512 + n], kc == 0, kc == C - 1) for kc in range(C)]
            S.op("pe", mmgroup(items), reads=[wt] + t_rhs, writes=[tb])
            evac(th, n, bk, tb)

    def proj_tok(self, wap, wt, kcols, col0, N, lhsT, t_lhs, tt, evac):
        S = self.S
        w3 = wap[:, 0:16 * kcols].rearrange("p (k n) -> p k n", n=kcols)
        bk, tb = self.bank()
        items = [(bk[:, 0:N], lhsT[:, kc, tt * 128:(tt + 1) * 128], w3[:, kc, col0:col0 + N], kc == 0, kc == C - 1) for kc in range(C)]
        S.op("pe", mmgroup(items), reads=[wt] + t_lhs, writes=[tb])
        evac(bk, tb)

    def setup_eps(self):
        S = self.S
        self.epsb = self.pool.alloc(1024, top=True)
        self.t_eps = Tile("eps")
        S.op("dve", lambda e: e.memset(self.epsb.f32()[:, 0:1], EPS), writes=[self.t_eps])

    def load_upaug(self, l):
        S, pool = self.S, self.pool
        self.upaug = pool.alloc(2 * 512 * 2)
        ap = self.upaug.bf().rearrange("p (a n) -> p a n", n=512)
        S.dma("pool", "upaug", lambda e: e.dma_start(out=ap[0:17, :, :], in_=self.d_upaug[l].rearrange("a r n -> r a n")), writes=self.upaug.tiles)
        self.upaug_ap = ap

    def compute_lr(self):
        S, pool = self.S, self.pool
        wap, wt = self.wnext()
        self.lrT = pool.alloc(2 * T * 2)
        lr = self.lrT.bf().rearrange("p (a n) -> p a n", n=T)
        S.op("dve", lambda e: e.memset(lr[0:17, :, :], 1.0), writes=self.lrT.tiles)
        for d in range(2):
            def evac(th, n, bk, tb, d=d):
                S.op("act", lambda e: e.activation(out=lr[0:16, d, th * 512:th * 512 + n], in_=bk[0:16, 0:n], func=AF.Copy),
                     reads=[tb], writes=self.lrT.tiles)
            self.proj_feat(wap, wt, 32, 16 * d, 16, self.hT, self.t_h, T, evac)
        self.lr_ap = lr

    def gla_gates(self, h, d):
        S, pool = self.S, self.pool
        g = pool.alloc(NT * 128 * 4)
        g3 = g.f32().rearrange("p (a n) -> p a n", n=128)
        for half in range(2):
            bk, tb = self.bank()
            items = []
            for i in range(4):
                tt = half * 4 + i
                items.append((bk[:, i * 128:(i + 1) * 128], self.lr_ap[0:17, d, tt * 128:(tt + 1) * 128],
                              self.upaug_ap[0:17, d, h * 128:(h + 1) * 128], True, True))
            S.op("pe", mmgroup(items), reads=self.lrT.tiles + self.upaug.tiles, writes=[tb])
            dst = g.f32()[:, half * 512:(half + 1) * 512]
            S.op("act", lambda e, bk=bk, dst=dst: e.activation(out=dst, in_=bk[:, :], func=AF.Exp, scale=-1.0), reads=[tb], writes=g.tiles)
            S.op("act", lambda e, dst=dst: e.activation(out=dst, in_=dst, func=AF.Ln, bias=self.one_ap()), reads=g.tiles + [self.t_eps], writes=g.tiles)
        return g, g3

    def one_ap(self):
        return self.epsb.f32()[:, 1:2]

    def gla_decays(self, h, d, g, g3, qT, kT, need_q):
        S, pool = self.S, self.pool
        res = {}
        bb = [self.bank(), self.bank()]
        for half in range(2):
            bk, tb = bb[half]
            items = []
            for i in range(4):
                tt = half * 4 + i
                items.append((bk[:, i * 128:(i + 1) * 128], g3[:, tt, :], self.cm[:, d, :], True, True))
            S.op("pe", mmgroup(items), reads=g.tiles + [self.t_const], writes=[tb])
        tmp = pool.alloc(T * 4)
        for half in range(2):
            bk, tb = bb[half]
            S.op("act", lambda e, bk=bk, half=half: e.activation(out=tmp.f32()[:, half * 512:(half + 1) * 512], in_=bk[:, :], func=AF.Exp),
                 reads=[tb], writes=tmp.tiles)
        dec = pool.alloc(1024)
        col = 127 if d == 0 else 0
        S.op("dve", lambda e: e.tensor_copy(out=dec.f32()[:, 0:NT], in_=tmp.f32()[:, col:T:128]), reads=tmp.tiles, writes=dec.tiles)
        res["dec"] = dec
        if need_q:
            qd = pool.alloc(T * 2)
            S.op("dve", lambda e: e.tensor_tensor(out=qd.bf()[:, 0:T], in0=qT.f32()[:, 0:T], in1=tmp.f32()[:, 0:T], op=ALU.mult),
                 reads=qT.tiles + tmp.tiles, writes=qd.tiles)
            res["q_dec"] = qd
            tmp2 = pool.alloc(T * 4)
            for half in range(2):
                bk, tb = bb[half]
                S.op("act", lambda e, bk=bk, half=half: e.activation(out=tmp2.f32()[:, half * 512:(half + 1) * 512], in_=bk[:, :], func=AF.Exp, scale=-1.0),
                     reads=[tb], writes=tmp2.tiles)
            ki = pool.alloc(T * 2)
            S.op("dve", lambda e: e.tensor_tensor(out=ki.bf()[:, 0:T], in0=kT.f32()[:, 0:T], in1=tmp2.f32()[:, 0:T], op=ALU.mult),
                 reads=kT.tiles + tmp2.tiles, writes=ki.tiles)
            res["k_inv"] = ki
            pool.free(tmp2)
        eb = [self.bank(), self.bank()]
        for half in range(2):
            bk, tb = eb[half]
            items = []
            for i in range(4):
                tt = half * 4 + i
                items.append((bk[:, i * 128:(i + 1) * 128], g3[:, tt, :], self.cm[:, 2 + d, :], True, True))
            S.op("pe", mmgroup(items), reads=g.tiles + [self.t_const], writes=[tb])
        for half in range(2):
            bk, tb = eb[half]
            S.op("act", lambda e, bk=bk, half=half: e.activation(out=tmp.f32()[:, half * 512:(half + 1) * 512], in_=bk[:, :], func=AF.Exp),
                 reads=[tb], writes=tmp.tiles)
        keT = pool.alloc(T * 2)
        S.op("dve", lambda e: e.tensor_tensor(out=keT.bf()[:, 0:T], in0=kT.f32()[:, 0:T], in1=tmp.f32()[:, 0:T], op=ALU.mult),
             reads=kT.tiles + tmp.tiles, writes=keT.tiles)
        bk, tb = self.bank()
        bkb = bk[:].bitcast(BF16)
        def trfn(e):
            ins = None
            for tt in range(NT):
                ins = e.transpose(bkb[:, tt * 128:(tt + 1) * 128], keT.bf()[:, tt * 128:(tt + 1) * 128], self.ident[:])
            return ins
        S.op("pe", trfn, reads=keT.tiles + [self.t_constb], writes=[tb])
        ke = pool.alloc(T * 2)
        S.op("act", lambda e: e.activation(out=ke.bf()[:, 0:T], in_=bkb[:, :], func=AF.Copy), reads=[tb], writes=ke.tiles)
        res["k_end"] = ke
        pool.free(tmp, keT)
        return res

    def gla_scan(self, d, ke, vh, dec, st, snaps):
        S, pool = self.S, self.pool
        ke3 = ke.bf().rearrange("p (a n) -> p a n", n=128)
        v3 = vh.bf().rearrange("p (a n) -> p a n", n=256)
        order = list(range(NT)) if d == 0 else list(range(NT - 1, -1, -1))
        for tt in order:
            if snaps is not None:
                sn3 = snaps.bf().rearrange("p (a n) -> p a n", n=256)
                S.op("act", lambda e, tt=tt, sn3=sn3: e.activation(out=sn3[:, tt, :], in_=st.f32()[:, 0:256], func=AF.Copy),
                     reads=st.tiles, writes=snaps.tiles)
            if snaps is not None and tt == order[-1]:
                break
            bk, tb = self.bank()
            S.op("pe", lambda e, tt=tt, bk=bk: e.matmul(bk[:, 0:256], lhsT=ke3[:, tt, :], rhs=v3[:, tt, :], start=True, stop=True),
                 reads=ke.tiles + vh.tiles, writes=[tb])
            S.op("dve", lambda e, tt=tt, bk=bk: e.scalar_tensor_tensor(out=st.f32()[:, 0:256], in0=st.f32()[:, 0:256], scalar=dec.f32()[:, tt:tt + 1],
                                                                      in1=bk[:, 0:256], op0=ALU.mult, op1=ALU.add),
                 reads=st.tiles + dec.tiles + [tb], writes=st.tiles)

    def allgather(self, k, l, pay, gat):
        S = self.S
        S.dma("pool", f"cc{k}", lambda e: e.collective_compute("AllGather", ALU.bypass, replica_groups=[list(range(NCORES))],
                                                                   ins=[pay.ap().opt()], outs=[gat.ap().opt()]),
              reads=[self.t_pay[(k, l)]], writes=[self.t_gat[(k, l)]], inc=1)

    def layer_A(self, l):
        S, pool = self.S, self.pool
        gb = 64 * l
        self.norm_T(self.xT, [[t] for t in self.t_x], T, gb, self.hT, [[t] for t in self.t_h])
        self.kTown = pool.alloc(4 * T * 2, top=True)
        self.kTown_ap = self.kTown.bf().rearrange("p (h n) -> p h n", n=T)
        for j in range(2):
            wap, wt = self.wnext()
            for hh in range(2):
                h = 2 * j + hh
                def evac(th, n, bk, tb, h=h):
                    S.op("act", lambda e: e.activation(out=self.kTown_ap[:, h, th * 512:th * 512 + n], in_=bk[:, 0:n], func=AF.Copy), reads=[tb], writes=self.kTown.tiles)
                self.proj_feat(wap, wt, 256, 128 * hh, 128, self.hT, self.t_h, T, evac)
                S.dma("sp", "payK", lambda e, h=h: e.dma_start(out=self.payK[l].ap()[h * 128:(h + 1) * 128, :], in_=self.kTown_ap[:, h, :]),
                      reads=self.kTown.tiles, writes=[self.t_pay[("K", l)]])
        self.allgather("K", l, self.payK[l], self.gatK[l])
        self.vaown = pool.alloc(NT * 4 * VA * 2, top=True)
        self.vaown_ap = self.vaown.bf().rearrange("p (t n) -> p t n", n=4 * VA)
        va4 = self.vaown.bf().rearrange("p (t a n) -> p t a n", t=NT, a=4)
        S.op("dve", lambda e: e.memset(self.vaown.bf()[:, :], 1.0), writes=self.vaown.tiles)
        wv = [self.wnext(), self.wnext()]
        for tt in range(NT):
            for j in range(2):
                def evac(bk, tb, j=j, tt=tt):
                    S.op("act", lambda e: e.activation(out=va4[:, tt, 2 * j:2 * j + 2, 0:128], in_=bk[:, 0:256].rearrange("p (a n) -> p a n", n=128), func=AF.Copy),
                         reads=[tb], writes=self.vaown.tiles)
                self.proj_tok(wv[j][0], wv[j][1], 256, 0, 256, self.hT, self.t_h, tt, evac)
            S.dma("sp", "payV", lambda e, tt=tt: e.dma_start(out=self.payV[l].ap()[tt * 128:(tt + 1) * 128, :], in_=self.vaown_ap[:, tt, :]),
                  reads=self.vaown.tiles, writes=[self.t_pay[("V", l)]])
        self.allgather("V", l, self.payV[l], self.gatV[l])
        self.load_upaug(l)
        self.compute_lr()
        for h in range(4):
            wqk, tqk = self.wnext()
            kT = pool.alloc(T * 4)
            def evk(th, n, bk, tb, kT=kT):
                S.op("act", lambda e: e.activation(out=kT.f32()[:, th * 512:th * 512 + n], in_=bk[:, 0:n], func=AF.Copy), reads=[tb], writes=kT.tiles)
            self.proj_feat(wqk, tqk, 256, 128, 128, self.hT, self.t_h, T, evk)
            wv_, tv_ = self.wnext()
            vh = pool.alloc(NT * 256 * 2)
            v3 = vh.bf().rearrange("p (a n) -> p a n", n=256)
            for tt in range(NT):
                def evv(bk, tb, tt=tt):
                    S.op("act", lambda e: e.activation(out=v3[:, tt, :], in_=bk[:, 0:256], func=AF.Copy), reads=[tb], writes=vh.tiles)
                self.proj_tok(wv_, tv_, 256, 0, 256, self.hT, self.t_h, tt, evv)
            for d in range(2):
                g, g3 = self.gla_gates(h, d)
                r = self.gla_decays(h, d, g, g3, None, kT, False)
                pool.free(g)
                st_ = pool.alloc(SW * 4)
                S.op("dve", lambda e, st_=st_: e.memset(st_.f32()[:, 0:256], 0.0), writes=st_.tiles)
                self.gla_scan(d, r["k_end"], vh, r["dec"], st_, None)
                r0 = (d * 4 + h) * 128
                dec = r["dec"]
                S.op("dve", lambda e, dec=dec, st_=st_: e.tensor_tensor(out=st_.f32()[:, 256:257], in0=dec.f32()[:, 0:1], in1=dec.f32()[:, 1:2], op=ALU.mult),
                     reads=dec.tiles, writes=st_.tiles)
                for tt in range(2, NT):
                    S.op("dve", lambda e, dec=dec, st_=st_, tt=tt: e.tensor_tensor(out=st_.f32()[:, 256:257], in0=st_.f32()[:, 256:257], in1=dec.f32()[:, tt:tt + 1], op=ALU.mult),
                         reads=dec.tiles + st_.tiles, writes=st_.tiles)
                S.dma("sp", "payS", lambda e, r0=r0, st_=st_: e.dma_start(out=self.payS[l].ap()[r0:r0 + 128, :], in_=st_.f32()[:, 0:SW]),
                      reads=st_.tiles, writes=[self.t_pay[("S", l)]])
                pool.free(st_, r["k_end"], r["dec"])
            pool.free(kT, vh)
        pool.free(self.lrT, self.upaug)
        self.allgather("S", l, self.payS[l], self.gatS[l])

    def attn_epilogue(self, obk, tob, mixed_ap, t_mixed):
        S, pool = self.S, self.pool
        sm = pool.alloc(1024)
        on = pool.alloc(128 * 4)
        junk = pool.alloc(128 * 4)
        S.op("dve", lambda e: e.reciprocal(out=sm.f32()[:, 0:1], in_=obk[:, 128:129]), reads=[tob], writes=sm.tiles)
        S.op("dve", lambda e: e.tensor_scalar(out=on.f32()[:, 0:128], in0=obk[:, 0:128], scalar1=sm.f32()[:, 0:1], scalar2=0.0, op0=ALU.mult, op1=ALU.add),
             reads=[tob] + sm.tiles, writes=on.tiles)
        S.op("act", lambda e: e.activation(out=junk.f32()[:, 0:128], in_=on.f32()[:, 0:128], func=AF.Square, accum_out=sm.f32()[:, 1:2]),
             reads=on.tiles, writes=junk.tiles + sm.tiles)
        self.rstd_from(sm.f32()[:, 2:3], sm.f32()[:, 1:2], 1.0 / 128, sm.tiles, sm.tiles)
        S.op("act", lambda e: e.activation(out=mixed_ap, in_=on.f32()[:, 0:128], func=AF.Copy, scale=sm.f32()[:, 2:3]),
             reads=on.tiles + sm.tiles, writes=t_mixed)
        pool.free(sm, on, junk)

    def transpose_to_mixT(self, mixed, nchunk, chunk0, gbase):
        S = self.S
        w = nchunk * 128
        m3 = mixed.bf().rearrange("p (a n) -> p a n", n=w)
        for ci in range(nchunk):
            bk, tb = self.bank()
            bkb = bk[:].bitcast(BF16)
            def trfn(e, ci=ci, bkb=bkb):
                ins = None
                for tt in range(NT):
                    ins = e.transpose(bkb[:, tt * 128:(tt + 1) * 128], m3[:, tt, ci * 128:(ci + 1) * 128], self.ident[:])
                return ins
            S.op("pe", trfn, reads=mixed.tiles + [self.t_constb], writes=[tb])
            dst = self.mixT_ap[:, (chunk0 + ci) % 8, :]
            S.op("act", lambda e, dst=dst, bkb=bkb, ci=ci: e.activation(out=dst, in_=bkb[:, :], func=AF.Copy, scale=self.gcol(gbase + chunk0 + ci)),
                 reads=[tb, self.t_gains], writes=self.t_mixT[(chunk0 + ci) % 8])

    def alloc_mixT(self):
        self.mixT = self.pool.alloc(8 * T * 2, top=True)
        self.mixT_ap = self.mixT.bf().rearrange("p (c n) -> p c n", n=T)
        self.t_mixT = [self.mixT.tiles[2 * c:2 * c + 2] for c in range(8)]

    def out_proj(self, chunk0):
        S = self.S
        for j in range(4):
            wap, wt = self.wnext()
            w3 = wap[:, 0:8 * 512].rearrange("p (k n) -> p k n", n=512)
            for dd in range(4):
                dc = 4 * j + dd
                for th in range(2):
                    bk, tb = self.bank()
                    items = [(bk[:, :], w3[:, kc, dd * 128:(dd + 1) * 128], self.mixT_ap[:, kc, th * 512:(th + 1) * 512], kc == 0, kc == 7) for kc in range(8)]
                    S.op("pe", mmgroup(items), reads=[wt] + self.mixT.tiles, writes=[tb])
                    S.op("dve", lambda e, dc=dc, th=th, bk=bk: e.tensor_tensor(out=self.xT[:, dc, th * 512:(th + 1) * 512], in0=self.xT[:, dc, th * 512:(th + 1) * 512],
                                                                            in1=bk[:, :], op=ALU.add),
                         reads=[tb, self.t_x[dc]], writes=[self.t_x[dc]])

    def layer_B(self, l):
        S, pool = self.S, self.pool
        gb = 64 * l
        self.alloc_mixT()
        SC = 128 ** -0.5

        memT = pool.alloc(C * MEM_LEN * 4)
        m3 = memT.f32().rearrange("p (c n) -> p c n", n=MEM_LEN)
        t_mem = [[memT.tiles[c]] for c in range(C)]
        for c in range(C):
            S.dma("sp", f"mem{c % 2}", lambda e, c=c: e.dma_start(out=m3[:, c, :], in_=self.d_memT[:, c, :]), writes=t_mem[c])
        hm = pool.alloc(C * MEM_LEN * 2)
        hm3 = hm.bf().rearrange("p (c n) -> p c n", n=MEM_LEN)
        t_hm = [[hm.tiles[c // 2]] for c in range(C)]
        self.norm_T(m3, t_mem, MEM_LEN, gb + 32, hm3, t_hm)
        kmT = pool.alloc(4 * MEM_LEN * 2)
        km3 = kmT.bf().rearrange("p (h n) -> p h n", n=MEM_LEN)
        for j in range(2):
            wap, wt = self.wnext()
            for hh in range(2):
                h = 2 * j + hh
                def evac(th, n, bk, tb, h=h):
                    S.op("act", lambda e: e.activation(out=km3[:, h, 0:n], in_=bk[:, 0:n], func=AF.Copy), reads=[tb], writes=kmT.tiles)
                self.proj_feat(wap, wt, 256, 128 * hh, 128, hm3, hm.tiles, MEM_LEN, evac)
        vma = pool.alloc(2 * 4 * VA * 2)
        vm4 = vma.bf().rearrange("p (m h n) -> p m h n", m=2, h=4)
        S.op("dve", lambda e: e.memset(vm4[:, :, :, 128:VA], 1.0), writes=vma.tiles)
        for j in range(2):
            wap, wt = self.wnext()
            for mt in range(2):
                def evac(bk, tb, j=j, mt=mt):
                    S.op("act", lambda e: e.activation(out=vm4[:, mt, 2 * j:2 * j + 2, 0:128], in_=bk[:, 0:256].rearrange("p (a n) -> p a n", n=128), func=AF.Copy),
                         reads=[tb], writes=vma.tiles)
                self.proj_tok(wap, wt, 256, 0, 256, hm3, hm.tiles, mt, evac)
        pool.free(memT, hm)
        pbufs = [pool.alloc(256 * 2) for _ in range(3)]
        for h in range(4):
            if h % 2 == 0:
                wap, wt = self.wnext()
            qT = pool.alloc(T * 2)
            def evq(th, n, bk, tb, qT=qT):
                S.op("act", lambda e: e.activation(out=qT.bf()[:, th * 512:th * 512 + n], in_=bk[:, 0:n], func=AF.Copy, scale=SC), reads=[tb], writes=qT.tiles)
            self.proj_feat(wap, wt, 256, 128 * (h % 2), 128, self.hT, self.t_h, T, evq)
            mixed = pool.alloc(NT * 128 * 2)
            mx3 = mixed.bf().rearrange("p (a n) -> p a n", n=128)
            for qt in range(NT):
                bk, tb = self.bank()
                items = [(bk[:, mt * 128:(mt + 1) * 128], km3[:, h, mt * 128:(mt + 1) * 128], qT.bf()[:, qt * 128:(qt + 1) * 128], True, True) for mt in range(2)]
                S.op("pe", mmgroup(items), reads=kmT.tiles + qT.tiles, writes=[tb])
                p = pbufs[qt % 3]
                S.op("act", lambda e, bk=bk, p=p: e.activation(out=p.bf()[:, 0:256], in_=bk[:, 0:256], func=AF.Exp), reads=[tb], writes=p.tiles)
                ob, tob = self.bank()
                items = [(ob[:, 0:129], p.bf()[:, mt * 128:(mt + 1) * 128], vm4[:, mt, h, 0:129], mt == 0, mt == 1) for mt in range(2)]
                S.op("pe", mmgroup(items), reads=p.tiles + vma.tiles, writes=[tob])
                self.attn_epilogue(ob, tob, mx3[:, qt, :], mixed.tiles)
            self.transpose_to_mixT(mixed, 1, 12 + h, gb + 48)
            pool.free(qT, mixed)
        pool.free(kmT, vma, *pbufs)
        if self.dbg == "MEM":
            return

        offs = key_tile_offsets()
        for h in range(4):
            if h % 2 == 0:
                wap, wt = self.wnext()
            qT = pool.alloc(T * 2)
            def evq(th, n, bk, tb, qT=qT):
                S.op("act", lambda e: e.activation(out=qT.bf()[:, th * 512:th * 512 + n], in_=bk[:, 0:n], func=AF.Copy, scale=SC), reads=[tb], writes=qT.tiles)
            self.proj_feat(wap, wt, 256, 128 * (h % 2), 128, self.hT, self.t_h, T, evq)
            kall = pool.alloc(NCORES * T * 2)
            ka3 = kall.bf().rearrange("p (r n) -> p r n", n=T)
            gk = self.gatK[l].ap().rearrange("(r h e) t -> e r h t", h=4, e=128)
            for r in range(NCORES):
                S.dma("sp", f"gk{r % 2}", lambda e, r=r, h=h: e.dma_start(out=ka3[:, r, :], in_=gk[:, r, h, :]), reads=[self.t_gat[("K", l)]], writes=kall.tiles)
            kLR = [pool.alloc(T * 2, top=True), pool.alloc(T * 2, top=True)]
            for side, selI in enumerate((self.IL, self.IR)):
                for th in range(2):
                    bk, tb = self.bank()
                    items = [(bk[:, :], selI[:, r, :], ka3[:, r, th * 512:(th + 1) * 512], r == 0, r == NCORES - 1) for r in range(NCORES)]
                    S.op("pe", mmgroup(items), reads=kall.tiles + [self.t_sel], writes=[tb])
                    S.op("act", lambda e, bk=bk, side=side, th=th: e.activation(out=kLR[side].bf()[:, th * 512:(th + 1) * 512], in_=bk[:, :], func=AF.Copy),
                         reads=[tb], writes=kLR[side].tiles)
            pool.free(kall)
            vall = pool.alloc(NCORES * NT * VA * 2)
            va4 = vall.bf().rearrange("p (r a n) -> p r a n", r=NCORES, a=NT)
            gv = self.gatV[l].ap()
            for r in range(NCORES):
                S.dma("sp", f"gv{r % 2}", lambda e, r=r, h=h: e.dma_start(out=va4[:, r, :, :], in_=gv[r * T:(r + 1) * T, h * VA:(h + 1) * VA].rearrange("(a p) n -> p a n", p=128)),
                      reads=[self.t_gat[("V", l)]], writes=vall.tiles)
            vLR = [pool.alloc(NT * VA * 2, top=True), pool.alloc(NT * VA * 2, top=True)]
            for side, selI in enumerate((self.IL, self.IR)):
                for g0 in (0, 3, 6):
                    tts = list(range(g0, min(NT, g0 + 3)))
                    bk, tb = self.bank()
                    items = []
                    for i, tt in enumerate(tts):
                        for r in range(NCORES):
                            items.append((bk[:, i * VA:(i + 1) * VA], selI[:, r, :], va4[:, r, tt, :], r == 0, r == NCORES - 1))
                    S.op("pe", mmgroup(items), reads=vall.tiles + [self.t_sel], writes=[tb])
                    w_ = len(tts) * VA
                    S.op("act", lambda e, bk=bk, side=side, g0=g0, w_=w_: e.activation(out=vLR[side].bf()[:, g0 * VA:g0 * VA + w_], in_=bk[:, 0:w_], func=AF.Copy),
                         reads=[tb], writes=vLR[side].tiles)
            pool.free(vall)
            kparts = [kLR[0].bf(), self.kTown_ap[:, h, :], kLR[1].bf()]
            vparts = [vLR[0].bf().rearrange("p (a n) -> p a n", n=VA), self.vaown_ap[:, :, h * VA:(h + 1) * VA], vLR[1].bf().rearrange("p (a n) -> p a n", n=VA)]
            kv_tiles = kLR[0].tiles + kLR[1].tiles + self.kTown.tiles + vLR[0].tiles + vLR[1].tiles + self.vaown.tiles
            em = pool.alloc(NKT * 128 * 2)
            for piece in range(4):
                bt_ = pool.alloc(800 * 4)
                S.dma("sp", "bias", lambda e, h=h, piece=piece, bt_=bt_: e.dma_start(out=bt_.f32()[:, 0:800], in_=self.d_bias[h, :, piece * 800:(piece + 1) * 800]), writes=bt_.tiles)
                S.op("act", lambda e, piece=piece, bt_=bt_, em=em: e.activation(out=em.bf()[:, piece * 800:(piece + 1) * 800], in_=bt_.f32()[:, 0:800], func=AF.Exp),
                     reads=bt_.tiles, writes=em.tiles)
                pool.free(bt_)
            mixed = pool.alloc(NT * 128 * 2)
            mx3 = mixed.bf().rearrange("p (a n) -> p a n", n=128)
            pes = [pool.alloc(512 * 2) for _ in range(3)]
            pms = [pool.alloc(512 * 2) for _ in range(3)]
            for qt in range(NT):
                q0 = T + qt * 128
                ob, tob = self.bank()
                for grp in range((NKT + 3) // 4):
                    js = list(range(grp * 4, min(NKT, grp * 4 + 4)))
                    w_ = len(js) * 128
                    bk, tb = self.bank()
                    items = []
                    for i, j in enumerate(js):
                        ks = q0 + offs[j][1]
                        part, loc = ks // T, ks % T
                        items.append((bk[:, i * 128:(i + 1) * 128], kparts[part][:, loc:loc + 128], qT.bf()[:, qt * 128:(qt + 1) * 128], True, True))
                    S.op("pe", mmgroup(items), reads=kv_tiles + qT.tiles, writes=[tb])
                    pe_ = pes[(qt * 7 + grp) % 3]
                    S.op("act", lambda e, bk=bk, pe_=pe_, w_=w_: e.activation(out=pe_.bf()[:, 0:w_], in_=bk[:, 0:w_], func=AF.Exp), reads=[tb], writes=pe_.tiles)
                    pm = pms[(qt * 7 + grp) % 3]
                    S.op("dve", lambda e, pe_=pe_, pm=pm, grp=grp, em=em, w_=w_: e.tensor_tensor(out=pm.bf()[:, 0:w_], in0=pe_.bf()[:, 0:w_], in1=em.bf()[:, grp * 512:grp * 512 + w_], op=ALU.mult),
                         reads=pe_.tiles + em.tiles, writes=pm.tiles)
                    items = []
                    for i, j in enumerate(js):
                        ks = q0 + offs[j][1]
                        part, loc = ks // T, ks % T
                        items.append((ob[:, 0:129], pm.bf()[:, i * 128:(i + 1) * 128], vparts[part][:, loc // 128, 0:129], j == 0, j == NKT - 1))
                    S.op("pe", mmgroup(items), reads=pm.tiles + kv_tiles, writes=[tob])
                self.attn_epilogue(ob, tob, mx3[:, qt, :], mixed.tiles)
            self.transpose_to_mixT(mixed, 1, 8 + h, gb + 48)
            pool.free(qT, em, mixed, *pes, *pms, *kLR, *vLR)
        pool.free(self.kTown, self.vaown)
        if self.dbg == "DIL":
            return

        self.out_proj(8)
        pool.free(self.mixT)
        if self.dbg == "OP8":
            return

        self.alloc_mixT()
        self.load_upaug(l)
        self.compute_lr()
        QS = 128 ** -0.5
        for h in range(4):
            wqk, tqk = self.wnext()
            qT = pool.alloc(T * 4)
            kT = pool.alloc(T * 4)
            def evq(th, n, bk, tb, qT=qT):
                S.op("act", lambda e: e.activation(out=qT.f32()[:, th * 512:th * 512 + n], in_=bk[:, 0:n], func=AF.Copy, scale=QS), reads=[tb], writes=qT.tiles)
            def evk(th, n, bk, tb, kT=kT):
                S.op("act", lambda e: e.activation(out=kT.f32()[:, th * 512:th * 512 + n], in_=bk[:, 0:n], func=AF.Copy), reads=[tb], writes=kT.tiles)
            self.proj_feat(wqk, tqk, 256, 0, 128, self.hT, self.t_h, T, evq)
            self.proj_feat(wqk, tqk, 256, 128, 128, self.hT, self.t_h, T, evk)
            R = []
            for d in range(2):
                g, g3 = self.gla_gates(h, d)
                R.append(self.gla_decays(h, d, g, g3, qT, kT, True))
                pool.free(g)
            pool.free(qT, kT)
            wv_, tv_ = self.wnext()
            vh = pool.alloc(NT * 256 * 2)
            v3 = vh.bf().rearrange("p (a n) -> p a n", n=256)
            for tt in range(NT):
                def evv(bk, tb, tt=tt):
                    S.op("act", lambda e: e.activation(out=v3[:, tt, :], in_=bk[:, 0:256], func=AF.Copy), reads=[tb], writes=vh.tiles)
                self.proj_tok(wv_, tv_, 256, 0, 256, self.hT, self.t_h, tt, evv)
            wr_, tr_ = self.wnext()
            sr = pool.alloc(NT * 256 * 2)
            sr3 = sr.bf().rearrange("p (a n) -> p a n", n=256)
            for tt in range(NT):
                def evr(bk, tb, tt=tt):
                    S.op("act", lambda e: e.activation(out=sr3[:, tt, :], in_=bk[:, 0:256], func=AF.Silu), reads=[tb], writes=sr.tiles)
                self.proj_tok(wr_, tr_, 256, 0, 256, self.hT, self.t_h, tt, evr)
            snaps = []
            for d in range(2):
                st_ = pool.alloc(1024)
                S.op("dve", lambda e, st_=st_: e.memset(st_.f32()[:, 0:256], 0.0), writes=st_.tiles)
                sall = pool.alloc(NCORES * SW * 4)
                sa3 = sall.f32().rearrange("p (r n) -> p r n", n=SW)
                gs = self.gatS[l].ap().rearrange("(r q p) n -> p r q n", q=8, p=128)
                S.dma("sp", f"gs{d}", lambda e, d=d, h=h, sa3=sa3, gs=gs: e.dma_start(out=sa3, in_=gs[:, :, d * 4 + h, :]),
                      reads=[self.t_gat[("S", l)]], writes=sall.tiles)
                mcol, ocol = 16 + 8 * d, 32 + 8 * d
                import os
                NR = int(os.environ.get("KNR", "8"))
                for r in range(NR):
                    S.op("dve", lambda e, r=r, sa3=sa3, mcol=mcol: e.tensor_scalar(out=sa3[:, r, :], in0=sa3[:, r, :], scalar1=self.selm[:, mcol + r:mcol + r + 1], scalar2=0.0,
                                                                               op0=ALU.mult, op1=ALU.add),
                         reads=sall.tiles + [self.t_sel], writes=sall.tiles)
                dp = pool.alloc(1024)
                S.op("dve", lambda e, sa3=sa3, dp=dp, ocol=ocol: e.tensor_tensor(out=dp.f32()[:, 0:NCORES], in0=sa3[:, :, 256], in1=self.selm[:, ocol:ocol + NCORES], op=ALU.add),
                     reads=sall.tiles + [self.t_sel], writes=dp.tiles)
                for r in (range(NR) if d == 0 else range(NR - 1, -1, -1)):
                    S.op("dve", lambda e, st_=st_, dp=dp, sa3=sa3, r=r: e.scalar_tensor_tensor(out=st_.f32()[:, 0:256], in0=st_.f32()[:, 0:256], scalar=dp.f32()[:, r:r + 1],
                                                                                            in1=sa3[:, r, 0:256], op0=ALU.mult, op1=ALU.add),
                         reads=st_.tiles + dp.tiles + sall.tiles, writes=st_.tiles)
                pool.free(sall)
                sn = pool.alloc(NT * 256 * 2)
                self.gla_scan(d, R[d]["k_end"], vh, R[d]["dec"], st_, sn)
                snaps.append(sn)
                pool.free(st_, dp)
            mixed = pool.alloc(NT * 256 * 2)
            mx3 = mixed.bf().rearrange("p (a n) -> p a n", n=256)
            sn3 = [s_.bf().rearrange("p (a n) -> p a n", n=256) for s_ in snaps]
            for tt in range(NT):
                tsl = slice(tt * 128, (tt + 1) * 128)
                bk, tb = self.bank()
                items = [(bk[:, d * 128:(d + 1) * 128], R[d]["k_inv"].bf()[:, tsl], R[d]["q_dec"].bf()[:, tsl], True, True) for d in range(2)]
                S.op("pe", mmgroup(items), reads=R[0]["k_inv"].tiles + R[0]["q_dec"].tiles + R[1]["k_inv"].tiles + R[1]["q_dec"].tiles, writes=[tb])
                at = pool.alloc(256 * 2)
                S.op("dve", lambda e, bk=bk, at=at: e.tensor_tensor(out=at.bf()[:, 0:256], in0=bk[:, 0:256], in1=self.mask2[:, :], op=ALU.mult),
                     reads=[tb, self.t_const], writes=at.tiles)
                ob, tob = self.bank()
                items = [(ob[:, 0:256], at.bf()[:, 0:128], v3[:, tt, :], True, False),
                         (ob[:, 0:256], at.bf()[:, 128:256], v3[:, tt, :], False, False),
                         (ob[:, 0:256], R[0]["q_dec"].bf()[:, tsl], sn3[0][:, tt, :], False, False),
                         (ob[:, 0:256], R[1]["q_dec"].bf()[:, tsl], sn3[1][:, tt, :], False, True)]
                S.op("pe", mmgroup(items), reads=at.tiles + vh.tiles + R[0]["q_dec"].tiles + R[1]["q_dec"].tiles + snaps[0].tiles + snaps[1].tiles, writes=[tob])
                pool.free(at)
                sm = pool.alloc(1024)
                junk = pool.alloc(256 * 4)
                S.op("act", lambda e, ob=ob, junk=junk, sm=sm: e.activation(out=junk.f32()[:, 0:256], in_=ob[:, 0:256], func=AF.Square, accum_out=sm.f32()[:, 1:2]),
                     reads=[tob], writes=junk.tiles + sm.tiles)
                self.rstd_from(sm.f32()[:, 2:3], sm.f32()[:, 1:2], 1.0 / 256, sm.tiles, sm.tiles)
                S.op("dve", lambda e, ob=ob, sm=sm, tt=tt: e.scalar_tensor_tensor(out=mx3[:, tt, :], in0=ob[:, 0:256], scalar=sm.f32()[:, 2:3], in1=sr3[:, tt, :],
                                                                               op0=ALU.mult, op1=ALU.mult),
                     reads=[tob] + sm.tiles + sr.tiles, writes=mixed.tiles)
                pool.free(sm, junk)
            self.transpose_to_mixT(mixed, 2, 2 * h, gb + 48)
            pool.free(mixed, vh, sr, snaps[0], snaps[1])
            for d in range(2):
                pool.free(R[d]["q_dec"], R[d]["k_inv"], R[d]["k_end"], R[d]["dec"])
        pool.free(self.lrT, self.upaug)
        if self.dbg == "GLA":
            return
        self.out_proj(0)
        if self.dbg == "OP0":
            return
        pool.free(self.mixT)

        self.norm_T(self.xT, [[t] for t in self.t_x], T, gb + 16, self.hT, [[t] for t in self.t_h])
        import os
        for fq in range(int(os.environ.get("KMLPQ", "4"))):
            uT = pool.alloc(16 * T * 2)
            u3 = uT.bf().rearrange("p (c n) -> p c n", n=T)
            for j in range(8):
                wap, wt = self.wnext()
                for cc in range(2):
                    fc = 2 * j + cc
                    def evac(th, n, bk, tb, fc=fc):
                        r_ = pool.alloc(512 * 4)
                        S.op("act", lambda e: e.activation(out=r_.f32()[:, 0:512], in_=bk[:, :], func=AF.Relu), reads=[tb], writes=r_.tiles)
                        S.op("dve", lambda e: e.tensor_tensor(out=u3[:, fc, th * 512:(th + 1) * 512], in0=r_.f32()[:, 0:512], in1=r_.f32()[:, 0:512], op=ALU.mult),
                             reads=r_.tiles, writes=uT.tiles[2 * fc:2 * fc + 2])
                        pool.free(r_)
                    self.proj_feat(wap, wt, 256, 128 * cc, 128, self.hT, self.t_h, T, evac)
            for j in range(8):
                wap, wt = self.wnext()
                w3 = wap[:, 0:16 * 256].rearrange("p (k n) -> p k n", n=256)
                for cc in range(2):
                    dc = 2 * j + cc
                    for th in range(2):
                        bk, tb = self.bank()
                        items = [(bk[:, :], w3[:, kc, cc * 128:(cc + 1) * 128], u3[:, kc, th * 512:(th + 1) * 512], kc == 0, kc == 15) for kc in range(16)]
                        S.op("pe", mmgroup(items), reads=[wt] + uT.tiles, writes=[tb])
                        S.op("dve", lambda e, dc=dc, th=th, bk=bk: e.tensor_tensor(out=self.xT[:, dc, th * 512:(th + 1) * 512], in0=self.xT[:, dc, th * 512:(th + 1) * 512],
                                                                                in1=bk[:, :], op=ALU.add),
                             reads=[tb, self.t_x[dc]], writes=[self.t_x[dc]])
            pool.free(uT)

    def program_F(self):
        S, pool = self.S, self.pool
        self.setup_eps()
        S.op("dve", lambda e: e.memset(self.epsb.f32()[:, 1:2], 1.0), writes=[self.t_eps])
        for r in range(NCORES):
            S.op("dve", lambda e, r=r: e.tensor_scalar(out=self.IL[:, r, :], in0=self.ident[:], scalar1=self.selm[:, r:r + 1], scalar2=0.0, op0=ALU.mult, op1=ALU.add),
                 reads=[self.t_sel, self.t_constb], writes=[self.t_sel])
            S.op("dve", lambda e, r=r: e.tensor_scalar(out=self.IR[:, r, :], in0=self.ident[:], scalar1=self.selm[:, 8 + r:9 + r], scalar2=0.0, op0=ALU.mult, op1=ALU.add),
                 reads=[self.t_sel, self.t_constb], writes=[self.t_sel])
        dbg = self.dbg
        for l in range(self.depth):
            self.layer_A(l)
            if dbg != "A":
                self.layer_B(l)
        if self.dbg != "MLP":
            self.norm_T(self.xT, [[t] for t in self.t_x], T, 64 * DEPTH, self.xT, [[t] for t in self.t_x])
        for c in range(C):
            S.dma("sp", f"yout{c % 4}", lambda e, c=c: e.dma_start(out=self.o_y[:, c, :], in_=self.xT[:, c, :]), reads=[self.t_x[c]])
        S.wait_all_dma("sp")


def EPS_AP(prog):
    return prog.epsb.f32()[:, 0:1]


_PROGS = {}


def get_prog(depth=DEPTH):
    if depth not in _PROGS:
        p = Prog(depth)
        p.build()
        _PROGS[depth] = p
    return _PROGS[depth]


def host_inputs(x, mem, norm_mix, w_in, gla_gate_up_fwd, gla_gate_bias_fwd, gla_gate_up_bwd, gla_gate_bias_bwd,
                gla_norm, rel_bias, dil_norm, mem_norm, w_mem_kv, mem_out_norm, w_out, norm_mlp, w_up, w_down,
                norm_final, depth=DEPTH):
    f = lambda a: np.asarray(a, dtype=np.float32)
    x, mem, w_in, w_mem_kv, w_out, w_up, w_down = map(f, (x, mem, w_in, w_mem_kv, w_out, w_up, w_down))
    cm, mask2, ident, ones = host_consts()
    biasT = host_bias_tiles(f(rel_bias)).reshape(4, 128, NKT * 128)
    memT = np.ascontiguousarray(mem[0].T.reshape(C, 128, MEM_LEN).transpose(1, 0, 2))
    import os
    bl = blocks_layer()
    wst = np.concatenate([host_blocks(bl, {"w_in": w_in[l], "w_mem_kv": w_mem_kv[l], "w_out": w_out[l], "w_up": w_up[l], "w_down": w_down[l]})
                          for l in range(depth)], axis=0)[:int(os.environ.get("KNBLK", "100000"))]
    gl = []
    for l in range(DEPTH):
        gl += [feat_cols(f(norm_mix)[l]), feat_cols(f(norm_mlp)[l]), feat_cols(f(mem_norm)[l]),
               feat_cols(np.concatenate([f(gla_norm)[l], f(dil_norm)[l], f(mem_out_norm)[l]]))]
    gl.append(feat_cols(f(norm_final)))
    gains = np.ascontiguousarray(np.concatenate(gl, axis=1))
    upaug = np.stack([np.stack([np.concatenate([f(gla_gate_up_fwd)[l], f(gla_gate_bias_fwd)[l][None]], axis=0),
                                np.concatenate([f(gla_gate_up_bwd)[l], f(gla_gate_bias_bwd)[l][None]], axis=0)]) for l in range(DEPTH)])
    wparts = {f"wst{i}": np.ascontiguousarray(wst[i * WCH:(i + 1) * WCH]) for i in range((wst.shape[0] + WCH - 1) // WCH)}
    common = dict(gains=gains, cm=cm, mask2=mask2, ident=ident, ones=ones, upaug=np.ascontiguousarray(upaug), memT=memT, biasT=biasT, **wparts)
    maps = []
    for c in range(NCORES):
        xT = np.ascontiguousarray(x[0, c * T:(c + 1) * T, :].T.reshape(C, 128, T).transpose(1, 0, 2))
        r = np.arange(NCORES)
        row = np.concatenate([(r == c - 1), (r == c + 1), (r < c), (r > c), ~(r < c), ~(r > c)]).astype(np.float32)
        selm = np.ascontiguousarray(np.broadcast_to(row[None, :], (128, 48)))
        maps.append(dict(xT=xT, selm=selm, **common))
    return maps


def kernel(x, mem, norm_mix, w_in, gla_gate_up_fwd, gla_gate_bias_fwd, gla_gate_up_bwd, gla_gate_bias_bwd,
           gla_norm, rel_bias, dil_norm, mem_norm, w_mem_kv, mem_out_norm, w_out, norm_mlp, w_up, w_down,
           norm_final):
    maps = host_inputs(x, mem, norm_mix, w_in, gla_gate_up_fwd, gla_gate_bias_fwd, gla_gate_up_bwd, gla_gate_bias_bwd,
                       gla_norm, rel_bias, dil_norm, mem_norm, w_mem_kv, mem_out_norm, w_out, norm_mlp, w_up, w_down, norm_final)
    prog = get_prog(DEPTH)
    cores = list(range(NCORES))
    res = run_bass_kernel_spmd(prog.nc, maps, core_ids=cores).results
    out = np.empty((1, SEQ, D), np.float32)
    for c in cores:
        out[0, c * T:(c + 1) * T, :] = np.asarray(res[c]["yT"]).transpose(1, 0, 2).reshape(D, T).T
    return out
```

```python
import contextlib
import math
import numpy as np
import ml_dtypes
import concourse.bass as bass
import concourse.mybir as mybir
from concourse.bass_utils import run_bass_kernel_spmd

F32 = mybir.dt.float32
BF16 = mybir.dt.bfloat16
AF = mybir.ActivationFunctionType
ALU = mybir.AluOpType

NCORES = 8
D = 2048
SEQ = 8192
T = SEQ // NCORES
NT = T // 128
C = D // 128
DEPTH = 4
DFF = 4 * D
EPS = 1e-6
MEM_LEN = 256
IN_W = 5152
O_GQ, O_GK, O_GV, O_GR, O_LRF, O_LRB, O_DQ, O_DK, O_DV, O_MQ = 0, 512, 1024, 2048, 3072, 3088, 3104, 3616, 4128, 4640
NKT = 25
VA = 132
NSLOT = 4
WCH = 53
SW = 264
NG = 64 * DEPTH + 16
BLK = 4096
ENGS = ("pe", "act", "dve", "pool", "sp")


class Tile:
    __slots__ = ("name", "w", "r")

    def __init__(self, name):
        self.name = name
        self.w = None
        self.r = []


class Rec:
    def __init__(self):
        self.calls = []

    def __getattr__(self, name):
        def f(*a, **k):
            self.calls.append((name, a, k))
            return None
        return f


def record(fn):
    r = Rec()
    fn(r)
    assert r.calls
    return r.calls


class Sched:
    def __init__(self, nc, stack):
        self.nc = nc
        self.stack = stack
        self.streams = {e: [] for e in ENGS}
        self.cnt = {e: 0 for e in ENGS}
        self.waited = {e: {} for e in ENGS}
        self.sems = {}
        self.dma_cnt = {}
        for e in ("pe", "act", "dve", "pool"):
            self.sems[e] = stack.enter_context(nc.semaphore("s_" + e))

    def dma_sem(self, name):
        key = "dma:" + name
        if key not in self.sems:
            self.sems[key] = self.stack.enter_context(self.nc.semaphore("d_" + name))
            self.dma_cnt[key] = 0
        return key

    def _deps(self, eng, reads, writes):
        deps = {}

        def add(tok):
            if tok is None:
                return
            k, c = tok
            if deps.get(k, 0) < c:
                deps[k] = c
        for t in reads:
            add(t.w)
        for t in writes:
            add(t.w)
            for r in t.r:
                add(r)
        waits = []
        for k, c in deps.items():
            if k == eng and eng not in ("act", "dve"):
                continue
            if self.waited[eng].get(k, 0) >= c:
                continue
            self.waited[eng][k] = c
            waits.append((k, c))
        return waits

    def _commit(self, tok, reads, writes):
        for t in reads:
            t.r.append(tok)
        for t in writes:
            t.w = tok
            t.r = []

    def op(self, eng, fn, reads=(), writes=()):
        waits = self._deps(eng, reads, writes)
        self.cnt[eng] += 1
        tok = (eng, self.cnt[eng])
        self.streams[eng].append((waits, record(fn), eng, 1))
        self._commit(tok, reads, writes)
        return tok

    def dma(self, eng, semname, fn, reads=(), writes=(), inc=16):
        key = self.dma_sem(semname)
        waits = self._deps(eng, reads, writes)
        prev = self.dma_cnt[key]
        if prev > 0 and self.waited[eng].get(key, 0) < prev:
            self.waited[eng][key] = prev
            waits.append((key, prev))
        self.dma_cnt[key] += inc
        tok = (key, self.dma_cnt[key])
        self.streams[eng].append((waits, record(fn), key, inc))
        self._commit(tok, reads, writes)
        return tok

    def wait_all_dma(self, eng):
        waits = [(k, c) for k, c in self.dma_cnt.items() if c > 0]
        self.streams[eng].append((waits, None, None, 0))

    def emit(self):
        nc = self.nc
        with nc.Block() as block:
            def run(e, name):
                for waits, fn, key, inc in self.streams[name]:
                    for k, c in waits:
                        e.wait_ge(self.sems[k], c)
                    if fn is not None:
                        ins = None
                        for (nm, a, k) in fn:
                            ins = getattr(e, nm)(*a, **k)
                        ins.then_inc(self.sems[key], inc)

            @block.tensor
            def _(e):
                run(e, "pe")

            @block.scalar
            def _(e):
                run(e, "act")

            @block.vector
            def _(e):
                run(e, "dve")

            @block.gpsimd
            def _(e):
                run(e, "pool")

            @block.sync
            def _(e):
                run(e, "sp")


PW = 256


class Buf:
    def __init__(self, pool, off, n, nbytes):
        self.pool, self.off, self.n, self.nb = pool, off, n, nbytes
        self.tiles = pool.tiles[off // PW:(off + n + PW - 1) // PW]

    def f32(self):
        return self.pool.t[:, self.off:self.off + self.nb // 4]

    def bf(self):
        return self.pool.t[:, self.off:self.off + self.n].bitcast(BF16)[:, 0:self.nb // 2]


class PagePool:
    def __init__(self, t, nwords):
        self.t = t
        self.np = nwords // PW
        self.tiles = [Tile(f"pg{i}") for i in range(self.np)]
        self.used = [False] * self.np
        self.peak = 0
        self.cur = 0

    def alloc(self, nbytes, top=False):
        k = (nbytes + 4 * PW - 1) // (4 * PW)
        SM = 20
        n = self.np

        def scan(idx_iter):
            run = 0
            prev = None
            for i in idx_iter:
                if prev is not None and abs(i - prev) != 1:
                    run = 0
                run = run + 1 if not self.used[i] else 0
                prev = i
                if run == k:
                    return min(i, i - (k - 1) * (1 if prev is None else 1)) if False else i
            return None
        cands = []
        if top:
            cands = [list(range(n - 1, SM - 1, -1)), list(range(n - 1, -1, -1))]
        elif k <= 2:
            cands = [list(range(self.cur, SM)), list(range(0, SM)), list(range(SM, n))]
        else:
            cands = [list(range(SM, n)), list(range(0, n))]
        for idxs in cands:
            run = 0
            for pos, i in enumerate(idxs):
                run = run + 1 if not self.used[i] else 0
                if run == k:
                    pages = idxs[pos - k + 1:pos + 1]
                    s0 = min(pages)
                    for j in pages:
                        self.used[j] = True
                    self.peak = max(self.peak, sum(self.used))
                    if k <= 2 and not top:
                        self.cur = (s0 + k) % SM
                    return Buf(self, s0 * PW, k * PW, nbytes)
        raise RuntimeError(f"pool OOM: need {k} pages, used {sum(self.used)}/{self.np}: " + "".join("X" if u else "." for u in self.used))

    def free(self, *bufs):
        for b in bufs:
            for j in range(b.off // PW, (b.off + b.n) // PW):
                self.used[j] = False


def mmgroup(items):
    def fn(e):
        ins = None
        for (o, l, r, st, sp) in items:
            ins = e.matmul(o, lhsT=l, rhs=r, start=st, stop=sp)
        return ins
    return fn


def blocks_A():
    b = []
    b += [("w_in", 0, 16, O_DK + 256 * j, 256) for j in range(2)]
    b += [("w_in", 0, 16, O_DV + 256 * j, 256) for j in range(2)]
    b += [("w_in", 0, 16, O_LRF, 32)]
    for h in range(4):
        b += [("gqk", h)]
        b += [("w_in", 0, 16, O_GV + 256 * h, 256)]
    return b


def blocks_B():
    b = []
    b += [("w_mem_kv", 0, 16, 256 * j, 256) for j in range(4)]
    b += [("w_in", 0, 16, O_MQ + 256 * j, 256) for j in range(2)]
    b += [("w_in", 0, 16, O_DQ + 256 * j, 256) for j in range(2)]
    b += [("w_out", 8, 8, 512 * j, 512) for j in range(4)]
    b += [("w_in", 0, 16, O_LRF, 32)]
    for h in range(4):
        b += [("gqk", h)]
        b += [("w_in", 0, 16, O_GV + 256 * h, 256)]
        b += [("w_in", 0, 16, O_GR + 256 * h, 256)]
    b += [("w_out", 0, 8, 512 * j, 512) for j in range(4)]
    for fq in range(4):
        b += [("w_up", 0, 16, fq * 2048 + 256 * j, 256) for j in range(8)]
        b += [("w_down", fq * 16, 16, 256 * j, 256) for j in range(8)]
    return b


def blocks_layer():
    return blocks_A() + blocks_B()


def host_blocks(blist, w):
    out = np.zeros((len(blist), 128, BLK), np.float32)
    for i, bd in enumerate(blist):
        if bd[0] == "gqk":
            h = bd[1]
            m = w["w_in"]
            a = np.concatenate([m[:, O_GQ + 128 * h:O_GQ + 128 * h + 128],
                                m[:, O_GK + 128 * h:O_GK + 128 * h + 128]], axis=1)
            blk = a.reshape(16, 128, 256).transpose(1, 0, 2).reshape(128, 16 * 256)
        else:
            name, kc0, nkc, c0, ncol = bd
            m = w[name][kc0 * 128:(kc0 + nkc) * 128, c0:c0 + ncol]
            blk = m.reshape(nkc, 128, ncol).transpose(1, 0, 2).reshape(128, nkc * ncol)
        out[i, :, :blk.shape[1]] = blk
    return out


def t5_bucket_np(rel):
    half, max_exact = 16, 8
    ret = np.where(rel > 0, half, 0)
    n = np.abs(rel)
    nf = np.maximum(n, 1).astype(np.float32)
    large = max_exact + (np.log(nf / np.float32(max_exact)) / np.float32(math.log(1024 / max_exact))
                         * np.float32(half - max_exact)).astype(np.int32)
    large = np.minimum(large, half - 1)
    return ret + np.where(n < max_exact, n, large)


def key_tile_offsets():
    offs = [(1, -128), (1, 0), (1, 128)]
    offs += [(4, -256 + 128 * j) for j in range(5)]
    offs += [(16, -1024 + 128 * j) for j in range(17)]
    return offs


def host_bias_tiles(rel_bias):
    out = np.full((4, 128, NKT, 128), -30000.0, np.float32)
    k = np.arange(128)[:, None]
    q = np.arange(128)[None, :]
    for j, (d, off) in enumerate(key_tile_offsets()):
        rel = off + k - q
        valid = (rel % d == 0) & (np.abs(rel) <= 64 * d)
        bk = t5_bucket_np(rel)
        for h in range(4):
            vals = rel_bias[bk, h]
            out[h, :, j, :] = np.where(valid, vals, np.float32(-30000.0))
    return out


def host_consts():
    s = np.arange(128)[:, None]
    t = np.arange(128)[None, :]
    v = np.float32(-1.0 / 16.0)
    cm = np.zeros((4, 128, 128), np.float32)
    cm[0] = np.where(s <= t, v, 0)
    cm[1] = np.where(s >= t, v, 0)
    cm[2] = np.where(s > t, v, 0)
    cm[3] = np.where(s < t, v, 0)
    mask2 = np.concatenate([(s <= t), (s >= t)], axis=1).astype(np.float32)
    ident = np.eye(128, dtype=np.float32)
    ones = np.ones((128, 128), np.float32)
    return cm, mask2, ident, ones


def feat_cols(vec):
    return np.ascontiguousarray(vec.reshape(-1, 128).T)


class Prog:
    def __init__(self, depth=DEPTH):
        self.depth = depth
        self.nc = bass.Bass("TRN2", target_bir_lowering=False)

    def build(self):
        nc = self.nc
        depth = self.depth
        with contextlib.ExitStack() as st:
            st.enter_context(nc.allow_low_precision("bf16 matmul operands, fp32 accumulate"))
            self.st = st
            self.S = S = Sched(nc, st)
            din = lambda name, shape, dt=F32: nc.dram_tensor(name, shape, dt, kind="ExternalInput").ap()
            dout = lambda name, shape, dt=F32: nc.dram_tensor(name, shape, dt, kind="ExternalOutput").ap()
            sb = lambda name, shape, dt: st.enter_context(nc.sbuf_tensor(name, shape, dt))

            import os
            self.dbg = os.environ.get("KDBG", "")
            self.blist = (blocks_layer() * depth)[:int(os.environ.get("KNBLK", "100000"))]
            self.d_x = din("xT", [128, C, T])
            nb = len(self.blist)
            self.d_w = [din(f"wst{i}", [min(WCH, nb - i * WCH), 128, BLK]) for i in range((nb + WCH - 1) // WCH)]
            self.d_gains = din("gains", [128, NG])
            self.d_cm = din("cm", [4, 128, 128])
            self.d_mask2 = din("mask2", [128, 256])
            self.d_ident = din("ident", [128, 128])
            self.d_ones = din("ones", [128, 128])
            self.d_upaug = din("upaug", [DEPTH, 2, 17, 512])
            self.d_memT = din("memT", [128, C, MEM_LEN])
            self.d_bias = din("biasT", [4, 128, NKT * 128])
            self.d_selm = din("selm", [128, 48])
            self.o_y = dout("yT", [128, C, T])
            idram = lambda name, shape, dt: nc.dram_tensor(name, shape, dt)
            self.payK = [idram(f"payK{l}", [512, T], BF16) for l in range(depth)]
            self.gatK = [idram(f"gatK{l}", [NCORES * 512, T], BF16) for l in range(depth)]
            self.payV = [idram(f"payV{l}", [T, 4 * VA], BF16) for l in range(depth)]
            self.gatV = [idram(f"gatV{l}", [NCORES * T, 4 * VA], BF16) for l in range(depth)]
            self.payS = [idram(f"payS{l}", [1024, SW], F32) for l in range(depth)]
            self.gatS = [idram(f"gatS{l}", [NCORES * 1024, SW], F32) for l in range(depth)]
            self.t_pay = {(k, l): Tile(f"pay{k}{l}") for k in "KVS" for l in range(depth)}
            self.t_gat = {(k, l): Tile(f"gat{k}{l}") for k in "KVS" for l in range(depth)}

            self.xT = sb("xT_sb", [128, C, T], F32)
            self.t_x = [Tile(f"x{c}") for c in range(C)]
            self.hT = sb("hT_sb", [128, C, T], BF16)
            self.t_h = [Tile(f"h{c}") for c in range(C)]
            self.wsl = sb("wslots", [128, NSLOT, BLK], BF16)
            self.t_ws = [Tile(f"ws{i}") for i in range(NSLOT)]
            self.gains = [sb(f"gains_sb{i}", [128, 64], F32) for i in range(DEPTH + 1)]
            self.selm = sb("selm_sb", [128, 48], F32)
            self.IL = sb("IL_sb", [128, NCORES, 128], BF16)
            self.IR = sb("IR_sb", [128, NCORES, 128], BF16)
            self.t_sel = Tile("sel")
            self.t_gains = Tile("gains")
            self.cm = sb("cm_sb", [128, 4, 128], F32)
            self.mask2 = sb("mask2_sb", [128, 256], F32)
            self.ident = sb("ident_sb", [128, 128], BF16)
            self.ones = sb("ones_sb", [128, 128], BF16)
            self.t_const = Tile("const")
            self.t_constb = Tile("constb")
            npw = (nc.sbuf_bytes_remaining - 2048) // 4
            npw = (npw // PW) * PW
            self.pool = PagePool(sb("pool_sb", [128, npw], F32), npw)
            self.ps = [st.enter_context(nc.psum_tensor(f"ps{i}", [128, 512], F32)) for i in range(8)]
            self.t_ps = [Tile(f"ps{i}") for i in range(8)]
            self.ps_i = 0
            self.w_load = 0
            self.w_use = 0

            for i in range(DEPTH + 1):
                w_ = 64 if i < DEPTH else 16
                S.dma("sp", "c0", lambda e, i=i, w_=w_: e.dma_start(out=self.gains[i][:, 0:w_], in_=self.d_gains[:, 64 * i:64 * i + w_]), writes=[self.t_gains])
            S.dma("sp", "c0", lambda e: e.dma_start(out=self.selm[:], in_=self.d_selm[:, :]), writes=[self.t_sel])
            S.dma("sp", "c1", lambda e: e.dma_start(out=self.cm[:], in_=self.d_cm.rearrange("a p n -> p a n")), writes=[self.t_const])
            S.dma("sp", "c1", lambda e: e.dma_start(out=self.mask2[:], in_=self.d_mask2[:, :]), writes=[self.t_const])
            S.dma("pool", "c2", lambda e: e.dma_start(out=self.ident[:], in_=self.d_ident[:, :]), writes=[self.t_constb])
            S.dma("pool", "c2", lambda e: e.dma_start(out=self.ones[:], in_=self.d_ones[:, :]), writes=[self.t_constb])
            for c in range(C):
                S.dma("sp", f"x{c % 4}", (lambda c: lambda e: e.dma_start(out=self.xT[:, c, :], in_=self.d_x[:, c, :]))(c),
                      writes=[self.t_x[c]])

            self.program_F()
            S.emit()
        return nc

    def bank(self):
        i = self.ps_i
        self.ps_i = (i + 1) % 8
        return self.ps[i], self.t_ps[i]

    def wnext(self):
        i = self.w_use
        self.w_use += 1
        nblk = len(self.blist)
        while self.w_load < min(nblk, i + NSLOT - 1):
            j = self.w_load
            s = j % NSLOT
            self.S.dma("pool", f"w{s}", (lambda j, s: lambda e: e.dma_start(out=self.wsl[:, s, :], in_=self.d_w[j // WCH][j % WCH]))(j, s),
                       writes=[self.t_ws[s]])
            self.w_load += 1
        s = i % NSLOT
        return self.wsl[:, s, :], self.t_ws[s]

    def oname(self):
        self._on = getattr(self, "_on", 0) + 1
        return f"oA{self._on % 4}"

    def gcol(self, idx):
        return self.gains[idx // 64][:, idx % 64:idx % 64 + 1]

    def rstd_from(self, out_ap, in_ap, scale, reads, writes):
        S = self.S
        S.op("act", lambda e: e.activation(out=out_ap, in_=in_ap, func=AF.Ln, scale=scale, bias=EPS_AP(self)), reads=reads + [self.t_eps], writes=writes)
        S.op("act", lambda e: e.activation(out=out_ap, in_=out_ap, func=AF.Exp, scale=-0.5), reads=writes, writes=writes)

    def norm_T(self, src, t_src, ntok, gbase, dst, t_dst, out_f32_dram=None):
        S, pool = self.S, self.pool
        nh = (ntok + 511) // 512
        for th in range(nh):
            n = min(512, ntok - th * 512)
            sl = slice(th * 512, th * 512 + n)
            acc, t_acc = self.bank()
            sq = [pool.alloc(n * 2) for _ in range(2)]
            for c in range(C):
                b = sq[c % 2]
                S.op("act", (lambda c, b: lambda e: e.activation(out=b.bf()[:, 0:n], in_=src[:, c, sl], func=AF.Square))(c, b),
                     reads=t_src[c], writes=b.tiles)
                S.op("pe", (lambda c, b: lambda e: e.matmul(acc[:, 0:n], lhsT=self.ones[:], rhs=b.bf()[:, 0:n], start=(c == 0), stop=(c == C - 1)))(c, b),
                     reads=b.tiles + [self.t_constb], writes=[t_acc])
            rs = pool.alloc(n * 4)
            self.rstd_from(rs.f32()[:, 0:n], acc[:, 0:n], 1.0 / D, [t_acc], rs.tiles)
            for c in range(C):
                if out_f32_dram is None:
                    S.op("dve", (lambda c: lambda e: e.scalar_tensor_tensor(out=dst[:, c, sl], in0=src[:, c, sl], scalar=self.gcol(gbase + c),
                                                                           in1=rs.f32()[:, 0:n], op0=ALU.mult, op1=ALU.mult))(c),
                         reads=t_src[c] + [self.t_gains] + rs.tiles, writes=t_dst[c])
                else:
                    ob = pool.alloc(n * 4)
                    S.op("dve", (lambda c, ob: lambda e: e.scalar_tensor_tensor(out=ob.f32()[:, 0:n], in0=src[:, c, sl], scalar=self.gcol(gbase + c),
                                                                               in1=rs.f32()[:, 0:n], op0=ALU.mult, op1=ALU.mult))(c, ob),
                         reads=t_src[c] + [self.t_gains] + rs.tiles, writes=ob.tiles)
                    S.dma("sp", f"yout{c % 4}", (lambda c, ob: lambda e: e.dma_start(out=out_f32_dram[:, c, sl], in_=ob.f32()[:, 0:n]))(c, ob),
                          reads=ob.tiles)
                    pool.free(ob)
            pool.free(rs, *sq)

    def proj_feat(self, wap, wt, kcols, col0, M, rhsT, t_rhs, ntok, evac):
        S = self.S
        w3 = wap[:, 0:16 * kcols].rearrange("p (k n) -> p k n", n=kcols)
        nh = (ntok + 511) // 512
        for th in range(nh):
            n = min(512, ntok - th * 512)
            bk, tb = self.bank()
            items = [(bk[0:M, 0:n], w3[:, kc, col0:col0 + M], rhsT[:, kc, th * 512:th * 512 + n], kc == 0, kc == C - 1) for kc in range(C)]
            S.op("pe", mmgroup(items), reads=[wt] + t_rhs, writes=[tb])
            evac(th, n, bk, tb)

    def proj_tok(self, wap, wt, kcols, col0, N, lhsT, t_lhs, tt, evac):
        S = self.S
        w3 = wap[:, 0:16 * kcols].rearrange("p (k n) -> p k n", n=kcols)
        bk, tb = self.bank()
        items = [(bk[:, 0:N], lhsT[:, kc, tt * 128:(tt + 1) * 128], w3[:, kc, col0:col0 + N], kc == 0, kc == C - 1) for kc in range(C)]
        S.op("pe", mmgroup(items), reads=[wt] + t_lhs, writes=[tb])
        evac(bk, tb)

    def setup_eps(self):
        S = self.S
        self.epsb = self.pool.alloc(1024, top=True)
        self.t_eps = Tile("eps")
        S.op("dve", lambda e: e.memset(self.epsb.f32()[:, 0:1], EPS), writes=[self.t_eps])

    def load_upaug(self, l):
        S, pool = self.S, self.pool
        self.upaug = pool.alloc(2 * 512 * 2)
        ap = self.upaug.bf().rearrange("p (a n) -> p a n", n=512)
        S.dma("pool", "upaug", lambda e: e.dma_start(out=ap[0:17, :, :], in_=self.d_upaug[l].rearrange("a r n -> r a n")), writes=self.upaug.tiles)
        self.upaug_ap = ap

    def compute_lr(self):
        S, pool = self.S, self.pool
        wap, wt = self.wnext()
        self.lrT = pool.alloc(2 * T * 2)
        lr = self.lrT.bf().rearrange("p (a n) -> p a n", n=T)
        S.op("dve", lambda e: e.memset(lr[0:17, :, :], 1.0), writes=self.lrT.tiles)
        for d in range(2):
            def evac(th, n, bk, tb, d=d):
                S.op("act", lambda e: e.activation(out=lr[0:16, d, th * 512:th * 512 + n], in_=bk[0:16, 0:n], func=AF.Copy),
                     reads=[tb], writes=self.lrT.tiles)
            self.proj_feat(wap, wt, 32, 16 * d, 16, self.hT, self.t_h, T, evac)
        self.lr_ap = lr

    def gla_gates(self, h, d):
        S, pool = self.S, self.pool
        g = pool.alloc(NT * 128 * 4)
        g3 = g.f32().rearrange("p (a n) -> p a n", n=128)
        for half in range(2):
            bk, tb = self.bank()
            items = []
            for i in range(4):
                tt = half * 4 + i
                items.append((bk[:, i * 128:(i + 1) * 128], self.lr_ap[0:17, d, tt * 128:(tt + 1) * 128],
                              self.upaug_ap[0:17, d, h * 128:(h + 1) * 128], True, True))
            S.op("pe", mmgroup(items), reads=self.lrT.tiles + self.upaug.tiles, writes=[tb])
            dst = g.f32()[:, half * 512:(half + 1) * 512]
            S.op("act", lambda e, bk=bk, dst=dst: e.activation(out=dst, in_=bk[:, :], func=AF.Exp, scale=-1.0), reads=[tb], writes=g.tiles)
            S.op("act", lambda e, dst=dst: e.activation(out=dst, in_=dst, func=AF.Ln, bias=self.one_ap()), reads=g.tiles + [self.t_eps], writes=g.tiles)
        return g, g3

    def one_ap(self):
        return self.epsb.f32()[:, 1:2]

    def gla_decays(self, h, d, g, g3, qT, kT, need_q):
        S, pool = self.S, self.pool
        res = {}
        bb = [self.bank(), self.bank()]
        for half in range(2):
            bk, tb = bb[half]
            items = []
            for i in range(4):
                tt = half * 4 + i
                items.append((bk[:, i * 128:(i + 1) * 128], g3[:, tt, :], self.cm[:, d, :], True, True))
            S.op("pe", mmgroup(items), reads=g.tiles + [self.t_const], writes=[tb])
        tmp = pool.alloc(T * 4)
        for half in range(2):
            bk, tb = bb[half]
            S.op("act", lambda e, bk=bk, half=half: e.activation(out=tmp.f32()[:, half * 512:(half + 1) * 512], in_=bk[:, :], func=AF.Exp),
                 reads=[tb], writes=tmp.tiles)
        dec = pool.alloc(1024)
        col = 127 if d == 0 else 0
        S.op("dve", lambda e: e.tensor_copy(out=dec.f32()[:, 0:NT], in_=tmp.f32()[:, col:T:128]), reads=tmp.tiles, writes=dec.tiles)
        res["dec"] = dec
        if need_q:
            qd = pool.alloc(T * 2)
            S.op("dve", lambda e: e.tensor_tensor(out=qd.bf()[:, 0:T], in0=qT.f32()[:, 0:T], in1=tmp.f32()[:, 0:T], op=ALU.mult),
                 reads=qT.tiles + tmp.tiles, writes=qd.tiles)
            res["q_dec"] = qd
            tmp2 = pool.alloc(T * 4)
            for half in range(2):
                bk, tb = bb[half]
                S.op("act", lambda e, bk=bk, half=half: e.activation(out=tmp2.f32()[:, half * 512:(half + 1) * 512], in_=bk[:, :], func=AF.Exp, scale=-1.0),
                     reads=[tb], writes=tmp2.tiles)
            ki = pool.alloc(T * 2)
            S.op("dve", lambda e: e.tensor_tensor(out=ki.bf()[:, 0:T], in0=kT.f32()[:, 0:T], in1=tmp2.f32()[:, 0:T], op=ALU.mult),
                 reads=kT.tiles + tmp2.tiles, writes=ki.tiles)
            res["k_inv"] = ki
            pool.free(tmp2)
        eb = [self.bank(), self.bank()]
        for half in range(2):
            bk, tb = eb[half]
            items = []
            for i in range(4):
                tt = half * 4 + i
                items.append((bk[:, i * 128:(i + 1) * 128], g3[:, tt, :], self.cm[:, 2 + d, :], True, True))
            S.op("pe", mmgroup(items), reads=g.tiles + [self.t_const], writes=[tb])
        for half in range(2):
            bk, tb = eb[half]
            S.op("act", lambda e, bk=bk, half=half: e.activation(out=tmp.f32()[:, half * 512:(half + 1) * 512], in_=bk[:, :], func=AF.Exp),
                 reads=[tb], writes=tmp.tiles)
        keT = pool.alloc(T * 2)
        S.op("dve", lambda e: e.tensor_tensor(out=keT.bf()[:, 0:T], in0=kT.f32()[:, 0:T], in1=tmp.f32()[:, 0:T], op=ALU.mult),
             reads=kT.tiles + tmp.tiles, writes=keT.tiles)
        bk, tb = self.bank()
        bkb = bk[:].bitcast(BF16)
        def trfn(e):
            ins = None
            for tt in range(NT):
                ins = e.transpose(bkb[:, tt * 128:(tt + 1) * 128], keT.bf()[:, tt * 128:(tt + 1) * 128], self.ident[:])
            return ins
        S.op("pe", trfn, reads=keT.tiles + [self.t_constb], writes=[tb])
        ke = pool.alloc(T * 2)
        S.op("act", lambda e: e.activation(out=ke.bf()[:, 0:T], in_=bkb[:, :], func=AF.Copy), reads=[tb], writes=ke.tiles)
        res["k_end"] = ke
        pool.free(tmp, keT)
        return res

    def gla_scan(self, d, ke, vh, dec, st, snaps):
        S, pool = self.S, self.pool
        ke3 = ke.bf().rearrange("p (a n) -> p a n", n=128)
        v3 = vh.bf().rearrange("p (a n) -> p a n", n=256)
        order = list(range(NT)) if d == 0 else list(range(NT - 1, -1, -1))
        for tt in order:
            if snaps is not None:
                sn3 = snaps.bf().rearrange("p (a n) -> p a n", n=256)
                S.op("act", lambda e, tt=tt, sn3=sn3: e.activation(out=sn3[:, tt, :], in_=st.f32()[:, 0:256], func=AF.Copy),
                     reads=st.tiles, writes=snaps.tiles)
            if snaps is not None and tt == order[-1]:
                break
            bk, tb = self.bank()
            S.op("pe", lambda e, tt=tt, bk=bk: e.matmul(bk[:, 0:256], lhsT=ke3[:, tt, :], rhs=v3[:, tt, :], start=True, stop=True),
                 reads=ke.tiles + vh.tiles, writes=[tb])
            S.op("dve", lambda e, tt=tt, bk=bk: e.scalar_tensor_tensor(out=st.f32()[:, 0:256], in0=st.f32()[:, 0:256], scalar=dec.f32()[:, tt:tt + 1],
                                                                      in1=bk[:, 0:256], op0=ALU.mult, op1=ALU.add),
                 reads=st.tiles + dec.tiles + [tb], writes=st.tiles)

    def allgather(self, k, l, pay, gat):
        S = self.S
        S.dma("pool", f"cc{k}", lambda e: e.collective_compute("AllGather", ALU.bypass, replica_groups=[list(range(NCORES))],
                                                                   ins=[pay.ap().opt()], outs=[gat.ap().opt()], dma_qos="P2"),
              reads=[self.t_pay[(k, l)]], writes=[self.t_gat[(k, l)]], inc=1)

    def layer_A(self, l):
        S, pool = self.S, self.pool
        gb = 64 * l
        self.norm_T(self.xT, [[t] for t in self.t_x], T, gb, self.hT, [[t] for t in self.t_h])
        self.kTown = pool.alloc(4 * T * 2, top=True)
        self.kTown_ap = self.kTown.bf().rearrange("p (h n) -> p h n", n=T)
        for j in range(2):
            wap, wt = self.wnext()
            for hh in range(2):
                h = 2 * j + hh
                def evac(th, n, bk, tb, h=h):
                    S.op("act", lambda e: e.activation(out=self.kTown_ap[:, h, th * 512:th * 512 + n], in_=bk[:, 0:n], func=AF.Copy), reads=[tb], writes=self.kTown.tiles)
                self.proj_feat(wap, wt, 256, 128 * hh, 128, self.hT, self.t_h, T, evac)
                S.dma("sp", "payK", lambda e, h=h: e.dma_start(out=self.payK[l].ap()[h * 128:(h + 1) * 128, :], in_=self.kTown_ap[:, h, :]),
                      reads=self.kTown.tiles, writes=[self.t_pay[("K", l)]])
        self.allgather("K", l, self.payK[l], self.gatK[l])
        self.vaown = pool.alloc(NT * 4 * VA * 2, top=True)
        self.vaown_ap = self.vaown.bf().rearrange("p (t n) -> p t n", n=4 * VA)
        va4 = self.vaown.bf().rearrange("p (t a n) -> p t a n", t=NT, a=4)
        S.op("dve", lambda e: e.memset(self.vaown.bf()[:, :], 1.0), writes=self.vaown.tiles)
        wv = [self.wnext(), self.wnext()]
        for tt in range(NT):
            for j in range(2):
                def evac(bk, tb, j=j, tt=tt):
                    S.op("act", lambda e: e.activation(out=va4[:, tt, 2 * j:2 * j + 2, 0:128], in_=bk[:, 0:256].rearrange("p (a n) -> p a n", n=128), func=AF.Copy),
                         reads=[tb], writes=self.vaown.tiles)
                self.proj_tok(wv[j][0], wv[j][1], 256, 0, 256, self.hT, self.t_h, tt, evac)
            S.dma("sp", "payV", lambda e, tt=tt: e.dma_start(out=self.payV[l].ap()[tt * 128:(tt + 1) * 128, :], in_=self.vaown_ap[:, tt, :]),
                  reads=self.vaown.tiles, writes=[self.t_pay[("V", l)]])
        self.allgather("V", l, self.payV[l], self.gatV[l])
        self.load_upaug(l)
        self.compute_lr()
        for h in range(4):
            wqk, tqk = self.wnext()
            kT = pool.alloc(T * 4)
            def evk(th, n, bk, tb, kT=kT):
                S.op("act", lambda e: e.activation(out=kT.f32()[:, th * 512:th * 512 + n], in_=bk[:, 0:n], func=AF.Copy), reads=[tb], writes=kT.tiles)
            self.proj_feat(wqk, tqk, 256, 128, 128, self.hT, self.t_h, T, evk)
            wv_, tv_ = self.wnext()
            vh = pool.alloc(NT * 256 * 2)
            v3 = vh.bf().rearrange("p (a n) -> p a n", n=256)
            for tt in range(NT):
                def evv(bk, tb, tt=tt):
                    S.op("act", lambda e: e.activation(out=v3[:, tt, :], in_=bk[:, 0:256], func=AF.Copy), reads=[tb], writes=vh.tiles)
                self.proj_tok(wv_, tv_, 256, 0, 256, self.hT, self.t_h, tt, evv)
            for d in range(2):
                g, g3 = self.gla_gates(h, d)
                r = self.gla_decays(h, d, g, g3, None, kT, False)
                pool.free(g)
                st_ = pool.alloc(SW * 4)
                S.op("dve", lambda e, st_=st_: e.memset(st_.f32()[:, 0:SW], 0.0), writes=st_.tiles)
                self.gla_scan(d, r["k_end"], vh, r["dec"], st_, None)
                r0 = (d * 4 + h) * 128
                dec = r["dec"]
                S.op("dve", lambda e, dec=dec, st_=st_: e.tensor_tensor(out=st_.f32()[:, 256:257], in0=dec.f32()[:, 0:1], in1=dec.f32()[:, 1:2], op=ALU.mult),
                     reads=dec.tiles, writes=st_.tiles)
                for tt in range(2, NT):
                    S.op("dve", lambda e, dec=dec, st_=st_, tt=tt: e.tensor_tensor(out=st_.f32()[:, 256:257], in0=st_.f32()[:, 256:257], in1=dec.f32()[:, tt:tt + 1], op=ALU.mult),
                         reads=dec.tiles + st_.tiles, writes=st_.tiles)
                S.dma("sp", "payS", lambda e, r0=r0, st_=st_: e.dma_start(out=self.payS[l].ap()[r0:r0 + 128, :], in_=st_.f32()[:, 0:SW]),
                      reads=st_.tiles, writes=[self.t_pay[("S", l)]])
                pool.free(st_, r["k_end"], r["dec"])
            pool.free(kT, vh)
        pool.free(self.lrT, self.upaug)
        self.allgather("S", l, self.payS[l], self.gatS[l])

    def attn_epilogue(self, obk, tob, mixed_ap, t_mixed):
        S, pool = self.S, self.pool
        sm = pool.alloc(1024)
        on = pool.alloc(128 * 4)
        junk = pool.alloc(128 * 4)
        S.op("dve", lambda e: e.reciprocal(out=sm.f32()[:, 0:1], in_=obk[:, 128:129]), reads=[tob], writes=sm.tiles)
        S.op("dve", lambda e: e.tensor_scalar(out=on.f32()[:, 0:128], in0=obk[:, 0:128], scalar1=sm.f32()[:, 0:1], scalar2=0.0, op0=ALU.mult, op1=ALU.add),
             reads=[tob] + sm.tiles, writes=on.tiles)
        S.op("act", lambda e: e.activation(out=junk.f32()[:, 0:128], in_=on.f32()[:, 0:128], func=AF.Square, accum_out=sm.f32()[:, 1:2]),
             reads=on.tiles, writes=junk.tiles + sm.tiles)
        self.rstd_from(sm.f32()[:, 2:3], sm.f32()[:, 1:2], 1.0 / 128, sm.tiles, sm.tiles)
        S.op("act", lambda e: e.activation(out=mixed_ap, in_=on.f32()[:, 0:128], func=AF.Copy, scale=sm.f32()[:, 2:3]),
             reads=on.tiles + sm.tiles, writes=t_mixed)
        pool.free(sm, on, junk)

    def transpose_to_mixT(self, mixed, nchunk, chunk0, gbase):
        S = self.S
        w = nchunk * 128
        m3 = mixed.bf().rearrange("p (a n) -> p a n", n=w)
        for ci in range(nchunk):
            bk, tb = self.bank()
            bkb = bk[:].bitcast(BF16)
            def trfn(e, ci=ci, bkb=bkb):
                ins = None
                for tt in range(NT):
                    ins = e.transpose(bkb[:, tt * 128:(tt + 1) * 128], m3[:, tt, ci * 128:(ci + 1) * 128], self.ident[:])
                return ins
            S.op("pe", trfn, reads=mixed.tiles + [self.t_constb], writes=[tb])
            dst = self.mixT_ap[:, (chunk0 + ci) % 8, :]
            S.op("act", lambda e, dst=dst, bkb=bkb, ci=ci: e.activation(out=dst, in_=bkb[:, :], func=AF.Copy, scale=self.gcol(gbase + chunk0 + ci)),
                 reads=[tb, self.t_gains], writes=self.t_mixT[(chunk0 + ci) % 8])

    def alloc_mixT(self):
        self.mixT = self.pool.alloc(8 * T * 2, top=True)
        self.mixT_ap = self.mixT.bf().rearrange("p (c n) -> p c n", n=T)
        self.t_mixT = [self.mixT.tiles[2 * c:2 * c + 2] for c in range(8)]

    def out_proj(self, chunk0):
        S = self.S
        for j in range(4):
            wap, wt = self.wnext()
            w3 = wap[:, 0:8 * 512].rearrange("p (k n) -> p k n", n=512)
            for dd in range(4):
                dc = 4 * j + dd
                for th in range(2):
                    bk, tb = self.bank()
                    items = [(bk[:, :], w3[:, kc, dd * 128:(dd + 1) * 128], self.mixT_ap[:, kc, th * 512:(th + 1) * 512], kc == 0, kc == 7) for kc in range(8)]
                    S.op("pe", mmgroup(items), reads=[wt] + self.mixT.tiles, writes=[tb])
                    S.op("dve", lambda e, dc=dc, th=th, bk=bk: e.tensor_tensor(out=self.xT[:, dc, th * 512:(th + 1) * 512], in0=self.xT[:, dc, th * 512:(th + 1) * 512],
                                                                            in1=bk[:, :], op=ALU.add),
                         reads=[tb, self.t_x[dc]], writes=[self.t_x[dc]])

    def layer_B(self, l):
        S, pool = self.S, self.pool
        gb = 64 * l
        self.alloc_mixT()
        SC = 128 ** -0.5

        memT = pool.alloc(C * MEM_LEN * 4)
        m3 = memT.f32().rearrange("p (c n) -> p c n", n=MEM_LEN)
        t_mem = [[memT.tiles[c]] for c in range(C)]
        for c in range(C):
            S.dma("sp", f"mem{c % 2}", lambda e, c=c: e.dma_start(out=m3[:, c, :], in_=self.d_memT[:, c, :]), writes=t_mem[c])
        hm = pool.alloc(C * MEM_LEN * 2)
        hm3 = hm.bf().rearrange("p (c n) -> p c n", n=MEM_LEN)
        t_hm = [[hm.tiles[c // 2]] for c in range(C)]
        self.norm_T(m3, t_mem, MEM_LEN, gb + 32, hm3, t_hm)
        kmT = pool.alloc(4 * MEM_LEN * 2)
        km3 = kmT.bf().rearrange("p (h n) -> p h n", n=MEM_LEN)
        for j in range(2):
            wap, wt = self.wnext()
            for hh in range(2):
                h = 2 * j + hh
                def evac(th, n, bk, tb, h=h):
                    S.op("act", lambda e: e.activation(out=km3[:, h, 0:n], in_=bk[:, 0:n], func=AF.Copy), reads=[tb], writes=kmT.tiles)
                self.proj_feat(wap, wt, 256, 128 * hh, 128, hm3, hm.tiles, MEM_LEN, evac)
        vma = pool.alloc(2 * 4 * VA * 2)
        vm4 = vma.bf().rearrange("p (m h n) -> p m h n", m=2, h=4)
        S.op("dve", lambda e: e.memset(vm4[:, :, :, 128:VA], 1.0), writes=vma.tiles)
        for j in range(2):
            wap, wt = self.wnext()
            for mt in range(2):
                def evac(bk, tb, j=j, mt=mt):
                    S.op("act", lambda e: e.activation(out=vm4[:, mt, 2 * j:2 * j + 2, 0:128], in_=bk[:, 0:256].rearrange("p (a n) -> p a n", n=128), func=AF.Copy),
                         reads=[tb], writes=vma.tiles)
                self.proj_tok(wap, wt, 256, 0, 256, hm3, hm.tiles, mt, evac)
        pool.free(memT, hm)
        pbufs = [pool.alloc(256 * 2) for _ in range(3)]
        for h in range(4):
            if h % 2 == 0:
                wap, wt = self.wnext()
            qT = pool.alloc(T * 2)
            def evq(th, n, bk, tb, qT=qT):
                S.op("act", lambda e: e.activation(out=qT.bf()[:, th * 512:th * 512 + n], in_=bk[:, 0:n], func=AF.Copy, scale=SC), reads=[tb], writes=qT.tiles)
            self.proj_feat(wap, wt, 256, 128 * (h % 2), 128, self.hT, self.t_h, T, evq)
            mixed = pool.alloc(NT * 128 * 2)
            mx3 = mixed.bf().rearrange("p (a n) -> p a n", n=128)
            for qt in range(NT):
                bk, tb = self.bank()
                items = [(bk[:, mt * 128:(mt + 1) * 128], km3[:, h, mt * 128:(mt + 1) * 128], qT.bf()[:, qt * 128:(qt + 1) * 128], True, True) for mt in range(2)]
                S.op("pe", mmgroup(items), reads=kmT.tiles + qT.tiles, writes=[tb])
                p = pbufs[qt % 3]
                S.op("act", lambda e, bk=bk, p=p: e.activation(out=p.bf()[:, 0:256], in_=bk[:, 0:256], func=AF.Exp), reads=[tb], writes=p.tiles)
                ob, tob = self.bank()
                items = [(ob[:, 0:129], p.bf()[:, mt * 128:(mt + 1) * 128], vm4[:, mt, h, 0:129], mt == 0, mt == 1) for mt in range(2)]
                S.op("pe", mmgroup(items), reads=p.tiles + vma.tiles, writes=[tob])
                self.attn_epilogue(ob, tob, mx3[:, qt, :], mixed.tiles)
            self.transpose_to_mixT(mixed, 1, 12 + h, gb + 48)
            pool.free(qT, mixed)
        pool.free(kmT, vma, *pbufs)
        if self.dbg == "MEM":
            return

        offs = key_tile_offsets()
        for h in range(4):
            if h % 2 == 0:
                wap, wt = self.wnext()
            qT = pool.alloc(T * 2)
            def evq(th, n, bk, tb, qT=qT):
                S.op("act", lambda e: e.activation(out=qT.bf()[:, th * 512:th * 512 + n], in_=bk[:, 0:n], func=AF.Copy, scale=SC), reads=[tb], writes=qT.tiles)
            self.proj_feat(wap, wt, 256, 128 * (h % 2), 128, self.hT, self.t_h, T, evq)
            kall = pool.alloc(NCORES * T * 2)
            ka3 = kall.bf().rearrange("p (r n) -> p r n", n=T)
            gk = self.gatK[l].ap().rearrange("(r h e) t -> e r h t", h=4, e=128)
            for r in range(NCORES):
                S.dma("sp", f"gk{r % 2}", lambda e, r=r, h=h: e.dma_start(out=ka3[:, r, :], in_=gk[:, r, h, :]), reads=[self.t_gat[("K", l)]], writes=kall.tiles)
            kLR = [pool.alloc(T * 2, top=True), pool.alloc(T * 2, top=True)]
            for side, selI in enumerate((self.IL, self.IR)):
                for th in range(2):
                    bk, tb = self.bank()
                    items = [(bk[:, :], selI[:, r, :], ka3[:, r, th * 512:(th + 1) * 512], r == 0, r == NCORES - 1) for r in range(NCORES)]
                    S.op("pe", mmgroup(items), reads=kall.tiles + [self.t_sel], writes=[tb])
                    S.op("act", lambda e, bk=bk, side=side, th=th: e.activation(out=kLR[side].bf()[:, th * 512:(th + 1) * 512], in_=bk[:, :], func=AF.Copy),
                         reads=[tb], writes=kLR[side].tiles)
            pool.free(kall)
            vall = pool.alloc(NCORES * NT * VA * 2)
            va4 = vall.bf().rearrange("p (r a n) -> p r a n", r=NCORES, a=NT)
            gv = self.gatV[l].ap()
            for r in range(NCORES):
                S.dma("sp", f"gv{r % 2}", lambda e, r=r, h=h: e.dma_start(out=va4[:, r, :, :], in_=gv[r * T:(r + 1) * T, h * VA:(h + 1) * VA].rearrange("(a p) n -> p a n", p=128)),
                      reads=[self.t_gat[("V", l)]], writes=vall.tiles)
            vLR = [pool.alloc(NT * VA * 2, top=True), pool.alloc(NT * VA * 2, top=True)]
            for side, selI in enumerate((self.IL, self.IR)):
                for g0 in (0, 3, 6):
                    tts = list(range(g0, min(NT, g0 + 3)))
                    bk, tb = self.bank()
                    items = []
                    for i, tt in enumerate(tts):
                        for r in range(NCORES):
                            items.append((bk[:, i * VA:(i + 1) * VA], selI[:, r, :], va4[:, r, tt, :], r == 0, r == NCORES - 1))
                    S.op("pe", mmgroup(items), reads=vall.tiles + [self.t_sel], writes=[tb])
                    w_ = len(tts) * VA
                    S.op("act", lambda e, bk=bk, side=side, g0=g0, w_=w_: e.activation(out=vLR[side].bf()[:, g0 * VA:g0 * VA + w_], in_=bk[:, 0:w_], func=AF.Copy),
                         reads=[tb], writes=vLR[side].tiles)
            pool.free(vall)
            kparts = [kLR[0].bf(), self.kTown_ap[:, h, :], kLR[1].bf()]
            vparts = [vLR[0].bf().rearrange("p (a n) -> p a n", n=VA), self.vaown_ap[:, :, h * VA:(h + 1) * VA], vLR[1].bf().rearrange("p (a n) -> p a n", n=VA)]
            kv_tiles = kLR[0].tiles + kLR[1].tiles + self.kTown.tiles + vLR[0].tiles + vLR[1].tiles + self.vaown.tiles
            em = pool.alloc(NKT * 128 * 2)
            for piece in range(4):
                bt_ = pool.alloc(800 * 4)
                S.dma("sp", "bias", lambda e, h=h, piece=piece, bt_=bt_: e.dma_start(out=bt_.f32()[:, 0:800], in_=self.d_bias[h, :, piece * 800:(piece + 1) * 800]), writes=bt_.tiles)
                S.op("act", lambda e, piece=piece, bt_=bt_, em=em: e.activation(out=em.bf()[:, piece * 800:(piece + 1) * 800], in_=bt_.f32()[:, 0:800], func=AF.Exp),
                     reads=bt_.tiles, writes=em.tiles)
                pool.free(bt_)
            mixed = pool.alloc(NT * 128 * 2)
            mx3 = mixed.bf().rearrange("p (a n) -> p a n", n=128)
            pes = [pool.alloc(512 * 2) for _ in range(3)]
            pms = [pool.alloc(512 * 2) for _ in range(3)]
            for qt in range(NT):
                q0 = T + qt * 128
                ob, tob = self.bank()
                for grp in range((NKT + 3) // 4):
                    js = list(range(grp * 4, min(NKT, grp * 4 + 4)))
                    w_ = len(js) * 128
                    bk, tb = self.bank()
                    items = []
                    for i, j in enumerate(js):
                        ks = q0 + offs[j][1]
                        part, loc = ks // T, ks % T
                        items.append((bk[:, i * 128:(i + 1) * 128], kparts[part][:, loc:loc + 128], qT.bf()[:, qt * 128:(qt + 1) * 128], True, True))
                    S.op("pe", mmgroup(items), reads=kv_tiles + qT.tiles, writes=[tb])
                    pe_ = pes[(qt * 7 + grp) % 3]
                    S.op("act", lambda e, bk=bk, pe_=pe_, w_=w_: e.activation(out=pe_.bf()[:, 0:w_], in_=bk[:, 0:w_], func=AF.Exp), reads=[tb], writes=pe_.tiles)
                    pm = pms[(qt * 7 + grp) % 3]
                    S.op("dve", lambda e, pe_=pe_, pm=pm, grp=grp, em=em, w_=w_: e.tensor_tensor(out=pm.bf()[:, 0:w_], in0=pe_.bf()[:, 0:w_], in1=em.bf()[:, grp * 512:grp * 512 + w_], op=ALU.mult),
                         reads=pe_.tiles + em.tiles, writes=pm.tiles)
                    items = []
                    for i, j in enumerate(js):
                        ks = q0 + offs[j][1]
                        part, loc = ks // T, ks % T
                        items.append((ob[:, 0:129], pm.bf()[:, i * 128:(i + 1) * 128], vparts[part][:, loc // 128, 0:129], j == 0, j == NKT - 1))
                    S.op("pe", mmgroup(items), reads=pm.tiles + kv_tiles, writes=[tob])
                self.attn_epilogue(ob, tob, mx3[:, qt, :], mixed.tiles)
            self.transpose_to_mixT(mixed, 1, 8 + h, gb + 48)
            pool.free(qT, em, mixed, *pes, *pms, *kLR, *vLR)
        pool.free(self.kTown, self.vaown)
        if self.dbg == "DIL":
            return

        self.out_proj(8)
        pool.free(self.mixT)
        if self.dbg == "OP8":
            return

        self.alloc_mixT()
        self.load_upaug(l)
        self.compute_lr()
        QS = 128 ** -0.5
        for h in range(4):
            wqk, tqk = self.wnext()
            qT = pool.alloc(T * 4)
            kT = pool.alloc(T * 4)
            def evq(th, n, bk, tb, qT=qT):
                S.op("act", lambda e: e.activation(out=qT.f32()[:, th * 512:th * 512 + n], in_=bk[:, 0:n], func=AF.Copy, scale=QS), reads=[tb], writes=qT.tiles)
            def evk(th, n, bk, tb, kT=kT):
                S.op("act", lambda e: e.activation(out=kT.f32()[:, th * 512:th * 512 + n], in_=bk[:, 0:n], func=AF.Copy), reads=[tb], writes=kT.tiles)
            self.proj_feat(wqk, tqk, 256, 0, 128, self.hT, self.t_h, T, evq)
            self.proj_feat(wqk, tqk, 256, 128, 128, self.hT, self.t_h, T, evk)
            R = []
            for d in range(2):
                g, g3 = self.gla_gates(h, d)
                R.append(self.gla_decays(h, d, g, g3, qT, kT, True))
                pool.free(g)
            pool.free(qT, kT)
            wv_, tv_ = self.wnext()
            vh = pool.alloc(NT * 256 * 2)
            v3 = vh.bf().rearrange("p (a n) -> p a n", n=256)
            for tt in range(NT):
                def evv(bk, tb, tt=tt):
                    S.op("act", lambda e: e.activation(out=v3[:, tt, :], in_=bk[:, 0:256], func=AF.Copy), reads=[tb], writes=vh.tiles)
                self.proj_tok(wv_, tv_, 256, 0, 256, self.hT, self.t_h, tt, evv)
            wr_, tr_ = self.wnext()
            sr = pool.alloc(NT * 256 * 2)
            sr3 = sr.bf().rearrange("p (a n) -> p a n", n=256)
            for tt in range(NT):
                def evr(bk, tb, tt=tt):
                    S.op("act", lambda e: e.activation(out=sr3[:, tt, :], in_=bk[:, 0:256], func=AF.Silu), reads=[tb], writes=sr.tiles)
                self.proj_tok(wr_, tr_, 256, 0, 256, self.hT, self.t_h, tt, evr)
            snaps = []
            for d in range(2):
                st_ = pool.alloc(1024)
                S.op("dve", lambda e, st_=st_: e.memset(st_.f32()[:, 0:256], 0.0), writes=st_.tiles)
                sall = pool.alloc(NCORES * SW * 4)
                sa3 = sall.f32().rearrange("p (r n) -> p r n", n=SW)
                gs = self.gatS[l].ap().rearrange("(r q p) n -> p r q n", q=8, p=128)
                S.dma("sp", f"gs{d}", lambda e, d=d, h=h, sa3=sa3, gs=gs: e.dma_start(out=sa3, in_=gs[:, :, d * 4 + h, :]),
                      reads=[self.t_gat[("S", l)]], writes=sall.tiles)
                mcol, ocol = 16 + 8 * d, 32 + 8 * d
                import os
                NR = int(os.environ.get("KNR", "8"))
                for r in range(NR):
                    S.op("dve", lambda e, r=r, sa3=sa3, mcol=mcol: e.tensor_scalar(out=sa3[:, r, :], in0=sa3[:, r, :], scalar1=self.selm[:, mcol + r:mcol + r + 1], scalar2=0.0,
                                                                               op0=ALU.mult, op1=ALU.add),
                         reads=sall.tiles + [self.t_sel], writes=sall.tiles)
                dp = pool.alloc(1024)
                S.op("dve", lambda e, sa3=sa3, dp=dp, ocol=ocol: e.tensor_tensor(out=dp.f32()[:, 0:NCORES], in0=sa3[:, :, 256], in1=self.selm[:, ocol:ocol + NCORES], op=ALU.add),
                     reads=sall.tiles + [self.t_sel], writes=dp.tiles)
                for r in (range(NR) if d == 0 else range(NR - 1, -1, -1)):
                    S.op("dve", lambda e, st_=st_, dp=dp, sa3=sa3, r=r: e.scalar_tensor_tensor(out=st_.f32()[:, 0:256], in0=st_.f32()[:, 0:256], scalar=dp.f32()[:, r:r + 1],
                                                                                            in1=sa3[:, r, 0:256], op0=ALU.mult, op1=ALU.add),
                         reads=st_.tiles + dp.tiles + sall.tiles, writes=st_.tiles)
                pool.free(sall)
                sn = pool.alloc(NT * 256 * 2)
                self.gla_scan(d, R[d]["k_end"], vh, R[d]["dec"], st_, sn)
                snaps.append(sn)
                pool.free(st_, dp)
            mixed = pool.alloc(NT * 256 * 2)
            mx3 = mixed.bf().rearrange("p (a n) -> p a n", n=256)
            sn3 = [s_.bf().rearrange("p (a n) -> p a n", n=256) for s_ in snaps]
            for tt in range(NT):
                tsl = slice(tt * 128, (tt + 1) * 128)
                bk, tb = self.bank()
                items = [(bk[:, d * 128:(d + 1) * 128], R[d]["k_inv"].bf()[:, tsl], R[d]["q_dec"].bf()[:, tsl], True, True) for d in range(2)]
                S.op("pe", mmgroup(items), reads=R[0]["k_inv"].tiles + R[0]["q_dec"].tiles + R[1]["k_inv"].tiles + R[1]["q_dec"].tiles, writes=[tb])
                at = pool.alloc(256 * 2)
                S.op("dve", lambda e, bk=bk, at=at: e.tensor_tensor(out=at.bf()[:, 0:256], in0=bk[:, 0:256], in1=self.mask2[:, :], op=ALU.mult),
                     reads=[tb, self.t_const], writes=at.tiles)
                ob, tob = self.bank()
                items = [(ob[:, 0:256], at.bf()[:, 0:128], v3[:, tt, :], True, False),
                         (ob[:, 0:256], at.bf()[:, 128:256], v3[:, tt, :], False, False),
                         (ob[:, 0:256], R[0]["q_dec"].bf()[:, tsl], sn3[0][:, tt, :], False, False),
                         (ob[:, 0:256], R[1]["q_dec"].bf()[:, tsl], sn3[1][:, tt, :], False, True)]
                S.op("pe", mmgroup(items), reads=at.tiles + vh.tiles + R[0]["q_dec"].tiles + R[1]["q_dec"].tiles + snaps[0].tiles + snaps[1].tiles, writes=[tob])
                pool.free(at)
                sm = pool.alloc(1024)
                junk = pool.alloc(256 * 4)
                S.op("act", lambda e, ob=ob, junk=junk, sm=sm: e.activation(out=junk.f32()[:, 0:256], in_=ob[:, 0:256], func=AF.Square, accum_out=sm.f32()[:, 1:2]),
                     reads=[tob], writes=junk.tiles + sm.tiles)
                self.rstd_from(sm.f32()[:, 2:3], sm.f32()[:, 1:2], 1.0 / 256, sm.tiles, sm.tiles)
                S.op("dve", lambda e, ob=ob, sm=sm, tt=tt: e.scalar_tensor_tensor(out=mx3[:, tt, :], in0=ob[:, 0:256], scalar=sm.f32()[:, 2:3], in1=sr3[:, tt, :],
                                                                               op0=ALU.mult, op1=ALU.mult),
                     reads=[tob] + sm.tiles + sr.tiles, writes=mixed.tiles)
                pool.free(sm, junk)
            self.transpose_to_mixT(mixed, 2, 2 * h, gb + 48)
            pool.free(mixed, vh, sr, snaps[0], snaps[1])
            for d in range(2):
                pool.free(R[d]["q_dec"], R[d]["k_inv"], R[d]["k_end"], R[d]["dec"])
        pool.free(self.lrT, self.upaug)
        if self.dbg == "GLA":
            return
        self.out_proj(0)
        if self.dbg == "OP0":
            return
        pool.free(self.mixT)

        self.norm_T(self.xT, [[t] for t in self.t_x], T, gb + 16, self.hT, [[t] for t in self.t_h])
        import os
        for fq in range(int(os.environ.get("KMLPQ", "4"))):
            uT = pool.alloc(16 * T * 2)
            u3 = uT.bf().rearrange("p (c n) -> p c n", n=T)
            for j in range(8):
                wap, wt = self.wnext()
                for cc in range(2):
                    fc = 2 * j + cc
                    def evac(th, n, bk, tb, fc=fc):
                        r_ = pool.alloc(512 * 4)
                        S.op("act", lambda e: e.activation(out=r_.f32()[:, 0:512], in_=bk[:, :], func=AF.Relu), reads=[tb], writes=r_.tiles)
                        S.op("dve", lambda e: e.tensor_tensor(out=u3[:, fc, th * 512:(th + 1) * 512], in0=r_.f32()[:, 0:512], in1=r_.f32()[:, 0:512], op=ALU.mult),
                             reads=r_.tiles, writes=uT.tiles[2 * fc:2 * fc + 2])
                        pool.free(r_)
                    self.proj_feat(wap, wt, 256, 128 * cc, 128, self.hT, self.t_h, T, evac)
            for j in range(8):
                wap, wt = self.wnext()
                w3 = wap[:, 0:16 * 256].rearrange("p (k n) -> p k n", n=256)
                for cc in range(2):
                    dc = 2 * j + cc
                    for th in range(2):
                        bk, tb = self.bank()
                        items = [(bk[:, :], w3[:, kc, cc * 128:(cc + 1) * 128], u3[:, kc, th * 512:(th + 1) * 512], kc == 0, kc == 15) for kc in range(16)]
                        S.op("pe", mmgroup(items), reads=[wt] + uT.tiles, writes=[tb])
                        S.op("dve", lambda e, dc=dc, th=th, bk=bk: e.tensor_tensor(out=self.xT[:, dc, th * 512:(th + 1) * 512], in0=self.xT[:, dc, th * 512:(th + 1) * 512],
                                                                                in1=bk[:, :], op=ALU.add),
                             reads=[tb, self.t_x[dc]], writes=[self.t_x[dc]])
            pool.free(uT)

    def program_F(self):
        S, pool = self.S, self.pool
        self.setup_eps()
        S.op("dve", lambda e: e.memset(self.epsb.f32()[:, 1:2], 1.0), writes=[self.t_eps])
        for r in range(NCORES):
            S.op("dve", lambda e, r=r: e.tensor_scalar(out=self.IL[:, r, :], in0=self.ident[:], scalar1=self.selm[:, r:r + 1], scalar2=0.0, op0=ALU.mult, op1=ALU.add),
                 reads=[self.t_sel, self.t_constb], writes=[self.t_sel])
            S.op("dve", lambda e, r=r: e.tensor_scalar(out=self.IR[:, r, :], in0=self.ident[:], scalar1=self.selm[:, 8 + r:9 + r], scalar2=0.0, op0=ALU.mult, op1=ALU.add),
                 reads=[self.t_sel, self.t_constb], writes=[self.t_sel])
        dbg = self.dbg
        for l in range(self.depth):
            self.layer_A(l)
            if dbg != "A":
                self.layer_B(l)
        if self.dbg != "MLP":
            self.norm_T(self.xT, [[t] for t in self.t_x], T, 64 * DEPTH, self.xT, [[t] for t in self.t_x])
        for c in range(C):
            S.dma("sp", f"yout{c % 4}", lambda e, c=c: e.dma_start(out=self.o_y[:, c, :], in_=self.xT[:, c, :]), reads=[self.t_x[c]])
        S.wait_all_dma("sp")


def EPS_AP(prog):
    return prog.epsb.f32()[:, 0:1]


_PROGS = {}


def get_prog(depth=DEPTH):
    if depth not in _PROGS:
        p = Prog(depth)
        p.build()
        _PROGS[depth] = p
    return _PROGS[depth]


def host_inputs(x, mem, norm_mix, w_in, gla_gate_up_fwd, gla_gate_bias_fwd, gla_gate_up_bwd, gla_gate_bias_bwd,
                gla_norm, rel_bias, dil_norm, mem_norm, w_mem_kv, mem_out_norm, w_out, norm_mlp, w_up, w_down,
                norm_final, depth=DEPTH):
    f = lambda a: np.asarray(a, dtype=np.float32)
    x, mem, w_in, w_mem_kv, w_out, w_up, w_down = map(f, (x, mem, w_in, w_mem_kv, w_out, w_up, w_down))
    cm, mask2, ident, ones = host_consts()
    biasT = host_bias_tiles(f(rel_bias)).reshape(4, 128, NKT * 128)
    memT = np.ascontiguousarray(mem[0].T.reshape(C, 128, MEM_LEN).transpose(1, 0, 2))
    import os
    bl = blocks_layer()
    wst = np.concatenate([host_blocks(bl, {"w_in": w_in[l], "w_mem_kv": w_mem_kv[l], "w_out": w_out[l], "w_up": w_up[l], "w_down": w_down[l]})
                          for l in range(depth)], axis=0)[:int(os.environ.get("KNBLK", "100000"))]
    gl = []
    for l in range(DEPTH):
        gl += [feat_cols(f(norm_mix)[l]), feat_cols(f(norm_mlp)[l]), feat_cols(f(mem_norm)[l]),
               feat_cols(np.concatenate([f(gla_norm)[l], f(dil_norm)[l], f(mem_out_norm)[l]]))]
    gl.append(feat_cols(f(norm_final)))
    gains = np.ascontiguousarray(np.concatenate(gl, axis=1))
    upaug = np.stack([np.stack([np.concatenate([f(gla_gate_up_fwd)[l], f(gla_gate_bias_fwd)[l][None]], axis=0),
                                np.concatenate([f(gla_gate_up_bwd)[l], f(gla_gate_bias_bwd)[l][None]], axis=0)]) for l in range(DEPTH)])
    wparts = {f"wst{i}": np.ascontiguousarray(wst[i * WCH:(i + 1) * WCH]) for i in range((wst.shape[0] + WCH - 1) // WCH)}
    common = dict(gains=gains, cm=cm, mask2=mask2, ident=ident, ones=ones, upaug=np.ascontiguousarray(upaug), memT=memT, biasT=biasT, **wparts)
    maps = []
    for c in range(NCORES):
        xT = np.ascontiguousarray(x[0, c * T:(c + 1) * T, :].T.reshape(C, 128, T).transpose(1, 0, 2))
        r = np.arange(NCORES)
        row = np.concatenate([(r == c - 1), (r == c + 1), (r < c), (r > c), ~(r < c), ~(r > c)]).astype(np.float32)
        selm = np.ascontiguousarray(np.broadcast_to(row[None, :], (128, 48)))
        maps.append(dict(xT=xT, selm=selm, **common))
    return maps


def kernel(x, mem, norm_mix, w_in, gla_gate_up_fwd, gla_gate_bias_fwd, gla_gate_up_bwd, gla_gate_bias_bwd,
           gla_norm, rel_bias, dil_norm, mem_norm, w_mem_kv, mem_out_norm, w_out, norm_mlp, w_up, w_down,
           norm_final):
    maps = host_inputs(x, mem, norm_mix, w_in, gla_gate_up_fwd, gla_gate_bias_fwd, gla_gate_up_bwd, gla_gate_bias_bwd,
                       gla_norm, rel_bias, dil_norm, mem_norm, w_mem_kv, mem_out_norm, w_out, norm_mlp, w_up, w_down, norm_final)
    prog = get_prog(DEPTH)
    cores = list(range(NCORES))
    res = run_bass_kernel_spmd(prog.nc, maps, core_ids=cores).results
    out = np.empty((1, SEQ, D), np.float32)
    for c in cores:
        out[0, c * T:(c + 1) * T, :] = np.asarray(res[c]["yT"]).transpose(1, 0, 2).reshape(D, T).T
    return out
```
